# Optimizing a Trainium2 kernel written in Bass

```python
import math
import jax
import jax.numpy as jnp
from jax import lax
import numpy as np

D_MODEL = 2048
BATCH = 2
SEQ = 4096
DEPTH = 4
DEC_BATCH = 8
DEC_SEQ = 4096
PAST_LEN = 128

N_MIXERS = 4
GRID_W = 64
PLE_DIM = 256
EPS = 1e-6
NEG_INF = -1e30

D_FF = 5632
FFN_CONV = 3

NA_HEADS = 16
NA_HEAD_DIM = D_MODEL // NA_HEADS
NA_WIN_R = 8
NA_WIN_C = 16

SG_CHUNK = 128
SG_WIDTH = D_MODEL
SG_GROUPS = 16
SG_GROUP_DIM = SG_WIDTH // SG_GROUPS

GDN_QK_HEADS = 16
GDN_V_HEADS = 32
GDN_DK = 128
GDN_DV = 128
GDN_CONV = 3
GDN_CHUNK = 64
GDN_QK_WIDTH = GDN_QK_HEADS * GDN_DK
GDN_V_WIDTH = GDN_V_HEADS * GDN_DV
GDN_CONV_WIDTH = 2 * GDN_QK_WIDTH + GDN_V_WIDTH
GDN_IN_WIDTH = GDN_CONV_WIDTH + GDN_V_WIDTH + 4 * GDN_V_HEADS

S5_GROUP_DIM = 16
S5_GROUPS = D_MODEL // S5_GROUP_DIM
S5_STATE = 64
S5_CHUNK = 128

N_NA = len(range(0, DEPTH, N_MIXERS))
N_SG = len(range(1, DEPTH, N_MIXERS))
N_GDN = len(range(2, DEPTH, N_MIXERS))
N_S5 = len(range(3, DEPTH, N_MIXERS))

kernel_name = 'hybrid_bidir_encoder_na_sgu_gdn_s5'


def rmsnorm(x, g):
    xf = x.astype(jnp.float32)
    y = xf * lax.rsqrt(jnp.mean(xf * xf, axis=-1, keepdims=True) + EPS)
    return (y * g.astype(jnp.float32)).astype(x.dtype)


def l2norm(t):
    return t * lax.rsqrt(jnp.sum(t * t, axis=-1, keepdims=True) + EPS)


def dwconv_centred(x, w):
    width = w.shape[0]
    half = width // 2
    seq = x.shape[1]
    xp = jnp.pad(x, ((0, 0), (half, half), (0, 0)))
    return sum(xp[:, k:k + seq] * w[k] for k in range(width))


def neighbourhood_attention(h, w_qkv, w_o, rpb):
    bsz, seq, _ = h.shape
    rows = seq // GRID_W
    win_r = min(NA_WIN_R, rows)
    q, k, v = jnp.split(h @ w_qkv, 3, axis=-1)
    grid = lambda t: t.reshape(bsz, rows, GRID_W, NA_HEADS, NA_HEAD_DIM)
    q, k, v = grid(q) * NA_HEAD_DIM ** -0.5, grid(k), grid(v)
    row_start = np.clip(np.arange(rows) - win_r // 2, 0, rows - win_r)
    cols = np.arange(GRID_W)
    col_start = np.clip(cols - NA_WIN_C // 2, 0, GRID_W - NA_WIN_C)
    col_valid = (cols[None, :] >= col_start[:, None]) & (cols[None, :] < col_start[:, None] + NA_WIN_C)
    dc_idx = np.clip(cols[None, :] - cols[:, None] + NA_WIN_C - 1, 0, 2 * NA_WIN_C - 2)
    bias_c = jnp.where(col_valid, rpb[:, :, dc_idx].astype(jnp.float32), NEG_INF)
    bias_c = jnp.transpose(bias_c, (0, 2, 1, 3))
    dr_idx = row_start[:, None] + np.arange(win_r)[None, :] - np.arange(rows)[:, None] + NA_WIN_R - 1

    def one_row(args):
        q_r, r0, dri = args
        k_w = lax.dynamic_slice_in_dim(k, r0, win_r, axis=1)
        v_w = lax.dynamic_slice_in_dim(v, r0, win_r, axis=1)
        s = jnp.einsum('bqhd,bajhd->bhqaj', q_r, k_w, preferred_element_type=jnp.float32)
        s = s + jnp.take(bias_c, dri, axis=2)
        pr = jax.nn.softmax(s.reshape(bsz, NA_HEADS, GRID_W, win_r * GRID_W), axis=-1).reshape(s.shape)
        return jnp.einsum('bhqaj,bajhd->bqhd', pr.astype(v.dtype), v_w)

    out = lax.map(one_row, (jnp.moveaxis(q, 1, 0), jnp.asarray(row_start, jnp.int32), jnp.asarray(dr_idx, jnp.int32)))
    out = jnp.moveaxis(out, 0, 1).reshape(bsz, seq, D_MODEL)
    return out @ w_o


def spatial_gating(h, w_in, sg_norm, w_s, b_s, w_o):
    bsz, seq, _ = h.shape
    n = seq // SG_CHUNK
    u, v = jnp.split(jax.nn.gelu(h @ w_in), 2, axis=-1)
    v = rmsnorm(v, sg_norm).reshape(bsz, n, SG_CHUNK, SG_GROUPS, SG_GROUP_DIM)
    mixed = jnp.einsum('gts,bnsgc->bntgc', w_s, v) + b_s.T[:, :, None]
    return (u * mixed.reshape(bsz, seq, SG_WIDTH)) @ w_o


def gated_delta_scan(q, k, v, g, beta):
    bsz, seq = q.shape[:2]
    n = seq // GDN_CHUNK
    rep = GDN_V_HEADS // GDN_QK_HEADS
    tri_incl = np.tril(np.ones((GDN_CHUNK, GDN_CHUNK), bool))
    tri_strict = np.tril(np.ones((GDN_CHUNK, GDN_CHUNK), bool), -1)
    eye = jnp.eye(GDN_CHUNK, dtype=jnp.float32)

    def chunks(t):
        return jnp.moveaxis(t.reshape(bsz, n, GDN_CHUNK, *t.shape[2:]), 1, 0)

    def tr(t):
        return jnp.swapaxes(t, -1, -2)

    def step(state, inp):
        qc, kc, vc, gc, bc = inp
        qh = jnp.swapaxes(jnp.repeat(qc, rep, axis=2), 1, 2)
        kh = jnp.swapaxes(jnp.repeat(kc, rep, axis=2), 1, 2)
        vh = jnp.swapaxes(vc, 1, 2)
        gam = jnp.cumsum(jnp.swapaxes(gc, 1, 2), axis=-1)
        bet = jnp.swapaxes(bc, 1, 2)
        decay = jnp.exp(jnp.where(tri_incl, gam[..., :, None] - gam[..., None, :], -jnp.inf))
        m = jnp.where(tri_strict, (kh @ tr(kh)) * decay, 0.0) * bet[..., :, None]
        e_gam = jnp.exp(gam)
        rhs = jnp.concatenate([kh * (bet * e_gam)[..., None], vh * bet[..., None]], axis=-1)
        sol = lax.linalg.triangular_solve(eye + m, rhs, left_side=True, lower=True, unit_diagonal=True)
        w_mat, u_val = sol[..., :GDN_DK], sol[..., GDN_DK:]
        u = u_val - w_mat @ state
        o = (qh * e_gam[..., None]) @ state + ((qh @ tr(kh)) * decay) @ u
        k_dec = kh * jnp.exp(gam[..., -1:] - gam)[..., None]
        new_state = e_gam[..., -1][..., None, None] * state + tr(k_dec) @ u
        return new_state, o

    state0 = jnp.zeros((bsz, GDN_V_HEADS, GDN_DK, GDN_DV), jnp.float32)
    _, o = lax.scan(step, state0, (chunks(q), chunks(k), chunks(v), chunks(g), chunks(beta)))
    return jnp.transpose(o, (1, 0, 3, 2, 4)).reshape(bsz, seq, GDN_V_HEADS, GDN_DV)


def gated_deltanet(h, w_in, conv_w, a_log, dt_bias, out_norm, w_o):
    bsz, seq, _ = h.shape
    proj = h @ w_in
    qkv = jax.nn.silu(dwconv_centred(proj[..., :GDN_CONV_WIDTH], conv_w)).astype(jnp.float32)
    z = proj[..., GDN_CONV_WIDTH:GDN_CONV_WIDTH + GDN_V_WIDTH]
    ab = proj[..., GDN_CONV_WIDTH + GDN_V_WIDTH:].astype(jnp.float32).reshape(bsz, seq, 2, 2, GDN_V_HEADS)
    q = l2norm(qkv[..., :GDN_QK_WIDTH].reshape(bsz, seq, GDN_QK_HEADS, GDN_DK)) * GDN_DK ** -0.5
    k = l2norm(qkv[..., GDN_QK_WIDTH:2 * GDN_QK_WIDTH].reshape(bsz, seq, GDN_QK_HEADS, GDN_DK))
    v = qkv[..., 2 * GDN_QK_WIDTH:].reshape(bsz, seq, GDN_V_HEADS, GDN_DV)
    decay_rate = jnp.exp(a_log.astype(jnp.float32))
    g = -decay_rate * jax.nn.softplus(ab[:, :, :, 0] + dt_bias.astype(jnp.float32))
    beta = jax.nn.sigmoid(ab[:, :, :, 1])
    o_fwd = gated_delta_scan(q, k, v, g[:, :, 0], beta[:, :, 0])
    o_bwd = jnp.flip(gated_delta_scan(jnp.flip(q, 1), jnp.flip(k, 1), jnp.flip(v, 1),
                                      jnp.flip(g[:, :, 1], 1), jnp.flip(beta[:, :, 1], 1)), 1)
    zg = jax.nn.silu(z.astype(jnp.float32).reshape(bsz, seq, GDN_V_HEADS, GDN_DV))
    o = rmsnorm(o_fwd + o_bwd, out_norm) * zg
    return o.reshape(bsz, seq, GDN_V_WIDTH).astype(h.dtype) @ w_o


def s5_direction(u, a_re, a_im, log_dt, b_re, b_im, c_re, c_im):
    bsz, seq = u.shape[:2]
    n = seq // S5_CHUNK
    f32 = jnp.float32
    lam = lax.complex(a_re.astype(f32), a_im.astype(f32))
    dt = jnp.exp(log_dt.astype(f32))[:, None]
    a_bar = jnp.exp(lam * dt)
    b_bar = ((a_bar - 1.0) / lam)[..., None] * lax.complex(b_re.astype(f32), b_im.astype(f32))
    c = lax.complex(c_re.astype(f32), c_im.astype(f32))
    bu = jnp.einsum('gpc,bsgc->bsgp', b_bar, u.astype(jnp.complex64))
    bu = jnp.moveaxis(bu.reshape(bsz, n, S5_CHUNK, S5_GROUPS, S5_STATE), 1, 0)
    a_elems = jnp.broadcast_to(a_bar, (bsz, S5_CHUNK, S5_GROUPS, S5_STATE))
    powers = jnp.exp(lam[None] * dt[None] * jnp.arange(1, S5_CHUNK + 1, dtype=f32)[:, None, None])

    def binop(e1, e2):
        return (e2[0] * e1[0], e2[0] * e1[1] + e2[1])

    def step(x_prev, bu_c):
        _, xs = lax.associative_scan(binop, (a_elems, bu_c), axis=1)
        xs = xs + powers[None] * x_prev[:, None]
        y = jnp.einsum('gcp,blgp->blgc', c, xs).real
        return xs[:, -1], y

    x0 = jnp.zeros((bsz, S5_GROUPS, S5_STATE), jnp.complex64)
    _, ys = lax.scan(step, x0, bu)
    return jnp.moveaxis(ys, 0, 1).reshape(bsz, seq, S5_GROUPS, S5_GROUP_DIM)


def s5_mixer(h, a_re, a_im, log_dt, b_re, b_im, c_re, c_im, d_skip, w_glu):
    bsz, seq, _ = h.shape
    hf = h.astype(jnp.float32)
    u = hf.reshape(bsz, seq, S5_GROUPS, S5_GROUP_DIM)
    y_f = s5_direction(u, a_re[0], a_im[0], log_dt[0], b_re[0], b_im[0], c_re[0], c_im[0])
    y_b = jnp.flip(s5_direction(jnp.flip(u, 1), a_re[1], a_im[1], log_dt[1], b_re[1], b_im[1], c_re[1], c_im[1]), 1)
    y = (y_f + y_b).reshape(bsz, seq, D_MODEL) + d_skip * hf
    y = jax.nn.gelu(y).astype(h.dtype)
    a, b = jnp.split(y @ w_glu, 2, axis=-1)
    return a * jax.nn.sigmoid(b)


def conv_glu_ffn(h, w_gu, conv_w, conv_b, w_down):
    gate, up = jnp.split(h @ w_gu, 2, axis=-1)
    gate = dwconv_centred(gate, conv_w) + conv_b
    return (jax.nn.silu(gate) * up) @ w_down


def trunk(x, p, w):
    for i in range(DEPTH):
        kind, j = i % N_MIXERS, i // N_MIXERS
        h = rmsnorm(x, w['norm_mix'][i])
        if kind == 0:
            mix = neighbourhood_attention(h, w['na_w_qkv'][j], w['na_w_o'][j], w['na_rpb'][j])
        elif kind == 1:
            mix = spatial_gating(h, w['sg_w_in'][j], w['sg_norm'][j], w['sg_w_s'][j], w['sg_b_s'][j], w['sg_w_o'][j])
        elif kind == 2:
            mix = gated_deltanet(h, w['gdn_w_in'][j], w['gdn_conv_w'][j], w['gdn_a_log'][j], w['gdn_dt_bias'][j],
                                 w['gdn_out_norm'][j], w['gdn_w_o'][j])
        else:
            mix = s5_mixer(h, w['s5_a_re'][j], w['s5_a_im'][j], w['s5_log_dt'][j], w['s5_b_re'][j], w['s5_b_im'][j],
                           w['s5_c_re'][j], w['s5_c_im'][j], w['s5_d'][j], w['s5_w_glu'][j])
        x = x + mix
        x = x + conv_glu_ffn(rmsnorm(x, w['norm_ffn'][i]), w['ffn_w_gu'][i], w['ffn_conv_w'][i],
                             w['ffn_conv_b'][i], w['ffn_w_down'][i])
        gate = jax.nn.sigmoid(rmsnorm(x, w['norm_ple'][i]) @ w['ple_w_gate'][i])
        x = x + gate * (p[i] @ w['ple_w_proj'][i])
    return rmsnorm(x, w['final_norm'])


def setup_inputs(seed: int = 0) -> dict:
    key = jax.random.key(seed)
    keys = iter(list(jax.random.split(key, 64)))
    f32 = jnp.float32

    def normal(shape, scale):
        return jax.random.normal(next(keys), shape, f32) * scale

    def uniform(shape, lo, hi):
        return jax.random.uniform(next(keys), shape, f32, lo, hi)

    def gain(shape):
        return 1.0 + normal(shape, 0.02)

    d = D_MODEL
    inp = {}
    inp['x_prompt'] = normal((BATCH, SEQ, d), 1.0)
    inp['x_sample'] = normal((DEC_BATCH, DEC_SEQ, d), 1.0)
    inp['p_prompt'] = normal((DEPTH, BATCH, SEQ, PLE_DIM), 1.0)
    inp['p_sample'] = normal((DEPTH, DEC_BATCH, DEC_SEQ, PLE_DIM), 1.0)
    inp['norm_mix'] = gain((DEPTH, d))
    inp['norm_ffn'] = gain((DEPTH, d))
    inp['norm_ple'] = gain((DEPTH, d))
    inp['final_norm'] = gain((d,))
    inp['na_w_qkv'] = normal((N_NA, d, 3 * d), d ** -0.5)
    inp['na_w_o'] = normal((N_NA, d, d), d ** -0.5)
    inp['na_rpb'] = normal((N_NA, NA_HEADS, 2 * NA_WIN_R - 1, 2 * NA_WIN_C - 1), 0.1)
    inp['sg_w_in'] = normal((N_SG, d, 2 * SG_WIDTH), d ** -0.5)
    inp['sg_norm'] = gain((N_SG, SG_WIDTH))
    inp['sg_w_s'] = normal((N_SG, SG_GROUPS, SG_CHUNK, SG_CHUNK), SG_CHUNK ** -0.5)
    inp['sg_b_s'] = gain((N_SG, SG_GROUPS, SG_CHUNK))
    inp['sg_w_o'] = normal((N_SG, SG_WIDTH, d), SG_WIDTH ** -0.5)
    inp['gdn_w_in'] = normal((N_GDN, d, GDN_IN_WIDTH), d ** -0.5)
    inp['gdn_conv_w'] = normal((N_GDN, GDN_CONV, GDN_CONV_WIDTH), GDN_CONV ** -0.5)
    inp['gdn_a_log'] = jnp.log(uniform((N_GDN, 2, GDN_V_HEADS), 1.0, 16.0))
    dt = jnp.exp(uniform((N_GDN, 2, GDN_V_HEADS), math.log(1e-3), math.log(1e-1)))
    inp['gdn_dt_bias'] = dt + jnp.log(-jnp.expm1(-dt))
    inp['gdn_out_norm'] = gain((N_GDN, GDN_DV))
    inp['gdn_w_o'] = normal((N_GDN, GDN_V_WIDTH, d), GDN_V_WIDTH ** -0.5)
    a_shape = (N_S5, 2, S5_GROUPS, S5_STATE)
    inp['s5_a_re'] = -0.5 + normal(a_shape, 0.01)
    inp['s5_a_im'] = jnp.broadcast_to(math.pi * jnp.arange(S5_STATE, dtype=f32), a_shape) + normal(a_shape, 0.01)
    inp['s5_log_dt'] = uniform((N_S5, 2, S5_GROUPS), math.log(1e-3), math.log(1e-1))
    inp['s5_b_re'] = normal((N_S5, 2, S5_GROUPS, S5_STATE, S5_GROUP_DIM), (2 * S5_GROUP_DIM) ** -0.5)
    inp['s5_b_im'] = normal((N_S5, 2, S5_GROUPS, S5_STATE, S5_GROUP_DIM), (2 * S5_GROUP_DIM) ** -0.5)
    inp['s5_c_re'] = normal((N_S5, 2, S5_GROUPS, S5_GROUP_DIM, S5_STATE), (2 * S5_STATE) ** -0.5)
    inp['s5_c_im'] = normal((N_S5, 2, S5_GROUPS, S5_GROUP_DIM, S5_STATE), (2 * S5_STATE) ** -0.5)
    inp['s5_d'] = normal((N_S5, d), 1.0)
    inp['s5_w_glu'] = normal((N_S5, d, 2 * d), d ** -0.5)
    inp['ffn_w_gu'] = normal((DEPTH, d, 2 * D_FF), d ** -0.5)
    inp['ffn_conv_w'] = normal((DEPTH, FFN_CONV, D_FF), FFN_CONV ** -0.5)
    inp['ffn_conv_b'] = normal((DEPTH, D_FF), 0.02)
    inp['ffn_w_down'] = normal((DEPTH, D_FF, d), D_FF ** -0.5)
    inp['ple_w_proj'] = normal((DEPTH, PLE_DIM, d), PLE_DIM ** -0.5)
    inp['ple_w_gate'] = normal((DEPTH, d, d), d ** -0.5)
    return inp


def reference(x_prompt, x_sample, p_prompt, p_sample,
              norm_mix, norm_ffn, norm_ple, final_norm,
              na_w_qkv, na_w_o, na_rpb,
              sg_w_in, sg_norm, sg_w_s, sg_b_s, sg_w_o,
              gdn_w_in, gdn_conv_w, gdn_a_log, gdn_dt_bias, gdn_out_norm, gdn_w_o,
              s5_a_re, s5_a_im, s5_log_dt, s5_b_re, s5_b_im, s5_c_re, s5_c_im, s5_d, s5_w_glu,
              ffn_w_gu, ffn_conv_w, ffn_conv_b, ffn_w_down,
              ple_w_proj, ple_w_gate):
    w = dict(norm_mix=norm_mix, norm_ffn=norm_ffn, norm_ple=norm_ple, final_norm=final_norm,
             na_w_qkv=na_w_qkv, na_w_o=na_w_o, na_rpb=na_rpb,
             sg_w_in=sg_w_in, sg_norm=sg_norm, sg_w_s=sg_w_s, sg_b_s=sg_b_s, sg_w_o=sg_w_o,
             gdn_w_in=gdn_w_in, gdn_conv_w=gdn_conv_w, gdn_a_log=gdn_a_log, gdn_dt_bias=gdn_dt_bias,
             gdn_out_norm=gdn_out_norm, gdn_w_o=gdn_w_o,
             s5_a_re=s5_a_re, s5_a_im=s5_a_im, s5_log_dt=s5_log_dt, s5_b_re=s5_b_re, s5_b_im=s5_b_im,
             s5_c_re=s5_c_re, s5_c_im=s5_c_im, s5_d=s5_d, s5_w_glu=s5_w_glu,
             ffn_w_gu=ffn_w_gu, ffn_conv_w=ffn_conv_w, ffn_conv_b=ffn_conv_b, ffn_w_down=ffn_w_down,
             ple_w_proj=ple_w_proj, ple_w_gate=ple_w_gate)
    y_prompt = trunk(x_prompt, p_prompt, w)
    y_sample = trunk(x_sample, p_sample, w)
    return (y_prompt, y_sample)
```

```python
import contextlib
import os
import numpy as np
import ml_dtypes
import concourse.bass as bass
import concourse.mybir as mybir
from concourse.bass_utils import run_bass_kernel_spmd

F32 = mybir.dt.float32
BF16 = mybir.dt.bfloat16
AF = mybir.ActivationFunctionType
ALU = mybir.AluOpType
AX = mybir.AxisListType

D = 2048
KC = 16
DFF = 5632
NFC = 44
PLE = 256
DEPTH = 4
EPS = 1e-6
SAME_ENGINE_SYNC = True


class DSem:
    def __init__(self, sem):
        self.sem = sem
        self.total = 0


class Buf:
    def __init__(self, name):
        self.name = name
        self.w = None
        self.r = {}
        self.dsem = None


class TT:
    def __init__(self, t, b):
        self.t = t
        self.b = b

    def __getitem__(self, idx):
        return self.t[idx]


class Ctx:
    ENGS = ["pe", "dve", "act", "pool", "sp"]
    CENGS = ["pe", "dve", "act", "pool"]

    def __init__(self, nc, stack, n_dsem=90):
        self.nc = nc
        self.ops = {e: [] for e in self.ENGS}
        self.csem = {e: stack.enter_context(nc.semaphore("c_" + e)) for e in self.CENGS}
        self.cnt = {e: 0 for e in self.CENGS}
        self.seen = {e: {} for e in self.ENGS}
        self.dsems = [DSem(stack.enter_context(nc.semaphore("d%d" % i))) for i in range(n_dsem)]
        self.free_ds = list(self.dsems)
        self.nuid = 0

    def sb(self, stack, name, shape, dtype, dma=False):
        self.nuid += 1
        t = stack.enter_context(self.nc.sbuf_tensor("%s_%d" % (name, self.nuid), list(shape), dtype))
        b = Buf(name)
        if dma:
            b.dsem = self.free_ds.pop()
            stack.callback(self.free_ds.append, b.dsem)
        return TT(t, b)

    def ps(self, stack, name, shape, dtype):
        self.nuid += 1
        t = stack.enter_context(self.nc.psum_tensor("%s_%d" % (name, self.nuid), list(shape), dtype))
        return TT(t, Buf(name))

    def _tok_key(self, tok):
        return (tok[0], tok[1] if tok[0] == "c" else id(tok[1]))

    def _collect(self, reads, writes):
        toks = []
        for b in reads:
            if b.w is not None:
                toks.append(b.w)
        for b in writes:
            if b.w is not None:
                toks.append(b.w)
            toks.extend(b.r.values())
        return toks

    def _waits(self, eng, toks):
        res = {}
        for tok in toks:
            if tok[0] == "c":
                e2, v = tok[1], tok[2]
                if e2 == eng and (eng == "pe" or not SAME_ENGINE_SYNC):
                    continue
                sh = self.csem[e2]
            else:
                v = tok[1].total
                sh = tok[1].sem
            key = self._tok_key(tok)
            if self.seen[eng].get(key, 0) >= v:
                continue
            if key in res and res[key][1] >= v:
                continue
            res[key] = (sh, v)
        for key, (sh, v) in res.items():
            self.seen[eng][key] = v
        return list(res.values())

    def _commit(self, tok, reads, writes):
        key = self._tok_key(tok)
        for b in reads:
            b.r[key] = tok
        for b in writes:
            b.w = tok
            b.r = {}

    def op(self, eng, name, reads=(), writes=(), **kw):
        reads = [x.b if isinstance(x, TT) else x for x in reads]
        writes = [x.b if isinstance(x, TT) else x for x in writes]
        waits = self._waits(eng, self._collect(reads, writes))
        self.cnt[eng] += 1
        sem = self.csem[eng]

        def run(e, name=name, kw=kw, waits=waits, sem=sem):
            for sh, v in waits:
                e.wait_ge(sh, v)
            getattr(e, name)(**kw).then_inc(sem, 1)

        self.ops[eng].append(run)
        self._commit(("c", eng, self.cnt[eng]), reads, writes)

    def dma(self, q, out, in_, sbt, reads=(), writes=(), **kw):
        reads = [x.b if isinstance(x, TT) else x for x in reads]
        writes = [x.b if isinstance(x, TT) else x for x in writes]
        waits = self._waits(q, self._collect(reads, writes))
        ds = sbt.b.dsem if isinstance(sbt, TT) else sbt
        ds.total += 16

        def run(e, out=out, in_=in_, kw=kw, waits=waits, ds=ds):
            for sh, v in waits:
                e.wait_ge(sh, v)
            e.dma_start(out=out, in_=in_, **kw).then_inc(ds.sem, 16)

        self.ops[q].append(run)
        self._commit(("d", ds), reads, writes)

    def barrier(self):
        for e in self.ENGS:
            waits = []
            for e2 in self.CENGS:
                if e2 == e:
                    continue
                v = self.cnt[e2]
                key = ("c", e2)
                if v > self.seen[e].get(key, 0):
                    self.seen[e][key] = v
                    waits.append((self.csem[e2], v))
            for ds in self.dsems:
                key = ("d", id(ds))
                if ds.total > self.seen[e].get(key, 0):
                    self.seen[e][key] = ds.total
                    waits.append((ds.sem, ds.total))
            if waits:
                def run(eh, waits=waits):
                    for sh, v in waits:
                        eh.wait_ge(sh, v)
                self.ops[e].append(run)

    def emit(self):
        nc = self.nc
        with nc.Block() as block:
            @block.tensor
            def _(e):
                for f in self.ops["pe"]:
                    f(e)

            @block.vector
            def _(e):
                for f in self.ops["dve"]:
                    f(e)

            @block.scalar
            def _(e):
                for f in self.ops["act"]:
                    f(e)

            @block.gpsimd
            def _(e):
                for f in self.ops["pool"]:
                    f(e)

            @block.sync
            def _(e):
                for f in self.ops["sp"]:
                    f(e)


def blocks_of(T, nbi_max=456):
    nblk = -(-T // nbi_max)
    base = -(-T // nblk)
    out = []
    t = 0
    while t < T:
        n = min(base, T - t)
        out.append((t, n))
        t += n
    return out


class Prog:
    def __init__(self, T, nslot, layers, mixers=True, do_ffn=True):
        self.do_ffn = do_ffn
        self.T = T
        self.NS = nslot
        self.layers = layers
        self.mixers = mixers
        self.nc = bass.Bass("TRN2", target_bir_lowering=False)
        self.din = {}
        self.small_off = {}
        self.small_n = 0

    def inp(self, name, shape, dtype=F32):
        ap = self.nc.dram_tensor(name, list(shape), dtype, kind="ExternalInput").ap()
        self.din[name] = ap
        return ap

    def scratch(self, name, shape, dtype):
        return self.nc.dram_tensor(name, list(shape), dtype, kind="Internal").ap()

    def cast_weight(self, C, name, src2d, K, N, ct, kg=None):
        kcs = K // 128
        kg = kg or kcs
        nt, ng = N // ct, kcs // kg
        ws = self.scratch(name, [nt, ng, 128, kg, ct], BF16)
        for i in range(nt):
            for g in range(ng):
                src = src2d[g * kg * 128:(g + 1) * kg * 128, i * ct:(i + 1) * ct].rearrange("(k p) c -> p k c", p=128)
                C.dma("pool", out=ws[i, g], in_=src, sbt=self.cast_ds)
        return ws

    def build(self):
        nc, T, NS = self.nc, self.T, self.NS
        x_in = self.inp("x_in", [NS, T, D])
        p_in = self.inp("p_in", [DEPTH, NS, T, PLE])
        y_out = nc.dram_tensor("y_out", [NS, T, D], F32, kind="ExternalOutput").ap()
        w_gu = {l: self.inp("ffn_w_gu.%d" % l, [D, 2 * DFF]) for l in self.layers}
        w_dn = {l: self.inp("ffn_w_down.%d" % l, [DFF, D]) for l in self.layers}
        w_pg = {l: self.inp("ple_w_gate.%d" % l, [D, D]) for l in self.layers}
        w_pp = {l: self.inp("ple_w_proj.%d" % l, [PLE, D]) for l in self.layers}
        self.declare_mixer_inputs()
        self.small_layout()
        smallp = self.inp("smallp", [128, self.small_n])
        ident_in = self.inp("ident", [128, 128])
        self.xT = [[self.scratch("xT_%d_%d" % (a, s), [D, T], F32) for s in range(NS)] for a in range(2)]

        with contextlib.ExitStack() as gstack:
            C = Ctx(nc, gstack)
            self.C = C
            self.cast_ds = C.free_ds.pop()
            self.PS = [C.ps(gstack, "ps%d" % i, [128, 512], F32) for i in range(8)]
            self.small = C.sb(gstack, "small", [128, self.small_n], F32, dma=True)
            C.dma("sp", out=self.small[:], in_=smallp[:, :], sbt=self.small, writes=[self.small])
            self.ident = C.sb(gstack, "ident", [128, 128], F32, dma=True)
            C.dma("sp", out=self.ident[:], in_=ident_in[:, :], sbt=self.ident, writes=[self.ident])
            self.ones = C.sb(gstack, "ones", [128, 128], F32)
            C.op("dve", "memset", writes=[self.ones], ap=self.ones[:], constant=1.0)
            self.onesb = C.sb(gstack, "onesb", [128, 128], BF16)
            C.op("dve", "memset", writes=[self.onesb], ap=self.onesb[:], constant=1.0)
            self.identb = C.sb(gstack, "identb", [128, 128], BF16)
            C.op("dve", "tensor_copy", reads=[self.ident], writes=[self.identb], out=self.identb[:], in_=self.ident[:])

            self.ws = {}
            for l in self.layers:
                if self.mixers:
                    self.cast_mixer_weights(l)
                if self.do_ffn:
                    self.ws[("gu", l)] = self.cast_weight(C, "ws_gu%d" % l, w_gu[l], D, 2 * DFF, 256)
                    self.ws[("dn", l)] = self.cast_weight(C, "ws_dn%d" % l, w_dn[l], DFF, D, 512, kg=11)
                    self.ws[("pg", l)] = self.cast_weight(C, "ws_pg%d" % l, w_pg[l], D, D, 256)
                    self.ws[("pp", l)] = self.cast_weight(C, "ws_pp%d" % l, w_pp[l], PLE, D, 2048)
            C.barrier()

            self.prologue(x_in)
            cur = 0
            for l in self.layers:
                if self.mixers:
                    for s in range(NS):
                        [self.mix_na, self.mix_sg, self.mix_gdn, self.mix_s5][l % 4](l, s, self.xT[cur][s], self.xT[1 - cur][s])
                    cur = 1 - cur
                if self.do_ffn:
                    for s in range(NS):
                        self.ffn_ple(l, s, self.xT[cur][s], self.xT[1 - cur][s], p_in[l, s])
                    cur = 1 - cur
            self.epilogue(self.xT[cur], y_out)
            C.barrier()
            C.emit()
        return nc

    def kinds(self):
        return sorted(set(l % 4 for l in self.layers)) if self.mixers else []

    def declare_mixer_inputs(self):
        ks = self.kinds()
        if 0 in ks:
            self.na_w_qkv = self.inp("na_w_qkv", [1, D, 3 * D])
            self.na_w_o = self.inp("na_w_o", [1, D, D])
            self.na_g = self.inp("na_g", [16, 128, 16, 64])
            self.na_mask = self.inp("na_mask", [128, 2, 16, 64])
            self.na_qk = self.scratch("na_qk", [2 * D, self.T], BF16)
            self.na_v = self.scratch("na_v", [self.T, D], BF16)
            self.na_o = self.scratch("na_o", [D, self.T], BF16)
        if 2 in ks:
            T = self.T
            self.gdn_w_in = self.inp("gdn_w_in", [1, D, 12416])
            self.gdn_w_o = self.inp("gdn_w_o", [1, 4096, D])
            self.gdn_a_log = self.inp("gdn_a_log", [1, 2, 32])
            self.gdn_dt_bias = self.inp("gdn_dt_bias", [1, 2, 32])
            self.gdn_out_norm = self.inp("gdn_out_norm", [1, 128])
            self.gdn_consts = self.inp("gdn_consts", [64, 8, 64])
            self.gdn_qk = self.scratch("gdn_qk", [4096, T], BF16)
            self.gdn_kv = self.scratch("gdn_kv", [T, 6144], BF16)
            self.gdn_z = self.scratch("gdn_z", [T, 4096], F32)
            self.gdn_gb = self.scratch("gdn_gb", [T, 128], F32)
            self.gdn_of = self.scratch("gdn_of", [T, 4096], F32)
            self.gdn_o = self.scratch("gdn_o", [4096, T], BF16)
        if 3 in ks:
            self.s5_w_glu = self.inp("s5_w_glu", [1, D, 2 * D])
            self.s5_lam = self.inp("s5_lam", [128, 2, 3, 64])
            self.s5_B = self.inp("s5_B", [2, 16, 128, 2, 4, 128])
            self.s5_C = self.inp("s5_C", [2, 16, 128, 2, 4, 128])
            self.iota_in = self.inp("iota128", [128, 128])
            self.s5_h = self.scratch("s5_h", [D, self.T], F32)
            self.s5_y = self.scratch("s5_y", [D, self.T], BF16)
        if 1 in ks:
            self.sg_w_in = self.inp("sg_w_in", [1, D, 2 * D])
            self.sg_norm = self.inp("sg_norm", [1, D])
            self.sg_w_s = self.inp("sg_w_s", [1, 16, 128, 128])
            self.sg_b_s = self.inp("sg_b_s", [1, 16, 128])
            self.sg_w_o = self.inp("sg_w_o", [1, D, D])

    def cast_mixer_weights(self, l):
        C = self.C
        k = l % 4
        if k == 0:
            self.ws["na_qk"] = self.cast_weight(C, "ws_naqk", self.na_w_qkv[0][:, 0:2 * D], D, 2 * D, 256)
            self.ws["na_v"] = self.cast_weight(C, "ws_nav", self.na_w_qkv[0][:, 2 * D:3 * D], D, D, 512)
            self.ws["na_o"] = self.cast_weight(C, "ws_nao", self.na_w_o[0], D, D, 256)
        if k == 2:
            self.ws["gdn_qkv"] = self.cast_weight(C, "ws_gqkv", self.gdn_w_in[0][:, 0:8192], D, 8192, 256)
            self.ws["gdn_z"] = self.cast_weight(C, "ws_gz", self.gdn_w_in[0][:, 8192:12288], D, 4096, 512)
            self.ws["gdn_ab"] = self.cast_weight(C, "ws_gab", self.gdn_w_in[0][:, 12288:12416], D, 128, 128)
            self.ws["gdn_o"] = self.cast_weight(C, "ws_go", self.gdn_w_o[0], 4096, D, 256)
        if k == 3:
            self.ws["s5_glu"] = self.cast_weight(C, "ws_s5glu", self.s5_w_glu[0], D, 2 * D, 256)
        if k == 1:
            self.ws["sg_u"] = self.cast_weight(C, "ws_sgu", self.sg_w_in[0][:, 0:D], D, D, 256)
            self.ws["sg_v"] = self.cast_weight(C, "ws_sgv", self.sg_w_in[0][:, D:2 * D], D, D, 512)
            self.ws["sg_o"] = self.cast_weight(C, "ws_sgo", self.sg_w_o[0], D, D, 256)

    def gelu_tanh(self, dst, dst_ap, src, src_ap, ta, tb, n):
        C = self.C
        C.op("act", "activation", reads=[src], writes=[ta], out=ta[:, :n], in_=src_ap, func=AF.Square)
        C.op("dve", "tensor_scalar", reads=[ta], writes=[ta], out=ta[:, :n], in0=ta[:, :n], scalar1=0.044715, scalar2=1.0,
             op0=ALU.mult, op1=ALU.add)
        C.op("dve", "tensor_tensor", reads=[ta, src], writes=[ta], out=ta[:, :n], in0=ta[:, :n], in1=src_ap, op=ALU.mult)
        C.op("act", "activation", reads=[ta], writes=[tb], out=tb[:, :n], in_=ta[:, :n], func=AF.Sigmoid, scale=1.5957691216057308)
        C.op("dve", "tensor_tensor", reads=[tb, src], writes=[dst], out=dst_ap, in0=tb[:, :n], in1=src_ap, op=ALU.mult)

    def bcast_row(self, out, tmp, name, src_row_ap, n):
        C = self.C
        row = C.sb(tmp, name + "_row", [1, n], F32, dma=True)
        C.dma("sp", out=row[:], in_=src_row_ap, sbt=row, writes=[row])
        for q in range(0, n, 512):
            qn = min(512, n - q)
            ps = self.PS[(q // 512) % 4]
            C.op("pe", "matmul", reads=[row, self.ones], writes=[ps], out=ps[:, :qn], lhsT=self.ones[0:1, 0:128], rhs=row[0:1, q:q + qn],
                 start=True, stop=True)
            C.op("dve", "tensor_copy", reads=[ps], writes=[out], out=out[:, q:q + qn], in_=ps[:, :qn])
        return out

    def mix_sg(self, l, s, xsrc, xdst):
        C, T, PS = self.C, self.T, self.PS
        BT = 256
        wu, wv, wo = self.ws["sg_u"], self.ws["sg_v"], self.ws["sg_o"]
        with contextlib.ExitStack() as st:
            WST = C.sb(st, "WST", [128, 16, 128], BF16)
            BHI = C.sb(st, "BHI", [1, D], BF16)
            BLO = C.sb(st, "BLO", [1, D], BF16)
            SGN = C.sb(st, "SGN", [128, D], F32)
            with contextlib.ExitStack() as tmp:
                self.bcast_row(SGN, tmp, "SGN", self.sg_norm[0:1, :], D)
                wsl = C.sb(tmp, "wsl", [128, 16, 128], F32, dma=True)
                C.dma("sp", out=wsl[:], in_=self.sg_w_s[0].rearrange("g t s -> t g s"), sbt=wsl, writes=[wsl])
                for g4 in range(4):
                    ps = PS[g4]
                    for gi in range(4):
                        g = g4 * 4 + gi
                        C.op("pe", "transpose", reads=[wsl, self.ident], writes=[ps], out=ps[:, gi * 128:(gi + 1) * 128], in_=wsl[:, g, :],
                             identity=self.ident[:])
                    C.op("dve", "tensor_copy", reads=[ps], writes=[WST], out=WST[:, g4 * 4:(g4 + 1) * 4, :],
                         in_=ps[:, 0:512].rearrange("p (g t) -> p g t", g=4))
                brow = C.sb(tmp, "brow", [1, D], F32, dma=True)
                C.dma("sp", out=brow[:], in_=self.sg_b_s[0:1].rearrange("o g t -> o (g t)"), sbt=brow, writes=[brow])
                bt = C.sb(tmp, "btmp", [1, D], F32)
                C.op("dve", "tensor_copy", reads=[brow], writes=[BHI], out=BHI[:], in_=brow[:])
                C.op("dve", "tensor_tensor", reads=[brow, BHI], writes=[bt], out=bt[:], in0=brow[:], in1=BHI[:], op=ALU.subtract)
                C.op("dve", "tensor_copy", reads=[bt], writes=[BLO], out=BLO[:], in_=bt[:])
                C.barrier()
            self.ensure_eps(st)
            XB = C.sb(st, "XB", [128, KC, BT], F32, dma=True)
            H = C.sb(st, "H", [128, KC, BT], BF16)
            UT = C.sb(st, "UT", [128, KC, BT], F32)
            GT = C.sb(st, "GT", [128, KC, BT], BF16)
            VT = [C.sb(st, "VT", [128, D], F32) for _ in range(BT // 128)]
            VN = [C.sb(st, "VN", [128, D], BF16) for _ in range(BT // 128)]
            WU = [C.sb(st, "WU", [128, KC, 256], BF16, dma=True) for _ in range(2)]
            WV = [C.sb(st, "WV", [128, KC, 512], BF16, dma=True) for _ in range(2)]
            WO = [C.sb(st, "WO", [128, KC, 256], BF16, dma=True) for _ in range(2)]
            ta = [C.sb(st, "ta", [128, 512], F32) for _ in range(2)]
            tb = [C.sb(st, "tb", [128, 512], F32) for _ in range(2)]
            sqb = [C.sb(st, "sq", [128, 512], F32) for _ in range(4)]
            rstd = C.sb(st, "rstd", [128, 512], F32)
            ssv = C.sb(st, "ssv", [128, 2], F32)
            iu = iv = io = ig = 0
            for t0 in range(0, T, BT):
                gn = min(BT, T - t0)
                ntt = gn // 128
                self.load_xblock(XB, xsrc, t0, gn, 0)
                self.rms_to_h(XB, H, gn, "norm_mix", l, sqb, rstd, PS[4])
                for jt in range(8):
                    W = WU[iu % 2]
                    iu += 1
                    C.dma("sp", out=W[:], in_=wu[jt, 0], sbt=W, writes=[W])
                    for jj in range(2):
                        j = jt * 2 + jj
                        ps = PS[j % 2]
                        for k in range(KC):
                            C.op("pe", "matmul", reads=[W, H], writes=[ps], out=ps[:, :gn], lhsT=W[:, k, jj * 128:(jj + 1) * 128],
                                 rhs=H[:, k, :gn], start=(k == 0), stop=(k == KC - 1))
                        self.gelu_tanh(UT, UT[:, j, :gn], ps, ps[:, :gn], ta[ig % 2], tb[ig % 2], gn)
                        ig += 1
                for cg in range(4):
                    W = WV[iv % 2]
                    iv += 1
                    C.dma("sp", out=W[:], in_=wv[cg, 0], sbt=W, writes=[W])
                    for tt in range(ntt):
                        ps = PS[2 + (cg * ntt + tt) % 2]
                        for k in range(KC):
                            C.op("pe", "matmul", reads=[W, H], writes=[ps], out=ps[:, :512], lhsT=H[:, k, tt * 128:(tt + 1) * 128],
                                 rhs=W[:, k, :], start=(k == 0), stop=(k == KC - 1))
                        self.gelu_tanh(VT[tt], VT[tt][:, cg * 512:(cg + 1) * 512], ps, ps[:, :512], ta[ig % 2], tb[ig % 2], 512)
                        ig += 1
                for tt in range(ntt):
                    C.op("act", "activation", reads=[VT[tt]], writes=[VN[tt], ssv], out=VN[tt][:], in_=VT[tt][:], func=AF.Square,
                         accum_out=ssv[:, tt:tt + 1])
                    C.op("act", "activation", reads=[ssv], writes=[ssv], out=ssv[:, tt:tt + 1], in_=ssv[:, tt:tt + 1], func=AF.Sqrt,
                         scale=1.0 / D, bias=self.epsb[:, 0:1])
                    C.op("dve", "reciprocal", reads=[ssv], writes=[ssv], out=ssv[:, tt:tt + 1], in_=ssv[:, tt:tt + 1])
                    C.op("dve", "scalar_tensor_tensor", reads=[VT[tt], ssv, SGN], writes=[VN[tt]], out=VN[tt][:], in0=VT[tt][:],
                         scalar=ssv[:, tt:tt + 1], in1=SGN[:], op0=ALU.mult, op1=ALU.mult)
                for tt in range(ntt):
                    for g4 in range(4):
                        ps = PS[4 + (tt * 4 + g4) % 4]
                        for gi in range(4):
                            g = g4 * 4 + gi
                            o = ps[:, gi * 128:(gi + 1) * 128]
                            C.op("pe", "matmul", reads=[VN[tt], WST], writes=[ps], out=o, lhsT=VN[tt][:, g * 128:(g + 1) * 128],
                                 rhs=WST[:, g, :], start=True, stop=False)
                            C.op("pe", "matmul", reads=[BHI, self.onesb], writes=[ps], out=o, lhsT=self.onesb[0:1, 0:128],
                                 rhs=BHI[0:1, g * 128:(g + 1) * 128], start=False, stop=False)
                            C.op("pe", "matmul", reads=[BLO, self.onesb], writes=[ps], out=o, lhsT=self.onesb[0:1, 0:128],
                                 rhs=BLO[0:1, g * 128:(g + 1) * 128], start=False, stop=True)
                        C.op("dve", "tensor_tensor", reads=[UT, ps], writes=[GT], out=GT[:, g4 * 4:(g4 + 1) * 4, tt * 128:(tt + 1) * 128],
                             in0=UT[:, g4 * 4:(g4 + 1) * 4, tt * 128:(tt + 1) * 128],
                             in1=ps[:, 0:512].rearrange("p (g t) -> p g t", g=4), op=ALU.mult)
                for ot in range(8):
                    W = WO[io % 2]
                    io += 1
                    C.dma("sp", out=W[:], in_=wo[ot, 0], sbt=W, writes=[W])
                    for oo in range(2):
                        oc = ot * 2 + oo
                        ps = PS[oc % 2]
                        for k in range(KC):
                            C.op("pe", "matmul", reads=[W, GT], writes=[ps], out=ps[:, :gn], lhsT=W[:, k, oo * 128:(oo + 1) * 128],
                                 rhs=GT[:, k, :gn], start=(k == 0), stop=(k == KC - 1))
                        C.op("dve", "tensor_tensor", reads=[XB, ps], writes=[XB], out=XB[:, oc, :gn], in0=XB[:, oc, :gn], in1=ps[:, :gn],
                             op=ALU.add)
                dst = xdst.rearrange("(c p) t -> p c t", p=128)[:, :, t0:t0 + gn]
                C.dma("pool", out=dst, in_=XB[:, :, :gn], sbt=XB, reads=[XB])
            C.barrier()

    def out_proj(self, xsrc, xdst, act_dram, kcs, wo, BT=512):
        C, T, PS = self.C, self.T, self.PS
        with contextlib.ExitStack() as st:
            XB = [C.sb(st, "XBo", [128, KC, BT], F32, dma=True) for _ in range(2)]
            AB = [C.sb(st, "ABo", [128, kcs, BT], BF16, dma=True) for _ in range(2)]
            WO = [C.sb(st, "WOo", [128, kcs, 256], BF16, dma=True) for _ in range(2)]
            io = ib = 0
            for t0 in range(0, T, BT):
                gn = min(BT, T - t0)
                X, Ab = XB[ib % 2], AB[ib % 2]
                ib += 1
                self.load_xblock(X, xsrc, t0, gn, 0)
                C.dma("sp", out=Ab[:, :, :gn], in_=act_dram.rearrange("(c p) t -> p c t", p=128)[:, :, t0:t0 + gn], sbt=Ab, writes=[Ab])
                for ot in range(8):
                    W = WO[io % 2]
                    io += 1
                    C.dma("sp", out=W[:], in_=wo[ot, 0], sbt=W, writes=[W])
                    for oo in range(2):
                        oc = ot * 2 + oo
                        ps = PS[oc % 4]
                        for k in range(kcs):
                            C.op("pe", "matmul", reads=[W, Ab], writes=[ps], out=ps[:, :gn], lhsT=W[:, k, oo * 128:(oo + 1) * 128],
                                 rhs=Ab[:, k, :gn], start=(k == 0), stop=(k == kcs - 1))
                        C.op("dve", "tensor_tensor", reads=[X, ps], writes=[X], out=X[:, oc, :gn], in0=X[:, oc, :gn], in1=ps[:, :gn],
                             op=ALU.add)
                dst = xdst.rearrange("(c p) t -> p c t", p=128)[:, :, t0:t0 + gn]
                C.dma("pool", out=dst, in_=X[:, :, :gn], sbt=X, reads=[X])
            C.barrier()

    def mix_na(self, l, s, xsrc, xdst):
        C, T, PS = self.C, self.T, self.PS
        BT = 512
        R = T // 64
        wqk, wv, wo = self.ws["na_qk"], self.ws["na_v"], self.ws["na_o"]
        with contextlib.ExitStack() as st:
            self.ensure_eps(st)
            XB = C.sb(st, "XB", [128, KC, BT], F32, dma=True)
            H = C.sb(st, "H", [128, KC, BT], BF16)
            QK = C.sb(st, "QK", [128, 32, BT], BF16, dma=True)
            VS = [C.sb(st, "VS", [128, D], BF16, dma=True) for _ in range(BT // 128)]
            WQ = [C.sb(st, "WQ", [128, KC, 256], BF16, dma=True) for _ in range(2)]
            WV = [C.sb(st, "WV", [128, KC, 512], BF16, dma=True) for _ in range(2)]
            sqb = [C.sb(st, "sq", [128, 512], F32) for _ in range(4)]
            rstd = C.sb(st, "rstd", [128, 512], F32)
            iq = iv = ie = 0
            for t0 in range(0, T, BT):
                gn = min(BT, T - t0)
                ntt = gn // 128
                self.load_xblock(XB, xsrc, t0, gn, 0)
                self.rms_to_h(XB, H, gn, "norm_mix", l, sqb, rstd, PS[4])
                for jt in range(16):
                    W = WQ[iq % 2]
                    iq += 1
                    C.dma("sp", out=W[:], in_=wqk[jt, 0], sbt=W, writes=[W])
                    for jj in range(2):
                        j = jt * 2 + jj
                        ps = PS[j % 2]
                        for k in range(KC):
                            C.op("pe", "matmul", reads=[W, H], writes=[ps], out=ps[:, :gn], lhsT=W[:, k, jj * 128:(jj + 1) * 128],
                                 rhs=H[:, k, :gn], start=(k == 0), stop=(k == KC - 1))
                        if ie % 2 == 0:
                            C.op("dve", "tensor_copy", reads=[ps], writes=[QK], out=QK[:, j, :gn], in_=ps[:, :gn])
                        else:
                            C.op("act", "activation", reads=[ps], writes=[QK], out=QK[:, j, :gn], in_=ps[:, :gn], func=AF.Copy)
                        ie += 1
                C.dma("pool", out=self.na_qk.rearrange("(c p) t -> p c t", p=128)[:, :, t0:t0 + gn], in_=QK[:, :, :gn], sbt=QK, reads=[QK])
                for cg in range(4):
                    W = WV[iv % 2]
                    iv += 1
                    C.dma("sp", out=W[:], in_=wv[cg, 0], sbt=W, writes=[W])
                    for tt in range(ntt):
                        ps = PS[2 + (cg * ntt + tt) % 2]
                        for k in range(KC):
                            C.op("pe", "matmul", reads=[W, H], writes=[ps], out=ps[:, :512], lhsT=H[:, k, tt * 128:(tt + 1) * 128],
                                 rhs=W[:, k, :], start=(k == 0), stop=(k == KC - 1))
                        if ie % 2 == 0:
                            C.op("dve", "tensor_copy", reads=[ps], writes=[VS[tt]], out=VS[tt][:, cg * 512:(cg + 1) * 512], in_=ps[:, :512])
                        else:
                            C.op("act", "activation", reads=[ps], writes=[VS[tt]], out=VS[tt][:, cg * 512:(cg + 1) * 512], in_=ps[:, :512],
                                 func=AF.Copy)
                        ie += 1
                for tt in range(ntt):
                    C.dma("pool", out=self.na_v[t0 + tt * 128:t0 + (tt + 1) * 128, :], in_=VS[tt][:], sbt=VS[tt], reads=[VS[tt]])
            C.barrier()
        scale = 128 ** -0.5

        def win(r):
            r0 = min(max(r - 4, 0), R - 8)
            return range(r0, r0 + 8)

        with contextlib.ExitStack() as st:
            MASK = C.sb(st, "MASK", [128, 2, 16, 64], F32, dma=True)
            C.dma("sp", out=MASK[:], in_=self.na_mask[:, :, :, :], sbt=MASK, writes=[MASK])
            QH = [C.sb(st, "QH", [128, T], BF16, dma=True) for _ in range(2)]
            KH = [C.sb(st, "KH", [128, T], BF16, dma=True) for _ in range(2)]
            VH = [C.sb(st, "VH", [128, T // 128, 128], BF16, dma=True) for _ in range(2)]
            GH = [C.sb(st, "GH", [128, 16, 64], F32, dma=True) for _ in range(2)]
            TBL = [C.sb(st, "TBL", [128, 2, 16, 64], F32) for _ in range(2)]
            OTH = [C.sb(st, "OTH", [128, T], BF16, dma=True) for _ in range(2)]
            sc = [C.sb(st, "sc", [128, 256], F32) for _ in range(3)]
            PT = [C.sb(st, "PT", [128, 256], BF16) for _ in range(3)]
            rden = [C.sb(st, "rden", [128, 256], F32) for _ in range(2)]
            it = 0
            for hd in range(16):
                qh, kh, vh, gh, tbl, oth = QH[hd % 2], KH[hd % 2], VH[hd % 2], GH[hd % 2], TBL[hd % 2], OTH[hd % 2]
                C.dma("sp", out=qh[:], in_=self.na_qk[hd * 128:(hd + 1) * 128, :], sbt=qh, writes=[qh])
                C.dma("sp", out=kh[:], in_=self.na_qk[D + hd * 128:D + (hd + 1) * 128, :], sbt=kh, writes=[kh])
                C.dma("sp", out=vh[:], in_=self.na_v[:, hd * 128:(hd + 1) * 128].rearrange("(n p) d -> p n d", p=128), sbt=vh, writes=[vh])
                C.dma("sp", out=gh[:], in_=self.na_g[hd], sbt=gh, writes=[gh])
                for kd in range(2):
                    C.op("pool", "tensor_tensor", reads=[gh, MASK], writes=[tbl], out=tbl[:, kd], in0=gh[:], in1=MASK[:, kd], op=ALU.add)
                for gq in range(R // 4):
                    rows = list(range(4 * gq, 4 * gq + 4))
                    chunks = sorted(set(a // 2 for r in rows for a in win(r)))
                    kind = [1 if (r < 4 or r >= R - 3) else 0 for r in rows]
                    runs = []
                    for ri, r in enumerate(rows):
                        if runs and runs[-1][2] == kind[ri]:
                            runs[-1][1] += 1
                        else:
                            runs.append([ri, 1, kind[ri]])
                    po, pd = PS[4 + gq % 2], PS[6 + gq % 2]
                    for ci, i in enumerate(chunks):
                        pss = PS[it % 4]
                        scb, ptb = sc[it % 3], PT[it % 3]
                        it += 1
                        C.op("pe", "matmul", reads=[kh, qh], writes=[pss], out=pss[:, :256], lhsT=kh[:, i * 128:(i + 1) * 128],
                             rhs=qh[:, gq * 256:(gq + 1) * 256], start=True, stop=True)
                        for (ri, nr, kd) in runs:
                            m0 = rows[ri] - 2 * i + 7
                            assert 0 <= m0 and m0 + nr <= 16, (m0, nr)
                            C.op("dve", "scalar_tensor_tensor", reads=[pss, tbl], writes=[scb], out=scb[:, ri * 64:(ri + nr) * 64],
                                 in0=pss[:, ri * 64:(ri + nr) * 64], scalar=scale,
                                 in1=tbl[:, kd, m0:m0 + nr, :].rearrange("p m c -> p (m c)"), op0=ALU.mult, op1=ALU.add)
                        C.op("act", "activation", reads=[scb], writes=[ptb], out=ptb[:], in_=scb[:], func=AF.Exp)
                        C.op("pe", "matmul", reads=[vh, ptb], writes=[po], out=po[:, :256], lhsT=vh[:, i, :], rhs=ptb[:],
                             start=(ci == 0), stop=(ci == len(chunks) - 1))
                        C.op("pe", "matmul", reads=[self.onesb, ptb], writes=[pd], out=pd[:, :256], lhsT=self.onesb[:], rhs=ptb[:],
                             start=(ci == 0), stop=(ci == len(chunks) - 1))
                    rd = rden[gq % 2]
                    C.op("dve", "reciprocal", reads=[pd], writes=[rd], out=rd[:], in_=pd[:, :256])
                    C.op("dve", "tensor_tensor", reads=[po, rd], writes=[oth], out=oth[:, gq * 256:(gq + 1) * 256], in0=po[:, :256], in1=rd[:],
                         op=ALU.mult)
                C.dma("pool", out=self.na_o[hd * 128:(hd + 1) * 128, :], in_=oth[:], sbt=oth, reads=[oth])
            C.barrier()
        self.out_proj(xsrc, xdst, self.na_o, KC, wo)

    def mix_gdn(self, l, s, xsrc, xdst):
        C, T, PS = self.C, self.T, self.PS
        BTG = 256
        L = 64
        NCK = T // L
        wqkv, wz, wab, wo = self.ws["gdn_qkv"], self.ws["gdn_z"], self.ws["gdn_ab"], self.ws["gdn_o"]

        def bc(ap, axis, shape):
            return ap.unsqueeze(axis).to_broadcast(list(shape))

        with contextlib.ExitStack() as st:
            DTB = C.sb(st, "DTB", [128, 64], F32)
            NRATE = C.sb(st, "NRATE", [128, 64], F32)
            with contextlib.ExitStack() as tmp:
                self.bcast_row(DTB, tmp, "dtb", self.gdn_dt_bias[0:1].rearrange("o d h -> o (d h)"), 64)
                self.bcast_row(NRATE, tmp, "alog", self.gdn_a_log[0:1].rearrange("o d h -> o (d h)"), 64)
                C.op("act", "activation", reads=[NRATE], writes=[NRATE], out=NRATE[:], in_=NRATE[:], func=AF.Exp)
                C.op("dve", "tensor_scalar", reads=[NRATE], writes=[NRATE], out=NRATE[:], in0=NRATE[:], scalar1=-1.0, scalar2=None, op0=ALU.mult)
                C.barrier()
            self.ensure_eps(st)
            oneb = C.sb(st, "oneb", [128, 1], F32)
            C.op("dve", "memset", writes=[oneb], ap=oneb[:], constant=1.0)
            XB = C.sb(st, "XB", [128, KC, BTG + 4], F32, dma=True)
            H = C.sb(st, "H", [128, KC, BTG + 4], BF16)
            WQ = [C.sb(st, "WQ", [128, KC, 256], BF16, dma=True) for _ in range(2)]
            WZ = [C.sb(st, "WZ", [128, KC, 512], BF16, dma=True) for _ in range(2)]
            WAB = C.sb(st, "WAB", [128, KC, 128], BF16, dma=True)
            C.dma("sp", out=WAB[:], in_=wab[0, 0], sbt=WAB, writes=[WAB])
            QKst = C.sb(st, "QKst", [128, 32, BTG], BF16, dma=True)
            KVt = [C.sb(st, "KVt", [128, 6144], BF16, dma=True) for _ in range(2)]
            ZS = [C.sb(st, "ZS", [128, 4096], F32, dma=True) for _ in range(2)]
            GBs = [C.sb(st, "GBs", [128, 128], F32, dma=True) for _ in range(2)]
            g0b = [C.sb(st, "g0", [128, BTG], F32) for _ in range(2)]
            g1b = [C.sb(st, "g1", [128, BTG], F32) for _ in range(2)]
            vlb = [C.sb(st, "vl", [128, BTG], F32) for _ in range(2)]
            sqb2 = [C.sb(st, "sq2", [128, BTG], F32) for _ in range(2)]
            rnb = [C.sb(st, "rn", [128, BTG], F32) for _ in range(2)]
            sqb = [C.sb(st, "sq", [128, 512], F32) for _ in range(4)]
            rstd = C.sb(st, "rstd", [128, 512], F32)
            tab = [C.sb(st, "tab", [128, 64], F32) for _ in range(2)]
            iq = iz = 0
            for t0 in range(0, T, BTG):
                nbi = min(BTG, T - t0)
                nb = nbi + 2
                ntt = nbi // 128
                self.load_xblock(XB, xsrc, t0, nbi, 1)
                self.rms_to_h(XB, H, nb, "norm_mix", l, sqb, rstd, PS[4])
                for jt in range(32):
                    W = WQ[iq % 2]
                    iq += 1
                    C.dma("sp", out=W[:], in_=wqkv[jt, 0], sbt=W, writes=[W])
                    for jj in range(2):
                        j = jt * 2 + jj
                        ps = PS[j % 2]
                        for k in range(KC):
                            C.op("pe", "matmul", reads=[W, H], writes=[ps], out=ps[:, :nb], lhsT=W[:, k, jj * 128:(jj + 1) * 128],
                                 rhs=H[:, k, :nb], start=(k == 0), stop=(k == KC - 1))
                        g0, g1, vl = g0b[j % 2], g1b[j % 2], vlb[j % 2]
                        cw = lambda kk: self.sm("gdn_cw", kk * 64 + j)
                        C.op("act", "activation", reads=[ps, self.small], writes=[g0], out=g0[:, :nbi], in_=ps[:, 1:1 + nbi], func=AF.Identity,
                             scale=cw(1))
                        C.op("dve", "scalar_tensor_tensor", reads=[ps, g0, self.small], writes=[g1], out=g1[:, :nbi], in0=ps[:, 0:nbi],
                             scalar=cw(0), in1=g0[:, :nbi], op0=ALU.mult, op1=ALU.add)
                        C.op("dve", "scalar_tensor_tensor", reads=[ps, g1, self.small], writes=[g0], out=g0[:, :nbi], in0=ps[:, 2:2 + nbi],
                             scalar=cw(2), in1=g1[:, :nbi], op0=ALU.mult, op1=ALU.add)
                        C.op("act", "activation", reads=[g0], writes=[vl], out=vl[:, :nbi], in_=g0[:, :nbi], func=AF.Silu)
                        if j < 32:
                            sq, rn = sqb2[j % 2], rnb[j % 2]
                            pn = PS[4 + j % 2]
                            C.op("pool", "tensor_tensor", reads=[vl], writes=[sq], out=sq[:, :nbi], in0=vl[:, :nbi], in1=vl[:, :nbi], op=ALU.mult)
                            C.op("pe", "matmul", reads=[sq, self.ones], writes=[pn], out=pn[:, :nbi], lhsT=self.ones[:], rhs=sq[:, :nbi],
                                 start=True, stop=True)
                            C.op("act", "activation", reads=[pn], writes=[rn], out=rn[:, :nbi], in_=pn[:, :nbi], func=AF.Sqrt, scale=1.0,
                                 bias=self.epsb[:, 0:1])
                            C.op("dve", "reciprocal", reads=[rn], writes=[rn], out=rn[:, :nbi], in_=rn[:, :nbi])
                            if j < 16:
                                C.op("dve", "scalar_tensor_tensor", reads=[vl, rn], writes=[QKst], out=QKst[:, j, :nbi], in0=vl[:, :nbi],
                                     scalar=128 ** -0.5, in1=rn[:, :nbi], op0=ALU.mult, op1=ALU.mult)
                            else:
                                C.op("dve", "tensor_tensor", reads=[vl, rn], writes=[vl], out=vl[:, :nbi], in0=vl[:, :nbi], in1=rn[:, :nbi],
                                     op=ALU.mult)
                                C.op("act", "activation", reads=[vl], writes=[QKst], out=QKst[:, j, :nbi], in_=vl[:, :nbi], func=AF.Copy)
                        if j >= 16:
                            for tt in range(ntt):
                                pt = PS[6 + tt % 2]
                                C.op("pe", "transpose", reads=[vl, self.ident], writes=[pt], out=pt[:, 0:128], in_=vl[:, tt * 128:(tt + 1) * 128],
                                     identity=self.ident[:])
                                C.op("act" if tt % 2 == 0 else "dve", "activation" if tt % 2 == 0 else "tensor_copy", reads=[pt], writes=[KVt[tt]],
                                     out=KVt[tt][:, (j - 16) * 128:(j - 15) * 128], in_=pt[:, 0:128], **({"func": AF.Copy} if tt % 2 == 0 else {}))
                C.dma("pool", out=self.gdn_qk.rearrange("(c p) t -> p c t", p=128)[:, :, t0:t0 + nbi], in_=QKst[:, :, :nbi], sbt=QKst, reads=[QKst])
                for tt in range(ntt):
                    C.dma("pool", out=self.gdn_kv[t0 + tt * 128:t0 + (tt + 1) * 128, :], in_=KVt[tt][:], sbt=KVt[tt], reads=[KVt[tt]])
                for cg in range(8):
                    W = WZ[iz % 2]
                    iz += 1
                    C.dma("sp", out=W[:], in_=wz[cg, 0], sbt=W, writes=[W])
                    for tt in range(ntt):
                        ps = PS[2 + (cg * ntt + tt) % 2]
                        for k in range(KC):
                            C.op("pe", "matmul", reads=[W, H], writes=[ps], out=ps[:, :512], lhsT=H[:, k, 1 + tt * 128:1 + (tt + 1) * 128],
                                 rhs=W[:, k, :], start=(k == 0), stop=(k == KC - 1))
                        C.op("act", "activation", reads=[ps], writes=[ZS[tt]], out=ZS[tt][:, cg * 512:(cg + 1) * 512], in_=ps[:, :512], func=AF.Silu)
                for tt in range(ntt):
                    C.dma("pool", out=self.gdn_z[t0 + tt * 128:t0 + (tt + 1) * 128, :], in_=ZS[tt][:], sbt=ZS[tt], reads=[ZS[tt]])
                for tt in range(ntt):
                    ps = PS[tt % 2]
                    for k in range(KC):
                        C.op("pe", "matmul", reads=[WAB, H], writes=[ps], out=ps[:, :128], lhsT=H[:, k, 1 + tt * 128:1 + (tt + 1) * 128],
                             rhs=WAB[:, k, :], start=(k == 0), stop=(k == KC - 1))
                    pv = ps[:, 0:128].rearrange("p (d w h) -> p d w h", d=2, w=2)
                    ta_, gb = tab[tt % 2], GBs[tt % 2]
                    C.op("dve", "tensor_tensor", reads=[ps, DTB], writes=[ta_], out=ta_[:].rearrange("p (d h) -> p d h", d=2), in0=pv[:, :, 0, :],
                         in1=DTB[:].rearrange("p (d h) -> p d h", d=2), op=ALU.add)
                    C.op("act", "activation", reads=[ta_], writes=[ta_], out=ta_[:], in_=ta_[:], func=AF.Exp)
                    C.op("act", "activation", reads=[ta_, oneb], writes=[ta_], out=ta_[:], in_=ta_[:], func=AF.Ln, bias=oneb[:, 0:1], scale=1.0)
                    C.op("dve", "tensor_tensor", reads=[ta_, NRATE], writes=[gb], out=gb[:, 0:64], in0=ta_[:], in1=NRATE[:], op=ALU.mult)
                    C.op("act", "activation", reads=[ps], writes=[gb], out=gb[:, 64:128].rearrange("p (d h) -> p d h", d=2), in_=pv[:, :, 1, :],
                         func=AF.Sigmoid)
                    C.dma("pool", out=self.gdn_gb[t0 + tt * 128:t0 + (tt + 1) * 128, :], in_=gb[:], sbt=gb, reads=[gb])
            C.barrier()

        dbg = int(os.environ.get("GDN_DBG", "9"))
        with contextlib.ExitStack() as st:
            self.ensure_eps(st)
            ONORM = C.sb(st, "ONORM", [128, 128], F32)
            with contextlib.ExitStack() as tmp:
                self.bcast_row(ONORM, tmp, "onorm", self.gdn_out_norm[0:1, :], 128)
                C.barrier()
            CONS = C.sb(st, "CONS", [64, 8, 64], F32, dma=True)
            C.dma("sp", out=CONS[:], in_=self.gdn_consts[:, :, :], sbt=CONS, writes=[CONS])
            S = C.sb(st, "S", [128, 4096], F32)
            Sb = C.sb(st, "Sb", [128, 4096], BF16)
            QKB = [C.sb(st, "QKB", [128, 32, 256], BF16, dma=True) for _ in range(2)]
            KVc = [C.sb(st, "KVc", [64, 6144], BF16, dma=True) for _ in range(2)]
            GBc = [C.sb(st, "GBc", [64, 128], F32, dma=True) for _ in range(2)]
            OC = C.sb(st, "OC", [64, 4096], F32, dma=True)
            OF = C.sb(st, "OF", [64, 2048], F32, dma=True)
            ZC = C.sb(st, "ZC", [64, 2048], F32, dma=True)
            OTst = C.sb(st, "OTst", [128, 32, 128], BF16, dma=True)
            EG = C.sb(st, "EG", [128, 96], F32)
            GAM = C.sb(st, "GAM", [64, 32], F32)
            NEGEG = C.sb(st, "NEGEG", [64, 32], F32)
            NEGB = C.sb(st, "NEGB", [64, 32], F32)
            RS = C.sb(st, "RS", [64, 32], F32)
            W2 = lambda nm, dt=F32: [C.sb(st, nm, [64, 8, 64], dt) for _ in range(2)]
            GTRI, DT, PTb = W2("GTRI"), W2("DT"), W2("PT", BF16)
            CK = [W2("CKa"), W2("CKb")]
            CKT = [W2("CKTa"), W2("CKTb")]
            RR = [W2("RRa"), W2("RRb")]
            W4 = lambda nm, dt=F32, p=64: [C.sb(st, nm, [p, 512], dt) for _ in range(2)]
            TKS, VP, UB, O1, KD = W4("TKS"), W4("VP"), W4("UB", BF16), W4("O1"), W4("KD", BF16)
            TS = W4("TS", F32, 128)
            ident64 = self.ident[0:64, 0:64]
            ones64 = self.ones[0:64, 0:64]
            iblk = 0
            ig = 0
            for dr in range(2):
                U, SUF, MNEG, ST01 = CONS[:, dr, :], CONS[:, 2 + dr, :], CONS[:, 4 + dr, :], CONS[:, 6 + dr, :]
                C.op("dve", "memset", writes=[S], ap=S[:], constant=0.0)
                C.op("pool", "memset", writes=[Sb], ap=Sb[:], constant=0.0)
                qkb = None
                for ci in range(NCK if dbg >= 1 else 0):
                    c = ci if dr == 0 else NCK - 1 - ci
                    if qkb is None or (c % 4 == (0 if dr == 0 else 3)):
                        qkb = QKB[iblk % 2]
                        iblk += 1
                        b0 = (c // 4) * 256
                        C.dma("sp", out=qkb[:], in_=self.gdn_qk.rearrange("(c p) t -> p c t", p=128)[:, :, b0:b0 + 256], sbt=qkb, writes=[qkb])
                    cc = slice((c % 4) * 64, (c % 4) * 64 + 64)
                    kv, gbc = KVc[ci % 2], GBc[ci % 2]
                    C.dma("sp", out=kv[:], in_=self.gdn_kv[c * 64:(c + 1) * 64, :], sbt=kv, writes=[kv])
                    C.dma("sp", out=gbc[:], in_=self.gdn_gb[c * 64:(c + 1) * 64, :], sbt=gbc, writes=[gbc])
                    g = gbc[:, dr * 32:(dr + 1) * 32]
                    beta = gbc[:, 64 + dr * 32:64 + (dr + 1) * 32]
                    p0 = PS[0]
                    C.op("pe", "matmul", reads=[CONS, gbc], writes=[p0], out=p0[0:64, 0:32], lhsT=U, rhs=g, start=True, stop=True)
                    C.op("pe", "matmul", reads=[CONS, gbc], writes=[p0], out=p0[0:64, 32:64], lhsT=SUF, rhs=g, start=True, stop=True)
                    C.op("pe", "matmul", reads=[self.ones, gbc], writes=[p0], out=p0[:, 64:96], lhsT=self.ones[0:64, :], rhs=g, start=True, stop=True)
                    C.op("act", "activation", reads=[p0], writes=[EG], out=EG[0:64, 0:64], in_=p0[0:64, 0:64], func=AF.Exp)
                    C.op("act", "activation", reads=[p0], writes=[EG], out=EG[:, 64:96], in_=p0[:, 64:96], func=AF.Exp)
                    C.op("dve", "tensor_copy", reads=[p0], writes=[GAM], out=GAM[:], in_=p0[0:64, 0:32])
                    C.op("dve", "tensor_scalar", reads=[EG], writes=[NEGEG], out=NEGEG[:], in0=EG[0:64, 0:32], scalar1=-1.0, scalar2=None, op0=ALU.mult)
                    C.op("dve", "tensor_scalar", reads=[gbc], writes=[NEGB], out=NEGB[:], in0=beta, scalar1=-1.0, scalar2=None, op0=ALU.mult)
                    for hg in range(4 if dbg >= 2 else 0):
                        h0, qk0 = 8 * hg, 4 * hg
                        i2 = ig % 2
                        ig += 1
                        p1 = PS[1]
                        for a in range(4):
                            kT = qkb[:, 16 + qk0 + a, cc]
                            C.op("pe", "matmul", reads=[qkb], writes=[p1], out=p1[0:64, a * 64:(a + 1) * 64], lhsT=kT, rhs=kT, start=True, stop=True)
                        for a in range(4):
                            C.op("pe", "matmul", reads=[qkb], writes=[p1], out=p1[0:64, 256 + a * 64:256 + (a + 1) * 64], lhsT=qkb[:, 16 + qk0 + a, cc],
                                 rhs=qkb[:, qk0 + a, cc], start=True, stop=True)
                        gtri, dt_, es, bb, ct0, ptb = GTRI[i2], DT[i2], GTRI[i2], CK[0][i2], CKT[0][i2], PTb[i2]
                        C.op("dve", "tensor_tensor", reads=[CONS, gbc], writes=[gtri], out=gtri[:], in0=bc(U, 1, [64, 8, 64]),
                             in1=bc(g[:, h0:h0 + 8], 2, [64, 8, 64]), op=ALU.mult)
                        p2 = PS[2]
                        for hh in range(8):
                            C.op("pe", "matmul", reads=[self.ones, gtri], writes=[p2], out=p2[0:64, hh * 64:(hh + 1) * 64], lhsT=ones64, rhs=gtri[:, hh, :],
                                 start=True, stop=True)
                        v8 = lambda p: p[0:64, 0:512].rearrange("p (h i) -> p h i", h=8)
                        C.op("dve", "tensor_tensor", reads=[p2, GAM], writes=[dt_], out=dt_[:], in0=v8(p2), in1=bc(GAM[:, h0:h0 + 8], 2, [64, 8, 64]),
                             op=ALU.subtract)
                        C.op("pool", "tensor_tensor", reads=[dt_, CONS], writes=[dt_], out=dt_[:], in0=dt_[:], in1=bc(MNEG, 1, [64, 8, 64]), op=ALU.add)
                        C.op("act", "activation", reads=[dt_], writes=[dt_], out=dt_[:], in_=dt_[:], func=AF.Exp)
                        C.op("pool", "tensor_tensor", reads=[dt_, CONS], writes=[es], out=es[:], in0=dt_[:], in1=bc(ST01, 1, [64, 8, 64]), op=ALU.mult)
                        kkv = p1[0:64, 0:256].rearrange("p (a i) -> p a i", a=4).unsqueeze(2).to_broadcast([64, 4, 2, 64])
                        kqv = p1[0:64, 256:512].rearrange("p (a i) -> p a i", a=4).unsqueeze(2).to_broadcast([64, 4, 2, 64])
                        r4 = lambda t: t[:].rearrange("p (a r) i -> p a r i", r=2)
                        C.op("dve", "tensor_tensor", reads=[p1, es], writes=[bb], out=r4(bb), in0=kkv, in1=r4(es), op=ALU.mult)
                        C.op("dve", "tensor_tensor", reads=[bb, NEGB], writes=[bb], out=bb[:], in0=bb[:], in1=bc(NEGB[:, h0:h0 + 8], 2, [64, 8, 64]),
                             op=ALU.mult)
                        C.op("dve", "tensor_tensor", reads=[p1, dt_], writes=[ptb], out=r4(ptb), in0=kqv, in1=r4(dt_), op=ALU.mult)
                        if dbg < 3:
                            continue
                        p3 = PS[3]
                        for hh in range(8):
                            C.op("pe", "matmul", reads=[bb, self.ident], writes=[p3], out=p3[0:64, hh * 64:(hh + 1) * 64], lhsT=bb[:, hh, :],
                                 rhs=ident64, start=True, stop=True)
                        sub = int(os.environ.get("GDN_SUB", "3"))
                        if sub & 1:
                            C.op("dve", "tensor_copy", reads=[p3], writes=[ct0], out=ct0[:], in_=v8(p3))
                        ck, ckt = bb, ct0
                        rr = RR[0][i2]
                        if sub & 2:
                            C.op("pool", "tensor_tensor", reads=[bb, self.ident], writes=[rr], out=rr[:], in0=bb[:], in1=bc(ident64, 1, [64, 8, 64]), op=ALU.add)
                        for kk in range(1, 6 if dbg >= 4 else 1):
                            cn, cnt, rn_ = CK[kk % 2][i2], CKT[kk % 2][i2], RR[kk % 2][i2]
                            pa, pb = PS[2], PS[3]
                            for hh in range(8):
                                C.op("pe", "matmul", reads=[ck, ckt], writes=[pa], out=pa[0:64, hh * 64:(hh + 1) * 64], lhsT=ck[:, hh, :], rhs=ckt[:, hh, :],
                                     start=True, stop=True)
                            C.op("dve", "tensor_copy", reads=[pa], writes=[cnt], out=cnt[:], in_=v8(pa))
                            if kk < 5:
                                for hh in range(8):
                                    C.op("pe", "matmul", reads=[ck, ckt], writes=[pb], out=pb[0:64, hh * 64:(hh + 1) * 64], lhsT=ckt[:, hh, :],
                                         rhs=ck[:, hh, :], start=True, stop=True)
                                C.op("dve", "tensor_copy", reads=[pb], writes=[cn], out=cn[:], in_=v8(pb))
                            pc = PS[4]
                            for hh in range(8):
                                C.op("pe", "matmul", reads=[cnt, rr], writes=[pc], out=pc[0:64, hh * 64:(hh + 1) * 64], lhsT=cnt[:, hh, :], rhs=rr[:, hh, :],
                                     start=True, stop=True)
                            C.op("dve", "tensor_tensor", reads=[pc, rr], writes=[rn_], out=rn_[:], in0=rr[:], in1=v8(pc), op=ALU.add)
                            ck, ckt, rr = cn, cnt, rn_
                        sub2 = int(os.environ.get("GDN_SUB2", "9"))
                        for hv in range(2 if dbg >= 5 else 0):
                            hd0 = h0 + 4 * hv
                            tks, vp, ub, o1, kd, ts = TKS[hv], VP[hv], UB[hv], O1[hv], KD[hv], TS[hv]
                            v4 = lambda p, np_=64: p[0:np_, 0:512].rearrange("p (a d) -> p a d", a=4)
                            p5, p6, p7 = PS[5], PS[6], PS[7]
                            for a in range(4):
                                hd = hd0 + a
                                C.op("pe", "matmul", reads=[qkb, Sb], writes=[p5], out=p5[0:64, a * 128:(a + 1) * 128], lhsT=qkb[:, 16 + hd // 2, cc],
                                     rhs=Sb[:, hd * 128:(hd + 1) * 128], start=True, stop=True)
                            for a in range(4):
                                hd = hd0 + a
                                C.op("pe", "matmul", reads=[qkb, Sb], writes=[p7], out=p7[0:64, a * 128:(a + 1) * 128], lhsT=qkb[:, hd // 2, cc],
                                     rhs=Sb[:, hd * 128:(hd + 1) * 128], start=True, stop=True)
                            C.op("dve", "tensor_tensor", reads=[p5, NEGEG], writes=[tks], out=v4(tks), in0=v4(p5), in1=bc(NEGEG[:, hd0:hd0 + 4], 2, [64, 4, 128]),
                                 op=ALU.mult)
                            C.op("pool", "tensor_tensor", reads=[tks, kv], writes=[vp], out=vp[:], in0=tks[:], in1=kv[:, 2048 + hd0 * 128:2048 + (hd0 + 4) * 128],
                                 op=ALU.add)
                            C.op("dve", "tensor_tensor", reads=[p7, EG], writes=[o1], out=v4(o1), in0=v4(p7), in1=bc(EG[0:64, hd0:hd0 + 4], 2, [64, 4, 128]),
                                 op=ALU.mult)
                            if sub2 < 2:
                                continue
                            for a in range(4):
                                C.op("pe", "matmul", reads=[rr, vp], writes=[p6], out=p6[0:64, a * 128:(a + 1) * 128], lhsT=rr[:, 4 * hv + a, :],
                                     rhs=vp[:, a * 128:(a + 1) * 128], start=True, stop=True)
                            C.op("dve", "tensor_tensor", reads=[p6, gbc], writes=[ub], out=v4(ub), in0=v4(p6), in1=bc(beta[:, hd0:hd0 + 4], 2, [64, 4, 128]),
                                 op=ALU.mult)
                            if sub2 < 3:
                                continue
                            for a in range(4):
                                C.op("pe", "matmul", reads=[ptb, ub], writes=[p5], out=p5[0:64, a * 128:(a + 1) * 128], lhsT=ptb[:, 4 * hv + a, :],
                                     rhs=ub[:, a * 128:(a + 1) * 128], start=True, stop=True)
                            C.op("dve", "tensor_tensor", reads=[p5, o1], writes=[OC], out=OC[:, hd0 * 128:(hd0 + 4) * 128], in0=o1[:], in1=p5[0:64, 0:512],
                                 op=ALU.add)
                            if sub2 < 4:
                                continue
                            kq0 = hd0 // 2
                            ktok = kv[:, kq0 * 128:(kq0 + 2) * 128].rearrange("p (a d) -> p a d", a=2).unsqueeze(2).to_broadcast([64, 2, 2, 128])
                            C.op("dve", "tensor_tensor", reads=[kv, EG], writes=[kd], out=kd[:].rearrange("p (a r d) -> p a r d", a=2, r=2), in0=ktok,
                                 in1=bc(EG[0:64, 32 + hd0:32 + hd0 + 4], 2, [64, 4, 128]).rearrange("p (a r) d -> p a r d", a=2), op=ALU.mult)
                            if sub2 < 5:
                                continue
                            for a in range(4):
                                C.op("pe", "matmul", reads=[kd, ub], writes=[p6], out=p6[:, a * 128:(a + 1) * 128], lhsT=kd[:, a * 128:(a + 1) * 128],
                                     rhs=ub[:, a * 128:(a + 1) * 128], start=True, stop=True)
                            ssl = S[:, hd0 * 128:(hd0 + 4) * 128]
                            C.op("pool", "tensor_tensor", reads=[S, EG], writes=[ts], out=v4(ts, 128), in0=ssl.rearrange("p (a d) -> p a d", a=4),
                                 in1=bc(EG[:, 64 + hd0:64 + hd0 + 4], 2, [128, 4, 128]), op=ALU.mult)
                            C.op("dve", "tensor_tensor", reads=[ts, p6], writes=[S], out=ssl, in0=ts[:], in1=p6[:, 0:512], op=ALU.add)
                            if sub2 < 6:
                                continue
                            C.op("pool", "tensor_copy", reads=[S], writes=[Sb], out=Sb[:, hd0 * 128:(hd0 + 4) * 128], in_=ssl)
                    if dbg < 6:
                        continue
                    if dr == 0:
                        C.dma("pool", out=self.gdn_of[c * 64:(c + 1) * 64, :], in_=OC[:], sbt=OC, reads=[OC])
                    else:
                        for hf in range(2):
                            cs_ = slice(hf * 2048, (hf + 1) * 2048)
                            C.dma("sp", out=OF[:], in_=self.gdn_of[c * 64:(c + 1) * 64, cs_], sbt=OF, writes=[OF])
                            C.dma("sp", out=ZC[:], in_=self.gdn_z[c * 64:(c + 1) * 64, cs_], sbt=ZC, writes=[ZC])
                            och = OC[:, cs_]
                            o3 = och.rearrange("p (h d) -> p h d", h=16)
                            rs = RS[:, hf * 16:(hf + 1) * 16]
                            C.op("pool", "tensor_tensor", reads=[OC, OF], writes=[OC], out=och, in0=och, in1=OF[:], op=ALU.add)
                            C.op("dve", "tensor_tensor", reads=[OC], writes=[OF], out=OF[:], in0=och, in1=och, op=ALU.mult)
                            C.op("dve", "tensor_reduce", reads=[OF], writes=[RS], out=rs, in_=OF[:].rearrange("p (h d) -> p h d", h=16), axis=AX.X, op=ALU.add)
                            C.op("act", "activation", reads=[RS], writes=[RS], out=rs, in_=rs, func=AF.Sqrt, scale=1.0 / 128, bias=self.epsb[0:64, 0:1])
                            C.op("dve", "reciprocal", reads=[RS], writes=[RS], out=rs, in_=rs)
                            C.op("dve", "tensor_tensor", reads=[OC, RS], writes=[OC], out=o3, in0=o3, in1=bc(rs, 2, [64, 16, 128]), op=ALU.mult)
                            C.op("pool", "tensor_tensor", reads=[OC, ONORM], writes=[OC], out=o3, in0=o3, in1=bc(ONORM[0:64, :], 1, [64, 16, 128]), op=ALU.mult)
                            C.op("dve", "tensor_tensor", reads=[OC, ZC], writes=[OC], out=och, in0=och, in1=ZC[:], op=ALU.mult)
                        for h8 in range(4):
                            pt = PS[4 + h8 % 2]
                            for hh in range(8):
                                hd = h8 * 8 + hh
                                C.op("pe", "transpose", reads=[OC, self.ident], writes=[pt], out=pt[:, hh * 64:(hh + 1) * 64], in_=OC[:, hd * 128:(hd + 1) * 128],
                                     identity=ident64)
                            dst = OTst[:, h8 * 8:(h8 + 1) * 8, (c % 2) * 64:(c % 2) * 64 + 64]
                            srcv = pt[:, 0:512].rearrange("p (h t) -> p h t", h=8)
                            C.op("dve", "tensor_copy", reads=[pt], writes=[OTst], out=dst, in_=srcv)
                        if c % 2 == 0:
                            b0 = (c // 2) * 128
                            C.dma("pool", out=self.gdn_o.rearrange("(c p) t -> p c t", p=128)[:, :, b0:b0 + 128], in_=OTst[:], sbt=OTst, reads=[OTst])
                C.barrier()
        self.out_proj(xsrc, xdst, self.gdn_o, 32, wo, BT=256)


    def mix_s5(self, l, s, xsrc, xdst):
        C, T, PS = self.C, self.T, self.PS
        L = 128
        NCH = T // L
        PI = float(np.pi)
        with contextlib.ExitStack() as st:
            self.ensure_eps(st)
            XB = [C.sb(st, "XB", [128, KC, 512], F32, dma=True) for _ in range(2)]
            HF = [C.sb(st, "HF", [128, KC, 512], F32, dma=True) for _ in range(2)]
            sqb = [C.sb(st, "sq", [128, 512], F32) for _ in range(4)]
            rstd = C.sb(st, "rstd", [128, 512], F32)
            ib = 0
            for t0 in range(0, T, 512):
                gn = min(512, T - t0)
                X, Hf = XB[ib % 2], HF[ib % 2]
                ib += 1
                self.load_xblock(X, xsrc, t0, gn, 0)
                self.rms_to_h(X, None, gn, "norm_mix", l, sqb, rstd, PS[4], out_f32=Hf)
                C.dma("pool", out=self.s5_h.rearrange("(c p) t -> p c t", p=128)[:, :, t0:t0 + gn], in_=Hf[:, :, :gn], sbt=Hf, reads=[Hf])
            C.barrier()

        def rev(t, a, b):
            return t[:, b - 1:a - 1:-1] if a > 0 else t[:, b - 1::-1]

        with contextlib.ExitStack() as st:
            IOTA = C.sb(st, "IOTA", [128, L], F32, dma=True)
            C.dma("sp", out=IOTA[:], in_=self.iota_in[:, :], sbt=IOTA, writes=[IOTA])
            LAM = C.sb(st, "LAM", [128, 2, 3, 64], F32, dma=True)
            C.dma("sp", out=LAM[:], in_=self.s5_lam[:, :, :, :], sbt=LAM, writes=[LAM])
            MUL = C.sb(st, "MUL", [128, 4, L], F32)
            C.op("dve", "memset", writes=[MUL], ap=MUL[:], constant=1.0)
            C.op("dve", "memset", writes=[MUL], ap=MUL[:, :, 0:1], constant=0.0)
            HB = [C.sb(st, "HB", [128, T], F32, dma=True) for _ in range(2)]
            YT = C.sb(st, "YT", [128, T], F32)
            YO = C.sb(st, "YO", [128, T], BF16, dma=True)
            Bt = [C.sb(st, "Bt", [128, 2, 4, 128], F32, dma=True) for _ in range(2)]
            Ct = [C.sb(st, "Ct", [128, 2, 4, 128], F32, dma=True) for _ in range(2)]
            tabs = [[C.sb(st, "tab", [128, 4, L], F32) for _ in range(4)] for _ in range(2)]
            CTt, SNt, GMt, GIt = [C.sb(st, "trig", [128, 4, L], F32) for _ in range(4)]
            ang = [C.sb(st, "ang", [128, L], F32) for _ in range(2)]
            angi = C.sb(st, "angi", [128, L], mybir.dt.int32)
            angf = C.sb(st, "angf", [128, L], F32)
            sm = C.sb(st, "s5sm", [128, 16, 4], F32)
            wk = [[C.sb(st, "wk", [128, 4, L], F32) for _ in range(2)] for _ in range(10)]
            ta = [C.sb(st, "ta", [128, 512], F32) for _ in range(2)]
            tb = [C.sb(st, "tb", [128, 512], F32) for _ in range(2)]
            tc_ = [C.sb(st, "tc", [128, 512], F32) for _ in range(2)]
            it = 0
            ifd = 0
            for fc in range(KC):
                hb = HB[fc % 2]
                C.dma("sp", out=hb[:], in_=self.s5_h[fc * 128:(fc + 1) * 128, :], sbt=hb, writes=[hb])
                for dr in range(2):
                    bt, ct = Bt[ifd % 2], Ct[ifd % 2]
                    T1r, T1i, T2r, T2i = tabs[ifd % 2]
                    ifd += 1
                    C.dma("sp", out=bt[:], in_=self.s5_B[dr, fc], sbt=bt, writes=[bt])
                    C.dma("sp", out=ct[:], in_=self.s5_C[dr, fc], sbt=ct, writes=[ct])
                    C.op("act", "activation", reads=[ct], writes=[ct], out=ct[:, 1], in_=ct[:, 1], func=AF.Copy, scale=-1.0)
                    are = LAM[:, dr, 0, 4 * fc:4 * fc + 4]
                    aim = LAM[:, dr, 1, 4 * fc:4 * fc + 4]
                    ldt = LAM[:, dr, 2, 4 * fc:4 * fc + 4]
                    S = lambda i: sm[:, i, :]
                    C.op("act", "activation", reads=[LAM], writes=[sm], out=S(0), in_=ldt, func=AF.Exp)
                    C.op("dve", "tensor_tensor", reads=[LAM, sm], writes=[sm], out=S(1), in0=are, in1=S(0), op=ALU.mult)
                    C.op("dve", "tensor_scalar", reads=[sm], writes=[sm], out=S(2), in0=S(1), scalar1=-1.0, scalar2=None, op0=ALU.mult)
                    C.op("dve", "tensor_tensor", reads=[LAM, sm], writes=[sm], out=S(3), in0=aim, in1=S(0), op=ALU.mult)
                    for q4 in range(4):
                        for (dst, off) in ((SNt, 0.0), (CTt, 0.5 * PI)):
                            a, a0 = ang[0], ang[1]
                            C1 = 6.28125
                            C2 = 2 * PI - C1
                            C.op("dve", "tensor_scalar", reads=[IOTA, sm], writes=[a0], out=a0[:], in0=IOTA[:], scalar1=sm[:, 3, q4:q4 + 1],
                                 scalar2=off, op0=ALU.mult, op1=ALU.add)
                            C.op("dve", "tensor_scalar", reads=[a0], writes=[angi], out=angi[:], in0=a0[:], scalar1=1.0 / (2 * PI), scalar2=None,
                                 op0=ALU.mult)
                            C.op("dve", "tensor_copy", reads=[angi], writes=[angf], out=angf[:], in_=angi[:])
                            C.op("dve", "scalar_tensor_tensor", reads=[angf, a0], writes=[a], out=a[:], in0=angf[:], scalar=-C1, in1=a0[:],
                                 op0=ALU.mult, op1=ALU.add)
                            C.op("dve", "scalar_tensor_tensor", reads=[angf, a], writes=[a], out=a[:], in0=angf[:], scalar=-C2, in1=a[:],
                                 op0=ALU.mult, op1=ALU.add)
                            C.op("dve", "tensor_scalar", reads=[a], writes=[a0], out=a0[:], in0=a[:], scalar1=PI, scalar2=-2 * PI, op0=ALU.is_gt,
                                 op1=ALU.mult)
                            C.op("dve", "tensor_tensor", reads=[a, a0], writes=[a], out=a[:], in0=a[:], in1=a0[:], op=ALU.add)
                            C.op("dve", "tensor_scalar", reads=[a], writes=[a0], out=a0[:], in0=a[:], scalar1=-PI, scalar2=2 * PI, op0=ALU.is_lt,
                                 op1=ALU.mult)
                            C.op("dve", "tensor_tensor", reads=[a, a0], writes=[a], out=a[:], in0=a[:], in1=a0[:], op=ALU.add)
                            C.op("act", "activation", reads=[a], writes=[dst], out=dst[:, q4, :], in_=a[:], func=AF.Sin)
                        C.op("act", "activation", reads=[IOTA, sm], writes=[GMt], out=GMt[:, q4, :], in_=IOTA[:], func=AF.Exp,
                             scale=sm[:, 1, q4:q4 + 1])
                        C.op("act", "activation", reads=[IOTA, sm], writes=[GIt], out=GIt[:, q4, :], in_=IOTA[:], func=AF.Exp,
                             scale=sm[:, 2, q4:q4 + 1])
                    rho, c1, s1 = GMt[:, :, 0], CTt[:, :, 0], SNt[:, :, 0]
                    tt_ = lambda o, a_, b_, op_, rd: C.op("dve", "tensor_tensor", reads=rd, writes=[sm], out=o, in0=a_, in1=b_, op=op_)
                    tt_(S(4), rho, c1, ALU.mult, [GMt, CTt])
                    C.op("dve", "tensor_scalar", reads=[sm], writes=[sm], out=S(4), in0=S(4), scalar1=-1.0, scalar2=None, op0=ALU.add)
                    tt_(S(5), rho, s1, ALU.mult, [GMt, SNt])
                    tt_(S(6), are, are, ALU.mult, [LAM])
                    tt_(S(7), aim, aim, ALU.mult, [LAM])
                    tt_(S(6), S(6), S(7), ALU.add, [sm])
                    C.op("dve", "reciprocal", reads=[sm], writes=[sm], out=S(6), in_=S(6))
                    tt_(S(7), S(4), are, ALU.mult, [sm, LAM])
                    tt_(S(8), S(5), aim, ALU.mult, [sm, LAM])
                    tt_(S(7), S(7), S(8), ALU.add, [sm])
                    tt_(S(9), S(7), S(6), ALU.mult, [sm])
                    tt_(S(7), S(5), are, ALU.mult, [sm, LAM])
                    tt_(S(8), S(4), aim, ALU.mult, [sm, LAM])
                    tt_(S(7), S(7), S(8), ALU.subtract, [sm])
                    tt_(S(10), S(7), S(6), ALU.mult, [sm])
                    C.op("dve", "tensor_scalar", reads=[sm], writes=[sm], out=S(11), in0=S(9), scalar1=-1.0, scalar2=None, op0=ALU.mult)
                    for q4 in range(4):
                        u = ang[q4 % 2]
                        C.op("dve", "tensor_scalar", reads=[CTt, sm], writes=[u], out=u[:], in0=CTt[:, q4, :], scalar1=sm[:, 9, q4:q4 + 1],
                             scalar2=None, op0=ALU.mult)
                        C.op("dve", "scalar_tensor_tensor", reads=[SNt, sm, u], writes=[u], out=u[:], in0=SNt[:, q4, :],
                             scalar=sm[:, 10, q4:q4 + 1], in1=u[:], op0=ALU.mult, op1=ALU.add)
                        C.op("dve", "tensor_tensor", reads=[u, GIt], writes=[T1r], out=T1r[:, q4, :], in0=u[:], in1=GIt[:, q4, :], op=ALU.mult)
                        C.op("dve", "tensor_scalar", reads=[CTt, sm], writes=[u], out=u[:], in0=CTt[:, q4, :], scalar1=sm[:, 10, q4:q4 + 1],
                             scalar2=None, op0=ALU.mult)
                        C.op("dve", "scalar_tensor_tensor", reads=[SNt, sm, u], writes=[u], out=u[:], in0=SNt[:, q4, :],
                             scalar=sm[:, 11, q4:q4 + 1], in1=u[:], op0=ALU.mult, op1=ALU.add)
                        C.op("dve", "tensor_tensor", reads=[u, GIt], writes=[T1i], out=T1i[:, q4, :], in0=u[:], in1=GIt[:, q4, :], op=ALU.mult)
                    C.op("dve", "tensor_tensor", reads=[GMt, CTt], writes=[T2r], out=T2r[:], in0=GMt[:], in1=CTt[:], op=ALU.mult)
                    C.op("dve", "tensor_tensor", reads=[GMt, SNt], writes=[T2i], out=T2i[:], in0=GMt[:], in1=SNt[:], op=ALU.mult)
                    xprev = None
                    for ci in range(NCH):
                        cb = ci if dr == 0 else NCH - 1 - ci
                        hcols = hb[:, cb * L:(cb + 1) * L] if dr == 0 else rev(hb, cb * L, (cb + 1) * L)
                        pr, pi_ = PS[it % 2], PS[2 + it % 2]
                        py = PS[4 + it % 2]
                        W = [wk[i][it % 2] for i in range(10)]
                        it += 1
                        for q4 in range(4):
                            C.op("pe", "matmul", reads=[bt, hb], writes=[pr], out=pr[:, q4 * L:(q4 + 1) * L], lhsT=bt[:, 0, q4, :], rhs=hcols,
                                 start=True, stop=True)
                        for q4 in range(4):
                            C.op("pe", "matmul", reads=[bt, hb], writes=[pi_], out=pi_[:, q4 * L:(q4 + 1) * L], lhsT=bt[:, 1, q4, :], rhs=hcols,
                                 start=True, stop=True)
                        v3 = lambda p: p[:, 0:4 * L].rearrange("p (q t) -> p q t", q=4)
                        m1, m2, m3, m4, wr, wi, sr, si, xr, xi = W
                        C.op("dve", "tensor_tensor", reads=[T1r, pr], writes=[m1], out=m1[:], in0=T1r[:], in1=v3(pr), op=ALU.mult)
                        C.op("dve", "tensor_tensor", reads=[T1i, pi_], writes=[m2], out=m2[:], in0=T1i[:], in1=v3(pi_), op=ALU.mult)
                        C.op("dve", "tensor_tensor", reads=[T1r, pi_], writes=[m3], out=m3[:], in0=T1r[:], in1=v3(pi_), op=ALU.mult)
                        C.op("dve", "tensor_tensor", reads=[T1i, pr], writes=[m4], out=m4[:], in0=T1i[:], in1=v3(pr), op=ALU.mult)
                        C.op("pool", "tensor_tensor", reads=[m1, m2], writes=[wr], out=wr[:], in0=m1[:], in1=m2[:], op=ALU.subtract)
                        C.op("pool", "tensor_tensor", reads=[m3, m4], writes=[wi], out=wi[:], in0=m3[:], in1=m4[:], op=ALU.add)
                        if xprev is not None:
                            C.op("dve", "tensor_tensor", reads=[wr, xprev[0]], writes=[wr], out=wr[:, :, 0:1], in0=wr[:, :, 0:1],
                                 in1=xprev[0][:, :, L - 1:L], op=ALU.add)
                            C.op("dve", "tensor_tensor", reads=[wi, xprev[1]], writes=[wi], out=wi[:, :, 0:1], in0=wi[:, :, 0:1],
                                 in1=xprev[1][:, :, L - 1:L], op=ALU.add)
                        f2 = lambda t: t[:].rearrange("p q t -> p (q t)")
                        C.op("dve", "tensor_tensor_scan", reads=[MUL, wr], writes=[sr], out=f2(sr), data0=f2(MUL), data1=f2(wr), initial=0.0,
                             op0=ALU.mult, op1=ALU.add)
                        C.op("dve", "tensor_tensor_scan", reads=[MUL, wi], writes=[si], out=f2(si), data0=f2(MUL), data1=f2(wi), initial=0.0,
                             op0=ALU.mult, op1=ALU.add)
                        C.op("pool", "tensor_tensor", reads=[T2r, sr], writes=[m1], out=m1[:], in0=T2r[:], in1=sr[:], op=ALU.mult)
                        C.op("pool", "tensor_tensor", reads=[T2i, si], writes=[m2], out=m2[:], in0=T2i[:], in1=si[:], op=ALU.mult)
                        C.op("dve", "tensor_tensor", reads=[T2r, si], writes=[m3], out=m3[:], in0=T2r[:], in1=si[:], op=ALU.mult)
                        C.op("dve", "tensor_tensor", reads=[T2i, sr], writes=[m4], out=m4[:], in0=T2i[:], in1=sr[:], op=ALU.mult)
                        C.op("pool", "tensor_tensor", reads=[m1, m2], writes=[xr], out=xr[:], in0=m1[:], in1=m2[:], op=ALU.subtract)
                        C.op("dve", "tensor_tensor", reads=[m3, m4], writes=[xi], out=xi[:], in0=m3[:], in1=m4[:], op=ALU.add)
                        xprev = (xr, xi)
                        for q4 in range(4):
                            C.op("pe", "matmul", reads=[ct, xr], writes=[py], out=py[:, :L], lhsT=ct[:, 0, q4, :], rhs=xr[:, q4, :],
                                 start=(q4 == 0), stop=False)
                        for q4 in range(4):
                            C.op("pe", "matmul", reads=[ct, xi], writes=[py], out=py[:, :L], lhsT=ct[:, 1, q4, :], rhs=xi[:, q4, :],
                                 start=False, stop=(q4 == 3))
                        if dr == 0:
                            C.op("act", "activation", reads=[py], writes=[YT], out=YT[:, cb * L:(cb + 1) * L], in_=py[:, :L], func=AF.Copy)
                        else:
                            yv = rev(YT, cb * L, (cb + 1) * L)
                            C.op("dve", "tensor_tensor", reads=[YT, py], writes=[YT], out=yv, in0=yv, in1=py[:, :L], op=ALU.add)
                for g0 in range(0, T, 512):
                    gn = min(512, T - g0)
                    i2 = (g0 // 512) % 2
                    C.op("dve", "scalar_tensor_tensor", reads=[hb, YT, self.small], writes=[tc_[i2]], out=tc_[i2][:, :gn], in0=hb[:, g0:g0 + gn],
                         scalar=self.sm("s5_d", fc), in1=YT[:, g0:g0 + gn], op0=ALU.mult, op1=ALU.add)
                    self.gelu_tanh(YO, YO[:, g0:g0 + gn], tc_[i2], tc_[i2][:, :gn], ta[i2], tb[i2], gn)
                C.dma("pool", out=self.s5_y[fc * 128:(fc + 1) * 128, :], in_=YO[:], sbt=YO, reads=[YO])
            C.barrier()
        wg = self.ws["s5_glu"]
        with contextlib.ExitStack() as st:
            XB = [C.sb(st, "XBo", [128, KC, 512], F32, dma=True) for _ in range(2)]
            AB = [C.sb(st, "ABo", [128, KC, 512], BF16, dma=True) for _ in range(2)]
            WA = [C.sb(st, "WA", [128, KC, 256], BF16, dma=True) for _ in range(2)]
            WB = [C.sb(st, "WB", [128, KC, 256], BF16, dma=True) for _ in range(2)]
            sgb = [C.sb(st, "sg", [128, 512], F32) for _ in range(2)]
            g0b = [C.sb(st, "g0", [128, 512], F32) for _ in range(2)]
            io = ib = 0
            for t0 in range(0, T, 512):
                gn = min(512, T - t0)
                X, Ab = XB[ib % 2], AB[ib % 2]
                ib += 1
                self.load_xblock(X, xsrc, t0, gn, 0)
                C.dma("sp", out=Ab[:, :, :gn], in_=self.s5_y.rearrange("(c p) t -> p c t", p=128)[:, :, t0:t0 + gn], sbt=Ab, writes=[Ab])
                for ot in range(8):
                    Wa, Wb = WA[io % 2], WB[io % 2]
                    io += 1
                    C.dma("sp", out=Wa[:], in_=wg[ot, 0], sbt=Wa, writes=[Wa])
                    C.dma("sp", out=Wb[:], in_=wg[8 + ot, 0], sbt=Wb, writes=[Wb])
                    for oo in range(2):
                        oc = ot * 2 + oo
                        pa, pb = PS[oc % 2], PS[2 + oc % 2]
                        for k in range(KC):
                            C.op("pe", "matmul", reads=[Wa, Ab], writes=[pa], out=pa[:, :gn], lhsT=Wa[:, k, oo * 128:(oo + 1) * 128],
                                 rhs=Ab[:, k, :gn], start=(k == 0), stop=(k == KC - 1))
                        for k in range(KC):
                            C.op("pe", "matmul", reads=[Wb, Ab], writes=[pb], out=pb[:, :gn], lhsT=Wb[:, k, oo * 128:(oo + 1) * 128],
                                 rhs=Ab[:, k, :gn], start=(k == 0), stop=(k == KC - 1))
                        sg, g0 = sgb[oc % 2], g0b[oc % 2]
                        C.op("act", "activation", reads=[pb], writes=[sg], out=sg[:, :gn], in_=pb[:, :gn], func=AF.Sigmoid)
                        C.op("dve", "tensor_tensor", reads=[sg, pa], writes=[g0], out=g0[:, :gn], in0=sg[:, :gn], in1=pa[:, :gn], op=ALU.mult)
                        C.op("pool", "tensor_tensor", reads=[X, g0], writes=[X], out=X[:, oc, :gn], in0=X[:, oc, :gn], in1=g0[:, :gn], op=ALU.add)
                dst = xdst.rearrange("(c p) t -> p c t", p=128)[:, :, t0:t0 + gn]
                C.dma("pool", out=dst, in_=X[:, :, :gn], sbt=X, reads=[X])
            C.barrier()

    def small_layout(self):
        def add(name, n):
            self.small_off[name] = self.small_n
            self.small_n += n
        add("norm_mix", DEPTH * KC)
        add("norm_ffn", DEPTH * KC)
        add("norm_ple", DEPTH * KC)
        add("final_norm", KC)
        add("conv_w", DEPTH * 3 * NFC)
        add("conv_b", DEPTH * NFC)
        add("s5_d", KC)
        add("gdn_cw", 3 * 64)

    def sm(self, name, idx):
        o = self.small_off[name] + idx
        return self.small[:, o:o + 1]

    def prologue(self, x_in):
        C, T = self.C, self.T
        with contextlib.ExitStack() as st:
            xtok = [C.sb(st, "xtok", [128, D], F32, dma=True) for _ in range(3)]
            stage = [C.sb(st, "stage", [128, KC, 512], F32, dma=True) for _ in range(2)]
            it = 0
            for s in range(self.NS):
                for g0 in range(0, T, 512):
                    gn = min(512, T - g0)
                    sg = stage[(g0 // 512) % 2]
                    for tt in range(0, gn, 128):
                        xt = xtok[it % 3]
                        it += 1
                        C.dma("sp", out=xt[:], in_=x_in[s, g0 + tt:g0 + tt + 128, :], sbt=xt, writes=[xt])
                        for c4 in range(0, KC, 4):
                            pst = self.PS[(c4 // 4) % 2 + 2 * ((tt // 128) % 2)]
                            for c in range(c4, c4 + 4):
                                C.op("pe", "transpose", reads=[xt, self.ident], writes=[pst],
                                     out=pst[:, (c - c4) * 128:(c - c4 + 1) * 128], in_=xt[:, c * 128:(c + 1) * 128],
                                     identity=self.ident[:])
                            eng = "dve" if (c4 // 4) % 2 == 0 else "act"
                            src = pst[:, 0:512].rearrange("p (c t) -> p c t", c=4)
                            if eng == "dve":
                                C.op("dve", "tensor_copy", reads=[pst], writes=[sg], out=sg[:, c4:c4 + 4, tt:tt + 128], in_=src)
                            else:
                                C.op("act", "activation", reads=[pst], writes=[sg], out=sg[:, c4:c4 + 4, tt:tt + 128], in_=src,
                                     func=AF.Copy)
                    dst = self.xT[0][s].rearrange("(c p) t -> p c t", p=128)[:, :, g0:g0 + gn]
                    C.dma("pool", out=dst, in_=sg[:, :, 0:gn], sbt=sg, reads=[sg])
            C.barrier()

    def rms_to_h(self, XB, H, nb, gain_name, l, sqb, rstd, psb, out_f32=None):
        C = self.C
        for c in range(KC):
            sq = sqb[c % len(sqb)]
            if c % 2 == 0:
                C.op("dve", "tensor_tensor", reads=[XB], writes=[sq], out=sq[:, :nb], in0=XB[:, c, :nb], in1=XB[:, c, :nb], op=ALU.mult)
            else:
                C.op("act", "activation", reads=[XB], writes=[sq], out=sq[:, :nb], in_=XB[:, c, :nb], func=AF.Square)
            C.op("pe", "matmul", reads=[sq, self.ones], writes=[psb], out=psb[:, :nb], lhsT=self.ones[:], rhs=sq[:, :nb],
                 start=(c == 0), stop=(c == KC - 1))
        C.op("act", "activation", reads=[psb], writes=[rstd], out=rstd[:, :nb], in_=psb[:, :nb], func=AF.Sqrt,
             scale=1.0 / D, bias=self.epsb[:, 0:1])
        C.op("dve", "reciprocal", reads=[rstd], writes=[rstd], out=rstd[:, :nb], in_=rstd[:, :nb])
        for c in range(KC):
            dst = H if out_f32 is None else out_f32
            C.op("dve", "scalar_tensor_tensor", reads=[XB, rstd, self.small], writes=[dst], out=dst[:, c, :nb], in0=XB[:, c, :nb],
                 scalar=self.sm(gain_name, l * KC + c), in1=rstd[:, :nb], op0=ALU.mult, op1=ALU.mult)

    def ensure_eps(self, st):
        C = self.C
        self.epsb = C.sb(st, "epsb", [128, 1], F32)
        C.op("dve", "memset", writes=[self.epsb], ap=self.epsb[:], constant=EPS)

    def load_xblock(self, XB, xsrc, t0, nbi, halo):
        C, T = self.C, self.T
        lo, hi = t0 - halo, t0 + nbi + halo
        clo, chi = max(lo, 0), min(hi, T)
        if clo > lo:
            C.op("dve", "memset", writes=[XB], ap=XB[:, :, 0:clo - lo], constant=0.0)
        if chi < hi:
            C.op("dve", "memset", writes=[XB], ap=XB[:, :, chi - lo:hi - lo], constant=0.0)
        src = xsrc.rearrange("(c p) t -> p c t", p=128)[:, :, clo:chi]
        C.dma("sp", out=XB[:, :, clo - lo:chi - lo], in_=src, sbt=XB, writes=[XB])

    def ffn_ple(self, l, s, xsrc, xdst, p_l):
        C, T = self.C, self.T
        PS = self.PS
        wgu, wdn, wpg, wpp = self.ws[("gu", l)], self.ws[("dn", l)], self.ws[("pg", l)], self.ws[("pp", l)]
        with contextlib.ExitStack() as st:
            self.ensure_eps(st)
            XB = C.sb(st, "XB", [128, KC, 512], F32, dma=True)
            H = C.sb(st, "H", [128, KC, 512], BF16)
            A = C.sb(st, "A", [128, 22, 512], BF16)
            WGU = [C.sb(st, "WGU", [128, 2, KC, 256], BF16, dma=True) for _ in range(2)]
            WD = [C.sb(st, "WD", [128, 11, 512], BF16, dma=True) for _ in range(3)]
            WPG = [C.sb(st, "WPG", [128, KC, 256], BF16, dma=True) for _ in range(2)]
            WPP = C.sb(st, "WPP", [128, 2, D], BF16, dma=True)
            sqb = [C.sb(st, "sq", [128, 512], F32) for _ in range(4)]
            rstd = C.sb(st, "rstd", [128, 512], F32)
            g0b = [C.sb(st, "g0", [128, 512], F32) for _ in range(2)]
            g1b = [C.sb(st, "g1", [128, 512], F32) for _ in range(2)]
            sgb = [C.sb(st, "sg", [128, 512], F32) for _ in range(2)]
            ptok = [C.sb(st, "ptok", [128, PLE], F32, dma=True) for _ in range(2)]
            PT = C.sb(st, "PT", [128, 2, 512], BF16)
            C.dma("sp", out=WPP[:], in_=wpp[0, 0], sbt=WPP, writes=[WPP])
            iw = 0
            idn = 0
            ipg = 0
            for (t0, nbi) in blocks_of(T):
                nb = nbi + 2
                self.load_xblock(XB, xsrc, t0, nbi, 1)
                self.rms_to_h(XB, H, nb, "norm_ffn", l, sqb, rstd, PS[4])
                for hf in range(2):
                    for jp in range(11):
                        jpg = hf * 11 + jp
                        W = WGU[iw % 2]
                        iw += 1
                        C.dma("sp", out=W[:, 0], in_=wgu[jpg, 0], sbt=W, writes=[W])
                        C.dma("sp", out=W[:, 1], in_=wgu[22 + jpg, 0], sbt=W, writes=[W])
                        for jj in range(2):
                            j = jpg * 2 + jj
                            ja = jp * 2 + jj
                            pg, pu = PS[j % 2], PS[2 + j % 2]
                            for k in range(KC):
                                C.op("pe", "matmul", reads=[W, H], writes=[pg], out=pg[:, :nb], lhsT=W[:, 0, k, jj * 128:(jj + 1) * 128],
                                     rhs=H[:, k, :nb], start=(k == 0), stop=(k == KC - 1))
                            for k in range(KC):
                                C.op("pe", "matmul", reads=[W, H], writes=[pu], out=pu[:, :nb], lhsT=W[:, 1, k, jj * 128:(jj + 1) * 128],
                                     rhs=H[:, k, :nb], start=(k == 0), stop=(k == KC - 1))
                            g0, g1, sg = g0b[j % 2], g1b[j % 2], sgb[j % 2]
                            cw = lambda kk: self.sm("conv_w", (l * 3 + kk) * NFC + j)
                            C.op("act", "activation", reads=[pg, self.small], writes=[g0], out=g0[:, :nbi], in_=pg[:, 1:1 + nbi],
                                 func=AF.Identity, scale=cw(1))
                            C.op("dve", "scalar_tensor_tensor", reads=[pg, g0, self.small], writes=[g1], out=g1[:, :nbi],
                                 in0=pg[:, 0:nbi], scalar=cw(0), in1=g0[:, :nbi], op0=ALU.mult, op1=ALU.add)
                            C.op("dve", "scalar_tensor_tensor", reads=[pg, g1, self.small], writes=[g0], out=g0[:, :nbi],
                                 in0=pg[:, 2:2 + nbi], scalar=cw(2), in1=g1[:, :nbi], op0=ALU.mult, op1=ALU.add)
                            C.op("act", "activation", reads=[g0, self.small], writes=[sg], out=sg[:, :nbi], in_=g0[:, :nbi],
                                 func=AF.Silu, bias=self.sm("conv_b", l * NFC + j), scale=1.0)
                            C.op("dve", "tensor_tensor", reads=[sg, pu], writes=[A], out=A[:, ja, :nbi], in0=sg[:, :nbi],
                                 in1=pu[:, 1:1 + nbi], op=ALU.mult)
                    for q in range(4):
                        for g in range(2):
                            Wd = WD[idn % 3]
                            idn += 1
                            C.dma("sp", out=Wd[:], in_=wdn[q, hf * 2 + g], sbt=Wd, writes=[Wd])
                            for jj in range(11):
                                ja = g * 11 + jj
                                for i in range(4):
                                    C.op("pe", "matmul", reads=[Wd, A], writes=[PS[4 + i]], out=PS[4 + i][:, :nbi],
                                         lhsT=Wd[:, jj, i * 128:(i + 1) * 128], rhs=A[:, ja, :nbi],
                                         start=(ja == 0), stop=(ja == 21))
                        for i in range(4):
                            c = q * 4 + i
                            C.op("dve", "tensor_tensor", reads=[XB, PS[4 + i]], writes=[XB], out=XB[:, c, 1:1 + nbi],
                                 in0=XB[:, c, 1:1 + nbi], in1=PS[4 + i][:, :nbi], op=ALU.add)
                for tt in range(0, nbi, 128):
                    tn = min(128, nbi - tt)
                    pt = ptok[(tt // 128) % 2]
                    C.dma("sp", out=pt[:tn, :], in_=p_l[t0 + tt:t0 + tt + tn, :], sbt=pt, writes=[pt])
                    pst = PS[(tt // 128) % 2]
                    for e in range(2):
                        C.op("pe", "transpose", reads=[pt, self.ident], writes=[pst], out=pst[:, e * 128:e * 128 + tn],
                             in_=pt[:tn, e * 128:(e + 1) * 128], identity=self.ident[:tn, :tn])
                    C.op("act", "activation", reads=[pst], writes=[PT], out=PT[:, :, tt:tt + tn],
                         in_=pst[:, 0:256].rearrange("p (e t) -> p e t", e=2)[:, :, :tn], func=AF.Copy)
                self.rms_to_h(XB, H, nb, "norm_ple", l, sqb, rstd, PS[4])
                for og in range(8):
                    Wg = WPG[ipg % 2]
                    ipg += 1
                    C.dma("sp", out=Wg[:], in_=wpg[og, 0], sbt=Wg, writes=[Wg])
                    for oo in range(2):
                        oc = og * 2 + oo
                        pgate, pproj = PS[oc % 2], PS[2 + oc % 2]
                        for k in range(KC):
                            C.op("pe", "matmul", reads=[Wg, H], writes=[pgate], out=pgate[:, :nbi], lhsT=Wg[:, k, oo * 128:(oo + 1) * 128],
                                 rhs=H[:, k, 1:1 + nbi], start=(k == 0), stop=(k == KC - 1))
                        for e in range(2):
                            C.op("pe", "matmul", reads=[WPP, PT], writes=[pproj], out=pproj[:, :nbi], lhsT=WPP[:, e, oc * 128:(oc + 1) * 128],
                                 rhs=PT[:, e, :nbi], start=(e == 0), stop=(e == 1))
                        sg, g0 = sgb[oc % 2], g0b[oc % 2]
                        C.op("act", "activation", reads=[pgate], writes=[sg], out=sg[:, :nbi], in_=pgate[:, :nbi], func=AF.Sigmoid)
                        C.op("dve", "tensor_tensor", reads=[sg, pproj], writes=[g0], out=g0[:, :nbi], in0=sg[:, :nbi], in1=pproj[:, :nbi],
                             op=ALU.mult)
                        C.op("pool", "tensor_tensor", reads=[XB, g0], writes=[XB], out=XB[:, oc, 1:1 + nbi], in0=XB[:, oc, 1:1 + nbi],
                             in1=g0[:, :nbi], op=ALU.add)
                dst = xdst.rearrange("(c p) t -> p c t", p=128)[:, :, t0:t0 + nbi]
                C.dma("pool", out=dst, in_=XB[:, :, 1:1 + nbi], sbt=XB, reads=[XB])
            C.barrier()

    def epilogue(self, xT, y_out):
        C, T = self.C, self.T
        PS = self.PS
        with contextlib.ExitStack() as st:
            self.ensure_eps(st)
            XB = [C.sb(st, "XBe", [128, KC, 512], F32, dma=True) for _ in range(2)]
            Y = C.sb(st, "Ye", [128, KC, 512], F32)
            sqb = [C.sb(st, "sq", [128, 512], F32) for _ in range(4)]
            rstd = C.sb(st, "rstd", [128, 512], F32)
            otok = [C.sb(st, "otok", [128, D], F32, dma=True) for _ in range(2)]
            ib = 0
            io = 0
            for s in range(self.NS):
                for g0 in range(0, T, 512):
                    gn = min(512, T - g0)
                    X = XB[ib % 2]
                    ib += 1
                    self.load_xblock(X, xT[s], g0, gn, 0)
                    self.rms_to_h(X, None, gn, "final_norm", 0, sqb, rstd, PS[4], out_f32=Y)
                    for tt in range(0, gn, 128):
                        ot = otok[io % 2]
                        io += 1
                        for c4 in range(0, KC, 4):
                            pst = PS[(c4 // 4) % 4]
                            for c in range(c4, c4 + 4):
                                C.op("pe", "transpose", reads=[Y, self.ident], writes=[pst], out=pst[:, (c - c4) * 128:(c - c4 + 1) * 128],
                                     in_=Y[:, c, tt:tt + 128], identity=self.ident[:])
                            if (c4 // 4) % 2 == 0:
                                C.op("dve", "tensor_copy", reads=[pst], writes=[ot], out=ot[:, c4 * 128:(c4 + 4) * 128], in_=pst[:, 0:512])
                            else:
                                C.op("act", "activation", reads=[pst], writes=[ot], out=ot[:, c4 * 128:(c4 + 4) * 128], in_=pst[:, 0:512],
                                     func=AF.Copy)
                        C.dma("pool", out=y_out[s, g0 + tt:g0 + tt + 128, :], in_=ot[:], sbt=ot, reads=[ot])
            C.barrier()


def pack_small(P, inputs):
    sp = np.zeros((128, P.small_n), np.float32)

    def put(name, arr2d):
        o = P.small_off[name]
        sp[:, o:o + arr2d.shape[0]] = arr2d.T

    def chunked(a, nch):
        return np.ascontiguousarray(a).reshape(-1, 128)

    put("norm_mix", chunked(inputs["norm_mix"], KC))
    put("norm_ffn", chunked(inputs["norm_ffn"], KC))
    put("norm_ple", chunked(inputs["norm_ple"], KC))
    put("final_norm", chunked(inputs["final_norm"], KC))
    put("conv_w", chunked(inputs["ffn_conv_w"], NFC))
    put("conv_b", chunked(inputs["ffn_conv_b"], NFC))
    if "s5_d" in inputs:
        put("s5_d", chunked(inputs["s5_d"], KC))
    if "gdn_conv_w" in inputs:
        put("gdn_cw", chunked(inputs["gdn_conv_w"], 64))
    return sp


def na_tables(rpb):
    NEG = -1.0e4
    par = np.arange(2)[:, None, None, None]
    j = np.arange(64)[None, :, None, None]
    m = np.arange(16)[None, None, :, None]
    c = np.arange(64)[None, None, None, :]
    dr = 7 - m + par
    dc = j - c
    g = rpb[:, np.clip(dr + 7, 0, 14), np.clip(dc + 15, 0, 30)]
    g = np.ascontiguousarray(np.broadcast_to(g, (rpb.shape[0], 2, 64, 16, 64))).reshape(rpb.shape[0], 128, 16, 64)
    cs = np.clip(c - 8, 0, 48)
    colv = (j >= cs) & (j < cs + 16)
    v_full = colv & (dr >= -7) & (dr <= 7)
    v_int = colv & (dr >= -4) & (dr <= 3)
    mask = np.stack([np.where(v_int, 0.0, NEG), np.where(v_full, 0.0, NEG)], 0)
    mask = np.ascontiguousarray(np.broadcast_to(mask, (2, 2, 64, 16, 64))).reshape(2, 128, 16, 64).transpose(1, 0, 2, 3)
    return g.astype(np.float32), np.ascontiguousarray(mask).astype(np.float32)


def s5_tables(inputs):
    lam = np.zeros((128, 2, 3, 64), np.float32)
    Bp = np.zeros((2, 16, 128, 2, 4, 128), np.float32)
    Cp = np.zeros((2, 16, 128, 2, 4, 128), np.float32)
    for d in range(2):
        lam[:, d, 0, :] = inputs["s5_a_re"][0, d].reshape(64, 128).T
        lam[:, d, 1, :] = inputs["s5_a_im"][0, d].reshape(64, 128).T
        lam[:, d, 2, :] = np.repeat(inputs["s5_log_dt"][0, d], 64).reshape(64, 128).T
        for ri, (bn, cn) in enumerate((("s5_b_re", "s5_c_re"), ("s5_b_im", "s5_c_im"))):
            b = inputs[bn][0, d]
            c = inputs[cn][0, d]
            for fc in range(16):
                for gi in range(8):
                    g = 8 * fc + gi
                    q4, half = gi // 2, gi % 2
                    Bp[d, fc, gi * 16:(gi + 1) * 16, ri, q4, half * 64:(half + 1) * 64] = b[g].T
                    Cp[d, fc, half * 64:(half + 1) * 64, ri, q4, gi * 16:(gi + 1) * 16] = c[g].T
    return lam, Bp, Cp


def gdn_consts():
    NEG = -1.0e5
    s = np.arange(64)[:, None]
    i = np.arange(64)[None, :]
    cs = np.zeros((64, 8, 64), np.float32)
    cs[:, 0] = s <= i
    cs[:, 1] = s >= i
    cs[:, 2] = s > i
    cs[:, 3] = s < i
    cs[:, 4] = np.where(i >= s, 0.0, NEG)
    cs[:, 5] = np.where(i <= s, 0.0, NEG)
    cs[:, 6] = i > s
    cs[:, 7] = i < s
    return cs


def run_prog(P, nc, inputs, slot_x, slot_p, n_cores):
    sp = pack_small(P, inputs)
    ident = np.eye(128, dtype=np.float32)
    in_maps = []
    for c in range(n_cores):
        m = {"x_in": slot_x[c], "p_in": slot_p[c], "smallp": sp, "ident": ident}
        if "gdn_consts" in P.din:
            m["gdn_consts"] = gdn_consts()
        if "s5_lam" in P.din:
            m["s5_lam"], m["s5_B"], m["s5_C"] = s5_tables(inputs)
            m["iota128"] = np.ascontiguousarray(np.broadcast_to(np.arange(1, 129, dtype=np.float32), (128, 128)))
        if "na_g" in P.din:
            m["na_g"], m["na_mask"] = na_tables(np.asarray(inputs["na_rpb"][0]))
        for k in P.din:
            if k not in m:
                if "." in k:
                    nm, l = k.split(".")
                    m[k] = np.ascontiguousarray(inputs[nm][int(l)])
                else:
                    m[k] = np.ascontiguousarray(inputs[k])
        in_maps.append(m)
    res = run_bass_kernel_spmd(nc, in_maps, core_ids=list(range(n_cores)))
    return [r["y_out"] for r in res.results]


def kernel(**inputs):
    T, NS = 4096, 2
    P = Prog(T, NS, list(range(DEPTH)))
    nc = P.build()
    xp, xs = inputs["x_prompt"], inputs["x_sample"]
    pp, psm = inputs["p_prompt"], inputs["p_sample"]
    slot_x, slot_p = [], []
    for c in range(8):
        j = c if c < 2 else 0
        slot_x.append(np.stack([xs[c], xp[j]], 0))
        slot_p.append(np.stack([psm[:, c], pp[:, j]], 1))
    outs = run_prog(P, nc, inputs, slot_x, slot_p, 8)
    y_sample = np.stack([outs[c][0] for c in range(8)], 0)
    y_prompt = np.stack([outs[0][1], outs[1][1]], 0)
    return (y_prompt, y_sample)
```

```python
import contextlib
import os
import numpy as np
import ml_dtypes
import concourse.bass as bass
import concourse.mybir as mybir
from concourse.bass_utils import run_bass_kernel_spmd

F32 = mybir.dt.float32
BF16 = mybir.dt.bfloat16
AF = mybir.ActivationFunctionType
ALU = mybir.AluOpType
AX = mybir.AxisListType

D = 2048
KC = 16
DFF = 5632
NFC = 44
PLE = 256
DEPTH = 4
EPS = 1e-6
SAME_ENGINE_SYNC = True
GDN_DEFER = True
GDN_ACTEV = False
GDN_ZIP = True
GDN_NEUMANN_BF16 = False


class DSem:
    def __init__(self, sem):
        self.sem = sem
        self.total = 0


class Buf:
    def __init__(self, name):
        self.name = name
        self.w = None
        self.r = {}
        self.dsem = None


class TT:
    def __init__(self, t, b):
        self.t = t
        self.b = b

    def __getitem__(self, idx):
        return self.t[idx]


class Ctx:
    ENGS = ["pe", "dve", "act", "pool", "sp"]
    CENGS = ["pe", "dve", "act", "pool"]

    def __init__(self, nc, stack, n_dsem=90):
        self.nc = nc
        self.ops = {e: [] for e in self.ENGS}
        self.csem = {e: stack.enter_context(nc.semaphore("c_" + e)) for e in self.CENGS}
        self.cnt = {e: 0 for e in self.CENGS}
        self.seen = {e: {} for e in self.ENGS}
        self.dsems = [DSem(stack.enter_context(nc.semaphore("d%d" % i))) for i in range(n_dsem)]
        self.free_ds = list(self.dsems)
        self.nuid = 0

    def sb(self, stack, name, shape, dtype, dma=False):
        self.nuid += 1
        t = stack.enter_context(self.nc.sbuf_tensor("%s_%d" % (name, self.nuid), list(shape), dtype))
        b = Buf(name)
        if dma:
            b.dsem = self.free_ds.pop()
            stack.callback(self.free_ds.append, b.dsem)
        return TT(t, b)

    def ps(self, stack, name, shape, dtype):
        self.nuid += 1
        t = stack.enter_context(self.nc.psum_tensor("%s_%d" % (name, self.nuid), list(shape), dtype))
        return TT(t, Buf(name))

    def _tok_key(self, tok):
        return (tok[0], tok[1] if tok[0] == "c" else id(tok[1]))

    def _collect(self, reads, writes):
        toks = []
        for b in reads:
            if b.w is not None:
                toks.append(b.w)
        for b in writes:
            if b.w is not None:
                toks.append(b.w)
            toks.extend(b.r.values())
        return toks

    def _waits(self, eng, toks):
        res = {}
        for tok in toks:
            if tok[0] == "c":
                e2, v = tok[1], tok[2]
                if e2 == eng and (eng == "pe" or not SAME_ENGINE_SYNC):
                    continue
                sh = self.csem[e2]
            else:
                v = tok[1].total
                sh = tok[1].sem
            key = self._tok_key(tok)
            if self.seen[eng].get(key, 0) >= v:
                continue
            if key in res and res[key][1] >= v:
                continue
            res[key] = (sh, v)
        for key, (sh, v) in res.items():
            self.seen[eng][key] = v
        return list(res.values())

    def _commit(self, tok, reads, writes):
        key = self._tok_key(tok)
        for b in reads:
            b.r[key] = tok
        for b in writes:
            b.w = tok
            b.r = {}

    def op(self, eng, name, reads=(), writes=(), **kw):
        reads = [x.b if isinstance(x, TT) else x for x in reads]
        writes = [x.b if isinstance(x, TT) else x for x in writes]
        waits = self._waits(eng, self._collect(reads, writes))
        self.cnt[eng] += 1
        sem = self.csem[eng]

        def run(e, name=name, kw=kw, waits=waits, sem=sem):
            for sh, v in waits:
                e.wait_ge(sh, v)
            getattr(e, name)(**kw).then_inc(sem, 1)

        self.ops[eng].append(run)
        self._commit(("c", eng, self.cnt[eng]), reads, writes)

    def dma(self, q, out, in_, sbt, reads=(), writes=(), **kw):
        reads = [x.b if isinstance(x, TT) else x for x in reads]
        writes = [x.b if isinstance(x, TT) else x for x in writes]
        waits = self._waits(q, self._collect(reads, writes))
        ds = sbt.b.dsem if isinstance(sbt, TT) else sbt
        ds.total += 16

        def run(e, out=out, in_=in_, kw=kw, waits=waits, ds=ds):
            for sh, v in waits:
                e.wait_ge(sh, v)
            e.dma_start(out=out, in_=in_, **kw).then_inc(ds.sem, 16)

        self.ops[q].append(run)
        self._commit(("d", ds), reads, writes)

    def barrier(self):
        for e in self.ENGS:
            waits = []
            for e2 in self.CENGS:
                if e2 == e:
                    continue
                v = self.cnt[e2]
                key = ("c", e2)
                if v > self.seen[e].get(key, 0):
                    self.seen[e][key] = v
                    waits.append((self.csem[e2], v))
            for ds in self.dsems:
                key = ("d", id(ds))
                if ds.total > self.seen[e].get(key, 0):
                    self.seen[e][key] = ds.total
                    waits.append((ds.sem, ds.total))
            if waits:
                def run(eh, waits=waits):
                    for sh, v in waits:
                        eh.wait_ge(sh, v)
                self.ops[e].append(run)

    def emit(self):
        nc = self.nc
        with nc.Block() as block:
            @block.tensor
            def _(e):
                for f in self.ops["pe"]:
                    f(e)

            @block.vector
            def _(e):
                for f in self.ops["dve"]:
                    f(e)

            @block.scalar
            def _(e):
                for f in self.ops["act"]:
                    f(e)

            @block.gpsimd
            def _(e):
                for f in self.ops["pool"]:
                    f(e)

            @block.sync
            def _(e):
                for f in self.ops["sp"]:
                    f(e)


def blocks_of(T, nbi_max=456):
    nblk = -(-T // nbi_max)
    base = -(-T // nblk)
    out = []
    t = 0
    while t < T:
        n = min(base, T - t)
        out.append((t, n))
        t += n
    return out


class Prog:
    def __init__(self, T, nslot, layers, mixers=True, do_ffn=True):
        self.do_ffn = do_ffn
        self.T = T
        self.NS = nslot
        self.layers = layers
        self.mixers = mixers
        self.nc = bass.Bass("TRN2", target_bir_lowering=False)
        self.din = {}
        self.small_off = {}
        self.small_n = 0

    def inp(self, name, shape, dtype=F32):
        ap = self.nc.dram_tensor(name, list(shape), dtype, kind="ExternalInput").ap()
        self.din[name] = ap
        return ap

    def scratch(self, name, shape, dtype):
        return self.nc.dram_tensor(name, list(shape), dtype, kind="Internal").ap()

    def cast_weight(self, C, name, src2d, K, N, ct, kg=None):
        kcs = K // 128
        kg = kg or kcs
        nt, ng = N // ct, kcs // kg
        ws = self.scratch(name, [nt, ng, 128, kg, ct], BF16)
        for i in range(nt):
            for g in range(ng):
                src = src2d[g * kg * 128:(g + 1) * kg * 128, i * ct:(i + 1) * ct].rearrange("(k p) c -> p k c", p=128)
                C.dma("pool", out=ws[i, g], in_=src, sbt=self.cast_ds)
        return ws

    def build(self):
        nc, T, NS = self.nc, self.T, self.NS
        x_in = self.inp("x_in", [NS, T, D])
        p_in = self.inp("p_in", [DEPTH, NS, T, PLE])
        y_out = nc.dram_tensor("y_out", [NS, T, D], F32, kind="ExternalOutput").ap()
        w_gu = {l: self.inp("ffn_w_gu.%d" % l, [D, 2 * DFF]) for l in self.layers}
        w_dn = {l: self.inp("ffn_w_down.%d" % l, [DFF, D]) for l in self.layers}
        w_pg = {l: self.inp("ple_w_gate.%d" % l, [D, D]) for l in self.layers}
        w_pp = {l: self.inp("ple_w_proj.%d" % l, [PLE, D]) for l in self.layers}
        self.declare_mixer_inputs()
        self.small_layout()
        smallp = self.inp("smallp", [128, self.small_n])
        ident_in = self.inp("ident", [128, 128])
        self.xT = [[self.scratch("xT_%d_%d" % (a, s), [D, T], F32) for s in range(NS)] for a in range(2)]

        with contextlib.ExitStack() as gstack:
            C = Ctx(nc, gstack)
            self.C = C
            self.cast_ds = C.free_ds.pop()
            self.PS = [C.ps(gstack, "ps%d" % i, [128, 512], F32) for i in range(8)]
            self.small = C.sb(gstack, "small", [128, self.small_n], F32, dma=True)
            C.dma("sp", out=self.small[:], in_=smallp[:, :], sbt=self.small, writes=[self.small])
            self.ident = C.sb(gstack, "ident", [128, 128], F32, dma=True)
            C.dma("sp", out=self.ident[:], in_=ident_in[:, :], sbt=self.ident, writes=[self.ident])
            self.ones = C.sb(gstack, "ones", [128, 128], F32)
            C.op("dve", "memset", writes=[self.ones], ap=self.ones[:], constant=1.0)
            self.onesb = C.sb(gstack, "onesb", [128, 128], BF16)
            C.op("dve", "memset", writes=[self.onesb], ap=self.onesb[:], constant=1.0)
            self.identb = C.sb(gstack, "identb", [128, 128], BF16)
            C.op("dve", "tensor_copy", reads=[self.ident], writes=[self.identb], out=self.identb[:], in_=self.ident[:])

            self.ws = {}
            for l in self.layers:
                if self.mixers:
                    self.cast_mixer_weights(l)
                if self.do_ffn:
                    self.ws[("gu", l)] = self.cast_weight(C, "ws_gu%d" % l, w_gu[l], D, 2 * DFF, 256)
                    self.ws[("dn", l)] = self.cast_weight(C, "ws_dn%d" % l, w_dn[l], DFF, D, 512, kg=11)
                    self.ws[("pg", l)] = self.cast_weight(C, "ws_pg%d" % l, w_pg[l], D, D, 256)
                    self.ws[("pp", l)] = self.cast_weight(C, "ws_pp%d" % l, w_pp[l], PLE, D, 2048)
            C.barrier()

            self.prologue(x_in)
            cur = 0
            for l in self.layers:
                if self.mixers:
                    for s in range(NS):
                        [self.mix_na, self.mix_sg, self.mix_gdn, self.mix_s5][l % 4](l, s, self.xT[cur][s], self.xT[1 - cur][s])
                    cur = 1 - cur
                if self.do_ffn:
                    for s in range(NS):
                        self.ffn_ple(l, s, self.xT[cur][s], self.xT[1 - cur][s], p_in[l, s])
                    cur = 1 - cur
            self.epilogue(self.xT[cur], y_out)
            C.barrier()
            C.emit()
        return nc

    def kinds(self):
        return sorted(set(l % 4 for l in self.layers)) if self.mixers else []

    def declare_mixer_inputs(self):
        ks = self.kinds()
        if 0 in ks:
            self.na_w_qkv = self.inp("na_w_qkv", [1, D, 3 * D])
            self.na_w_o = self.inp("na_w_o", [1, D, D])
            self.na_g = self.inp("na_g", [16, 128, 16, 64])
            self.na_mask = self.inp("na_mask", [128, 2, 16, 64])
            self.na_qk = self.scratch("na_qk", [2 * D, self.T], BF16)
            self.na_v = self.scratch("na_v", [self.T, D], BF16)
            self.na_o = self.scratch("na_o", [D, self.T], BF16)
        if 2 in ks:
            T = self.T
            self.gdn_w_in = self.inp("gdn_w_in", [1, D, 12416])
            self.gdn_w_o = self.inp("gdn_w_o", [1, 4096, D])
            self.gdn_a_log = self.inp("gdn_a_log", [1, 2, 32])
            self.gdn_dt_bias = self.inp("gdn_dt_bias", [1, 2, 32])
            self.gdn_out_norm = self.inp("gdn_out_norm", [1, 128])
            self.gdn_consts = self.inp("gdn_consts", [64, 8, 64])
            self.gdn_qk = self.scratch("gdn_qk", [4096, T], BF16)
            self.gdn_kv = self.scratch("gdn_kv", [T, 6144], BF16)
            self.gdn_z = self.scratch("gdn_z", [T, 4096], F32)
            self.gdn_gb = self.scratch("gdn_gb", [T, 128], F32)
            self.gdn_of = self.scratch("gdn_of", [T, 4096], F32)
            self.gdn_o = self.scratch("gdn_o", [4096, T], BF16)
        if 3 in ks:
            self.s5_w_glu = self.inp("s5_w_glu", [1, D, 2 * D])
            self.s5_lam = self.inp("s5_lam", [128, 2, 3, 64])
            self.s5_B = self.inp("s5_B", [2, 16, 128, 2, 4, 128])
            self.s5_C = self.inp("s5_C", [2, 16, 128, 2, 4, 128])
            self.iota_in = self.inp("iota128", [128, 128])
            self.s5_h = self.scratch("s5_h", [D, self.T], F32)
            self.s5_y = self.scratch("s5_y", [D, self.T], BF16)
        if 1 in ks:
            self.sg_w_in = self.inp("sg_w_in", [1, D, 2 * D])
            self.sg_norm = self.inp("sg_norm", [1, D])
            self.sg_w_s = self.inp("sg_w_s", [1, 16, 128, 128])
            self.sg_b_s = self.inp("sg_b_s", [1, 16, 128])
            self.sg_w_o = self.inp("sg_w_o", [1, D, D])

    def cast_mixer_weights(self, l):
        C = self.C
        k = l % 4
        if k == 0:
            self.ws["na_qk"] = self.cast_weight(C, "ws_naqk", self.na_w_qkv[0][:, 0:2 * D], D, 2 * D, 256)
            self.ws["na_v"] = self.cast_weight(C, "ws_nav", self.na_w_qkv[0][:, 2 * D:3 * D], D, D, 512)
            self.ws["na_o"] = self.cast_weight(C, "ws_nao", self.na_w_o[0], D, D, 256)
        if k == 2:
            self.ws["gdn_qkv"] = self.cast_weight(C, "ws_gqkv", self.gdn_w_in[0][:, 0:8192], D, 8192, 256)
            self.ws["gdn_z"] = self.cast_weight(C, "ws_gz", self.gdn_w_in[0][:, 8192:12288], D, 4096, 512)
            self.ws["gdn_ab"] = self.cast_weight(C, "ws_gab", self.gdn_w_in[0][:, 12288:12416], D, 128, 128)
            self.ws["gdn_o"] = self.cast_weight(C, "ws_go", self.gdn_w_o[0], 4096, D, 256)
        if k == 3:
            self.ws["s5_glu"] = self.cast_weight(C, "ws_s5glu", self.s5_w_glu[0], D, 2 * D, 256)
        if k == 1:
            self.ws["sg_u"] = self.cast_weight(C, "ws_sgu", self.sg_w_in[0][:, 0:D], D, D, 256)
            self.ws["sg_v"] = self.cast_weight(C, "ws_sgv", self.sg_w_in[0][:, D:2 * D], D, D, 512)
            self.ws["sg_o"] = self.cast_weight(C, "ws_sgo", self.sg_w_o[0], D, D, 256)

    def gelu_tanh(self, dst, dst_ap, src, src_ap, ta, tb, n):
        C = self.C
        C.op("act", "activation", reads=[src], writes=[ta], out=ta[:, :n], in_=src_ap, func=AF.Square)
        C.op("dve", "tensor_scalar", reads=[ta], writes=[ta], out=ta[:, :n], in0=ta[:, :n], scalar1=0.044715, scalar2=1.0,
             op0=ALU.mult, op1=ALU.add)
        C.op("dve", "tensor_tensor", reads=[ta, src], writes=[ta], out=ta[:, :n], in0=ta[:, :n], in1=src_ap, op=ALU.mult)
        C.op("act", "activation", reads=[ta], writes=[tb], out=tb[:, :n], in_=ta[:, :n], func=AF.Sigmoid, scale=1.5957691216057308)
        C.op("dve", "tensor_tensor", reads=[tb, src], writes=[dst], out=dst_ap, in0=tb[:, :n], in1=src_ap, op=ALU.mult)

    def bcast_row(self, out, tmp, name, src_row_ap, n):
        C = self.C
        row = C.sb(tmp, name + "_row", [1, n], F32, dma=True)
        C.dma("sp", out=row[:], in_=src_row_ap, sbt=row, writes=[row])
        for q in range(0, n, 512):
            qn = min(512, n - q)
            ps = self.PS[(q // 512) % 4]
            C.op("pe", "matmul", reads=[row, self.ones], writes=[ps], out=ps[:, :qn], lhsT=self.ones[0:1, 0:128], rhs=row[0:1, q:q + qn],
                 start=True, stop=True)
            C.op("dve", "tensor_copy", reads=[ps], writes=[out], out=out[:, q:q + qn], in_=ps[:, :qn])
        return out

    def mix_sg(self, l, s, xsrc, xdst):
        C, T, PS = self.C, self.T, self.PS
        BT = 256
        wu, wv, wo = self.ws["sg_u"], self.ws["sg_v"], self.ws["sg_o"]
        with contextlib.ExitStack() as st:
            WST = C.sb(st, "WST", [128, 16, 128], BF16)
            BHI = C.sb(st, "BHI", [1, D], BF16)
            BLO = C.sb(st, "BLO", [1, D], BF16)
            SGN = C.sb(st, "SGN", [128, D], F32)
            with contextlib.ExitStack() as tmp:
                self.bcast_row(SGN, tmp, "SGN", self.sg_norm[0:1, :], D)
                wsl = C.sb(tmp, "wsl", [128, 16, 128], F32, dma=True)
                C.dma("sp", out=wsl[:], in_=self.sg_w_s[0].rearrange("g t s -> t g s"), sbt=wsl, writes=[wsl])
                for g4 in range(4):
                    ps = PS[g4]
                    for gi in range(4):
                        g = g4 * 4 + gi
                        C.op("pe", "transpose", reads=[wsl, self.ident], writes=[ps], out=ps[:, gi * 128:(gi + 1) * 128], in_=wsl[:, g, :],
                             identity=self.ident[:])
                    C.op("dve", "tensor_copy", reads=[ps], writes=[WST], out=WST[:, g4 * 4:(g4 + 1) * 4, :],
                         in_=ps[:, 0:512].rearrange("p (g t) -> p g t", g=4))
                brow = C.sb(tmp, "brow", [1, D], F32, dma=True)
                C.dma("sp", out=brow[:], in_=self.sg_b_s[0:1].rearrange("o g t -> o (g t)"), sbt=brow, writes=[brow])
                bt = C.sb(tmp, "btmp", [1, D], F32)
                C.op("dve", "tensor_copy", reads=[brow], writes=[BHI], out=BHI[:], in_=brow[:])
                C.op("dve", "tensor_tensor", reads=[brow, BHI], writes=[bt], out=bt[:], in0=brow[:], in1=BHI[:], op=ALU.subtract)
                C.op("dve", "tensor_copy", reads=[bt], writes=[BLO], out=BLO[:], in_=bt[:])
                C.barrier()
            self.ensure_eps(st)
            XB = C.sb(st, "XB", [128, KC, BT], F32, dma=True)
            H = C.sb(st, "H", [128, KC, BT], BF16)
            UT = C.sb(st, "UT", [128, KC, BT], F32)
            GT = C.sb(st, "GT", [128, KC, BT], BF16)
            VT = [C.sb(st, "VT", [128, D], F32) for _ in range(BT // 128)]
            VN = [C.sb(st, "VN", [128, D], BF16) for _ in range(BT // 128)]
            WU = [C.sb(st, "WU", [128, KC, 256], BF16, dma=True) for _ in range(2)]
            WV = [C.sb(st, "WV", [128, KC, 512], BF16, dma=True) for _ in range(2)]
            WO = [C.sb(st, "WO", [128, KC, 256], BF16, dma=True) for _ in range(2)]
            ta = [C.sb(st, "ta", [128, 512], F32) for _ in range(2)]
            tb = [C.sb(st, "tb", [128, 512], F32) for _ in range(2)]
            sqb = [C.sb(st, "sq", [128, 512], F32) for _ in range(4)]
            rstd = C.sb(st, "rstd", [128, 512], F32)
            ssv = C.sb(st, "ssv", [128, 2], F32)
            iu = iv = io = ig = 0
            for t0 in range(0, T, BT):
                gn = min(BT, T - t0)
                ntt = gn // 128
                self.load_xblock(XB, xsrc, t0, gn, 0)
                self.rms_to_h(XB, H, gn, "norm_mix", l, sqb, rstd, PS[4])
                for jt in range(8):
                    W = WU[iu % 2]
                    iu += 1
                    C.dma("sp", out=W[:], in_=wu[jt, 0], sbt=W, writes=[W])
                    for jj in range(2):
                        j = jt * 2 + jj
                        ps = PS[j % 2]
                        for k in range(KC):
                            C.op("pe", "matmul", reads=[W, H], writes=[ps], out=ps[:, :gn], lhsT=W[:, k, jj * 128:(jj + 1) * 128],
                                 rhs=H[:, k, :gn], start=(k == 0), stop=(k == KC - 1))
                        self.gelu_tanh(UT, UT[:, j, :gn], ps, ps[:, :gn], ta[ig % 2], tb[ig % 2], gn)
                        ig += 1
                for cg in range(4):
                    W = WV[iv % 2]
                    iv += 1
                    C.dma("sp", out=W[:], in_=wv[cg, 0], sbt=W, writes=[W])
                    for tt in range(ntt):
                        ps = PS[2 + (cg * ntt + tt) % 2]
                        for k in range(KC):
                            C.op("pe", "matmul", reads=[W, H], writes=[ps], out=ps[:, :512], lhsT=H[:, k, tt * 128:(tt + 1) * 128],
                                 rhs=W[:, k, :], start=(k == 0), stop=(k == KC - 1))
                        self.gelu_tanh(VT[tt], VT[tt][:, cg * 512:(cg + 1) * 512], ps, ps[:, :512], ta[ig % 2], tb[ig % 2], 512)
                        ig += 1
                for tt in range(ntt):
                    C.op("act", "activation", reads=[VT[tt]], writes=[VN[tt], ssv], out=VN[tt][:], in_=VT[tt][:], func=AF.Square,
                         accum_out=ssv[:, tt:tt + 1])
                    C.op("act", "activation", reads=[ssv], writes=[ssv], out=ssv[:, tt:tt + 1], in_=ssv[:, tt:tt + 1], func=AF.Sqrt,
                         scale=1.0 / D, bias=self.epsb[:, 0:1])
                    C.op("dve", "reciprocal", reads=[ssv], writes=[ssv], out=ssv[:, tt:tt + 1], in_=ssv[:, tt:tt + 1])
                    C.op("dve", "scalar_tensor_tensor", reads=[VT[tt], ssv, SGN], writes=[VN[tt]], out=VN[tt][:], in0=VT[tt][:],
                         scalar=ssv[:, tt:tt + 1], in1=SGN[:], op0=ALU.mult, op1=ALU.mult)
                for tt in range(ntt):
                    for g4 in range(4):
                        ps = PS[4 + (tt * 4 + g4) % 4]
                        for gi in range(4):
                            g = g4 * 4 + gi
                            o = ps[:, gi * 128:(gi + 1) * 128]
                            C.op("pe", "matmul", reads=[VN[tt], WST], writes=[ps], out=o, lhsT=VN[tt][:, g * 128:(g + 1) * 128],
                                 rhs=WST[:, g, :], start=True, stop=False)
                            C.op("pe", "matmul", reads=[BHI, self.onesb], writes=[ps], out=o, lhsT=self.onesb[0:1, 0:128],
                                 rhs=BHI[0:1, g * 128:(g + 1) * 128], start=False, stop=False)
                            C.op("pe", "matmul", reads=[BLO, self.onesb], writes=[ps], out=o, lhsT=self.onesb[0:1, 0:128],
                                 rhs=BLO[0:1, g * 128:(g + 1) * 128], start=False, stop=True)
                        C.op("dve", "tensor_tensor", reads=[UT, ps], writes=[GT], out=GT[:, g4 * 4:(g4 + 1) * 4, tt * 128:(tt + 1) * 128],
                             in0=UT[:, g4 * 4:(g4 + 1) * 4, tt * 128:(tt + 1) * 128],
                             in1=ps[:, 0:512].rearrange("p (g t) -> p g t", g=4), op=ALU.mult)
                for ot in range(8):
                    W = WO[io % 2]
                    io += 1
                    C.dma("sp", out=W[:], in_=wo[ot, 0], sbt=W, writes=[W])
                    for oo in range(2):
                        oc = ot * 2 + oo
                        ps = PS[oc % 2]
                        for k in range(KC):
                            C.op("pe", "matmul", reads=[W, GT], writes=[ps], out=ps[:, :gn], lhsT=W[:, k, oo * 128:(oo + 1) * 128],
                                 rhs=GT[:, k, :gn], start=(k == 0), stop=(k == KC - 1))
                        C.op("dve", "tensor_tensor", reads=[XB, ps], writes=[XB], out=XB[:, oc, :gn], in0=XB[:, oc, :gn], in1=ps[:, :gn],
                             op=ALU.add)
                dst = xdst.rearrange("(c p) t -> p c t", p=128)[:, :, t0:t0 + gn]
                C.dma("pool", out=dst, in_=XB[:, :, :gn], sbt=XB, reads=[XB])
            C.barrier()

    def out_proj(self, xsrc, xdst, act_dram, kcs, wo, BT=512):
        C, T, PS = self.C, self.T, self.PS
        with contextlib.ExitStack() as st:
            XB = [C.sb(st, "XBo", [128, KC, BT], F32, dma=True) for _ in range(2)]
            AB = [C.sb(st, "ABo", [128, kcs, BT], BF16, dma=True) for _ in range(2)]
            WO = [C.sb(st, "WOo", [128, kcs, 256], BF16, dma=True) for _ in range(2)]
            io = ib = 0
            for t0 in range(0, T, BT):
                gn = min(BT, T - t0)
                X, Ab = XB[ib % 2], AB[ib % 2]
                ib += 1
                self.load_xblock(X, xsrc, t0, gn, 0)
                C.dma("sp", out=Ab[:, :, :gn], in_=act_dram.rearrange("(c p) t -> p c t", p=128)[:, :, t0:t0 + gn], sbt=Ab, writes=[Ab])
                for ot in range(8):
                    W = WO[io % 2]
                    io += 1
                    C.dma("sp", out=W[:], in_=wo[ot, 0], sbt=W, writes=[W])
                    for oo in range(2):
                        oc = ot * 2 + oo
                        ps = PS[oc % 4]
                        for k in range(kcs):
                            C.op("pe", "matmul", reads=[W, Ab], writes=[ps], out=ps[:, :gn], lhsT=W[:, k, oo * 128:(oo + 1) * 128],
                                 rhs=Ab[:, k, :gn], start=(k == 0), stop=(k == kcs - 1))
                        C.op("dve", "tensor_tensor", reads=[X, ps], writes=[X], out=X[:, oc, :gn], in0=X[:, oc, :gn], in1=ps[:, :gn],
                             op=ALU.add)
                dst = xdst.rearrange("(c p) t -> p c t", p=128)[:, :, t0:t0 + gn]
                C.dma("pool", out=dst, in_=X[:, :, :gn], sbt=X, reads=[X])
            C.barrier()

    def mix_na(self, l, s, xsrc, xdst):
        C, T, PS = self.C, self.T, self.PS
        BT = 512
        R = T // 64
        wqk, wv, wo = self.ws["na_qk"], self.ws["na_v"], self.ws["na_o"]
        with contextlib.ExitStack() as st:
            self.ensure_eps(st)
            XB = C.sb(st, "XB", [128, KC, BT], F32, dma=True)
            H = C.sb(st, "H", [128, KC, BT], BF16)
            QK = C.sb(st, "QK", [128, 32, BT], BF16, dma=True)
            VS = [C.sb(st, "VS", [128, D], BF16, dma=True) for _ in range(BT // 128)]
            WQ = [C.sb(st, "WQ", [128, KC, 256], BF16, dma=True) for _ in range(2)]
            WV = [C.sb(st, "WV", [128, KC, 512], BF16, dma=True) for _ in range(2)]
            sqb = [C.sb(st, "sq", [128, 512], F32) for _ in range(4)]
            rstd = C.sb(st, "rstd", [128, 512], F32)
            iq = iv = ie = 0
            for t0 in range(0, T, BT):
                gn = min(BT, T - t0)
                ntt = gn // 128
                self.load_xblock(XB, xsrc, t0, gn, 0)
                self.rms_to_h(XB, H, gn, "norm_mix", l, sqb, rstd, PS[4])
                for jt in range(16):
                    W = WQ[iq % 2]
                    iq += 1
                    C.dma("sp", out=W[:], in_=wqk[jt, 0], sbt=W, writes=[W])
                    for jj in range(2):
                        j = jt * 2 + jj
                        ps = PS[j % 2]
                        for k in range(KC):
                            C.op("pe", "matmul", reads=[W, H], writes=[ps], out=ps[:, :gn], lhsT=W[:, k, jj * 128:(jj + 1) * 128],
                                 rhs=H[:, k, :gn], start=(k == 0), stop=(k == KC - 1))
                        if ie % 2 == 0:
                            C.op("dve", "tensor_copy", reads=[ps], writes=[QK], out=QK[:, j, :gn], in_=ps[:, :gn])
                        else:
                            C.op("act", "activation", reads=[ps], writes=[QK], out=QK[:, j, :gn], in_=ps[:, :gn], func=AF.Copy)
                        ie += 1
                C.dma("pool", out=self.na_qk.rearrange("(c p) t -> p c t", p=128)[:, :, t0:t0 + gn], in_=QK[:, :, :gn], sbt=QK, reads=[QK])
                for cg in range(4):
                    W = WV[iv % 2]
                    iv += 1
                    C.dma("sp", out=W[:], in_=wv[cg, 0], sbt=W, writes=[W])
                    for tt in range(ntt):
                        ps = PS[2 + (cg * ntt + tt) % 2]
                        for k in range(KC):
                            C.op("pe", "matmul", reads=[W, H], writes=[ps], out=ps[:, :512], lhsT=H[:, k, tt * 128:(tt + 1) * 128],
                                 rhs=W[:, k, :], start=(k == 0), stop=(k == KC - 1))
                        if ie % 2 == 0:
                            C.op("dve", "tensor_copy", reads=[ps], writes=[VS[tt]], out=VS[tt][:, cg * 512:(cg + 1) * 512], in_=ps[:, :512])
                        else:
                            C.op("act", "activation", reads=[ps], writes=[VS[tt]], out=VS[tt][:, cg * 512:(cg + 1) * 512], in_=ps[:, :512],
                                 func=AF.Copy)
                        ie += 1
                for tt in range(ntt):
                    C.dma("pool", out=self.na_v[t0 + tt * 128:t0 + (tt + 1) * 128, :], in_=VS[tt][:], sbt=VS[tt], reads=[VS[tt]])
            C.barrier()
        scale = 128 ** -0.5

        def win(r):
            r0 = min(max(r - 4, 0), R - 8)
            return range(r0, r0 + 8)

        with contextlib.ExitStack() as st:
            MASK = C.sb(st, "MASK", [128, 2, 16, 64], F32, dma=True)
            C.dma("sp", out=MASK[:], in_=self.na_mask[:, :, :, :], sbt=MASK, writes=[MASK])
            QH = [C.sb(st, "QH", [128, T], BF16, dma=True) for _ in range(2)]
            KH = [C.sb(st, "KH", [128, T], BF16, dma=True) for _ in range(2)]
            VH = [C.sb(st, "VH", [128, T // 128, 128], BF16, dma=True) for _ in range(2)]
            GH = [C.sb(st, "GH", [128, 16, 64], F32, dma=True) for _ in range(2)]
            TBL = [C.sb(st, "TBL", [128, 2, 16, 64], F32) for _ in range(2)]
            OTH = [C.sb(st, "OTH", [128, T], BF16, dma=True) for _ in range(2)]
            sc = [C.sb(st, "sc", [128, 256], F32) for _ in range(3)]
            PT = [C.sb(st, "PT", [128, 256], BF16) for _ in range(3)]
            rden = [C.sb(st, "rden", [128, 256], F32) for _ in range(2)]
            it = 0
            for hd in range(16):
                qh, kh, vh, gh, tbl, oth = QH[hd % 2], KH[hd % 2], VH[hd % 2], GH[hd % 2], TBL[hd % 2], OTH[hd % 2]
                C.dma("sp", out=qh[:], in_=self.na_qk[hd * 128:(hd + 1) * 128, :], sbt=qh, writes=[qh])
                C.dma("sp", out=kh[:], in_=self.na_qk[D + hd * 128:D + (hd + 1) * 128, :], sbt=kh, writes=[kh])
                C.dma("sp", out=vh[:], in_=self.na_v[:, hd * 128:(hd + 1) * 128].rearrange("(n p) d -> p n d", p=128), sbt=vh, writes=[vh])
                C.dma("sp", out=gh[:], in_=self.na_g[hd], sbt=gh, writes=[gh])
                for kd in range(2):
                    C.op("pool", "tensor_tensor", reads=[gh, MASK], writes=[tbl], out=tbl[:, kd], in0=gh[:], in1=MASK[:, kd], op=ALU.add)
                for gq in range(R // 4):
                    rows = list(range(4 * gq, 4 * gq + 4))
                    chunks = sorted(set(a // 2 for r in rows for a in win(r)))
                    kind = [1 if (r < 4 or r >= R - 3) else 0 for r in rows]
                    runs = []
                    for ri, r in enumerate(rows):
                        if runs and runs[-1][2] == kind[ri]:
                            runs[-1][1] += 1
                        else:
                            runs.append([ri, 1, kind[ri]])
                    po, pd = PS[4 + gq % 2], PS[6 + gq % 2]
                    for ci, i in enumerate(chunks):
                        pss = PS[it % 4]
                        scb, ptb = sc[it % 3], PT[it % 3]
                        it += 1
                        C.op("pe", "matmul", reads=[kh, qh], writes=[pss], out=pss[:, :256], lhsT=kh[:, i * 128:(i + 1) * 128],
                             rhs=qh[:, gq * 256:(gq + 1) * 256], start=True, stop=True)
                        for (ri, nr, kd) in runs:
                            m0 = rows[ri] - 2 * i + 7
                            assert 0 <= m0 and m0 + nr <= 16, (m0, nr)
                            C.op("dve", "scalar_tensor_tensor", reads=[pss, tbl], writes=[scb], out=scb[:, ri * 64:(ri + nr) * 64],
                                 in0=pss[:, ri * 64:(ri + nr) * 64], scalar=scale,
                                 in1=tbl[:, kd, m0:m0 + nr, :].rearrange("p m c -> p (m c)"), op0=ALU.mult, op1=ALU.add)
                        C.op("act", "activation", reads=[scb], writes=[ptb], out=ptb[:], in_=scb[:], func=AF.Exp)
                        C.op("pe", "matmul", reads=[vh, ptb], writes=[po], out=po[:, :256], lhsT=vh[:, i, :], rhs=ptb[:],
                             start=(ci == 0), stop=(ci == len(chunks) - 1))
                        C.op("pe", "matmul", reads=[self.onesb, ptb], writes=[pd], out=pd[:, :256], lhsT=self.onesb[:], rhs=ptb[:],
                             start=(ci == 0), stop=(ci == len(chunks) - 1))
                    rd = rden[gq % 2]
                    C.op("dve", "reciprocal", reads=[pd], writes=[rd], out=rd[:], in_=pd[:, :256])
                    C.op("dve", "tensor_tensor", reads=[po, rd], writes=[oth], out=oth[:, gq * 256:(gq + 1) * 256], in0=po[:, :256], in1=rd[:],
                         op=ALU.mult)
                C.dma("pool", out=self.na_o[hd * 128:(hd + 1) * 128, :], in_=oth[:], sbt=oth, reads=[oth])
            C.barrier()
        self.out_proj(xsrc, xdst, self.na_o, KC, wo)

    def mix_gdn(self, l, s, xsrc, xdst):
        C, T, PS = self.C, self.T, self.PS
        BTG = 256
        L = 64
        NCK = T // L
        wqkv, wz, wab, wo = self.ws["gdn_qkv"], self.ws["gdn_z"], self.ws["gdn_ab"], self.ws["gdn_o"]

        def bc(ap, axis, shape):
            return ap.unsqueeze(axis).to_broadcast(list(shape))

        with contextlib.ExitStack() as st:
            DTB = C.sb(st, "DTB", [128, 64], F32)
            NRATE = C.sb(st, "NRATE", [128, 64], F32)
            with contextlib.ExitStack() as tmp:
                self.bcast_row(DTB, tmp, "dtb", self.gdn_dt_bias[0:1].rearrange("o d h -> o (d h)"), 64)
                self.bcast_row(NRATE, tmp, "alog", self.gdn_a_log[0:1].rearrange("o d h -> o (d h)"), 64)
                C.op("act", "activation", reads=[NRATE], writes=[NRATE], out=NRATE[:], in_=NRATE[:], func=AF.Exp)
                C.op("dve", "tensor_scalar", reads=[NRATE], writes=[NRATE], out=NRATE[:], in0=NRATE[:], scalar1=-1.0, scalar2=None, op0=ALU.mult)
                C.barrier()
            self.ensure_eps(st)
            oneb = C.sb(st, "oneb", [128, 1], F32)
            C.op("dve", "memset", writes=[oneb], ap=oneb[:], constant=1.0)
            XB = C.sb(st, "XB", [128, KC, BTG + 4], F32, dma=True)
            H = C.sb(st, "H", [128, KC, BTG + 4], BF16)
            WQ = [C.sb(st, "WQ", [128, KC, 256], BF16, dma=True) for _ in range(2)]
            WZ = [C.sb(st, "WZ", [128, KC, 512], BF16, dma=True) for _ in range(2)]
            WAB = C.sb(st, "WAB", [128, KC, 128], BF16, dma=True)
            C.dma("sp", out=WAB[:], in_=wab[0, 0], sbt=WAB, writes=[WAB])
            QKst = C.sb(st, "QKst", [128, 32, BTG], BF16, dma=True)
            KVt = [C.sb(st, "KVt", [128, 6144], BF16, dma=True) for _ in range(2)]
            ZS = [C.sb(st, "ZS", [128, 4096], F32, dma=True) for _ in range(2)]
            GBs = [C.sb(st, "GBs", [128, 128], F32, dma=True) for _ in range(2)]
            g0b = [C.sb(st, "g0", [128, BTG], F32) for _ in range(2)]
            g1b = [C.sb(st, "g1", [128, BTG], F32) for _ in range(2)]
            vlb = [C.sb(st, "vl", [128, BTG], F32) for _ in range(2)]
            sqb2 = [C.sb(st, "sq2", [128, BTG], F32) for _ in range(2)]
            rnb = [C.sb(st, "rn", [128, BTG], F32) for _ in range(2)]
            sqb = [C.sb(st, "sq", [128, 512], F32) for _ in range(4)]
            rstd = C.sb(st, "rstd", [128, 512], F32)
            tab = [C.sb(st, "tab", [128, 64], F32) for _ in range(2)]
            iq = iz = 0
            for t0 in range(0, T, BTG):
                nbi = min(BTG, T - t0)
                nb = nbi + 2
                ntt = nbi // 128
                self.load_xblock(XB, xsrc, t0, nbi, 1)
                self.rms_to_h(XB, H, nb, "norm_mix", l, sqb, rstd, PS[4])
                pend_back = None
                for jt in range(32):
                    W = WQ[iq % 2]
                    iq += 1
                    C.dma("sp", out=W[:], in_=wqkv[jt, 0], sbt=W, writes=[W])
                    for jj in range(2):
                        j = jt * 2 + jj
                        ps = PS[j % 2]
                        for k in range(KC):
                            C.op("pe", "matmul", reads=[W, H], writes=[ps], out=ps[:, :nb], lhsT=W[:, k, jj * 128:(jj + 1) * 128],
                                 rhs=H[:, k, :nb], start=(k == 0), stop=(k == KC - 1))
                        g0, g1, vl = g0b[j % 2], g1b[j % 2], vlb[j % 2]
                        cw = lambda kk, j=j: self.sm("gdn_cw", kk * 64 + j)
                        C.op("act", "activation", reads=[ps, self.small], writes=[g0], out=g0[:, :nbi], in_=ps[:, 1:1 + nbi], func=AF.Identity,
                             scale=cw(1))
                        C.op("dve", "scalar_tensor_tensor", reads=[ps, g0, self.small], writes=[g1], out=g1[:, :nbi], in0=ps[:, 0:nbi],
                             scalar=cw(0), in1=g0[:, :nbi], op0=ALU.mult, op1=ALU.add)
                        C.op("dve", "scalar_tensor_tensor", reads=[ps, g1, self.small], writes=[g0], out=g0[:, :nbi], in0=ps[:, 2:2 + nbi],
                             scalar=cw(2), in1=g1[:, :nbi], op0=ALU.mult, op1=ALU.add)
                        C.op("act", "activation", reads=[g0], writes=[vl], out=vl[:, :nbi], in_=g0[:, :nbi], func=AF.Silu)
                        if j < 32:
                            sq = sqb2[j % 2]
                            C.op("pool", "tensor_tensor", reads=[vl], writes=[sq], out=sq[:, :nbi], in0=vl[:, :nbi], in1=vl[:, :nbi], op=ALU.mult)

                        def back(j=j, vl=vl, nbi=nbi, ntt=ntt):
                            if j < 32:
                                sq, rn = sqb2[j % 2], rnb[j % 2]
                                pn = PS[4 + j % 2]
                                C.op("pe", "matmul", reads=[sq, self.ones], writes=[pn], out=pn[:, :nbi], lhsT=self.ones[:], rhs=sq[:, :nbi],
                                     start=True, stop=True)
                                C.op("act", "activation", reads=[pn], writes=[rn], out=rn[:, :nbi], in_=pn[:, :nbi], func=AF.Sqrt, scale=1.0,
                                     bias=self.epsb[:, 0:1])
                                C.op("dve", "reciprocal", reads=[rn], writes=[rn], out=rn[:, :nbi], in_=rn[:, :nbi])
                                if j < 16:
                                    C.op("dve", "scalar_tensor_tensor", reads=[vl, rn], writes=[QKst], out=QKst[:, j, :nbi], in0=vl[:, :nbi],
                                         scalar=128 ** -0.5, in1=rn[:, :nbi], op0=ALU.mult, op1=ALU.mult)
                                else:
                                    C.op("dve", "tensor_tensor", reads=[vl, rn], writes=[vl], out=vl[:, :nbi], in0=vl[:, :nbi], in1=rn[:, :nbi],
                                         op=ALU.mult)
                                    C.op("act", "activation", reads=[vl], writes=[QKst], out=QKst[:, j, :nbi], in_=vl[:, :nbi], func=AF.Copy)
                            if j >= 16:
                                for tt in range(ntt):
                                    pt = PS[6 + tt % 2]
                                    C.op("pe", "transpose", reads=[vl, self.ident], writes=[pt], out=pt[:, 0:128], in_=vl[:, tt * 128:(tt + 1) * 128],
                                         identity=self.ident[:])
                                    C.op("act" if tt % 2 == 0 else "dve", "activation" if tt % 2 == 0 else "tensor_copy", reads=[pt], writes=[KVt[tt]],
                                         out=KVt[tt][:, (j - 16) * 128:(j - 15) * 128], in_=pt[:, 0:128], **({"func": AF.Copy} if tt % 2 == 0 else {}))

                        if not GDN_DEFER:
                            back()
                            continue
                        if pend_back is not None:
                            pend_back()
                        pend_back = back
                if pend_back is not None:
                    pend_back()
                    pend_back = None
                C.dma("pool", out=self.gdn_qk.rearrange("(c p) t -> p c t", p=128)[:, :, t0:t0 + nbi], in_=QKst[:, :, :nbi], sbt=QKst, reads=[QKst])
                for tt in range(ntt):
                    C.dma("pool", out=self.gdn_kv[t0 + tt * 128:t0 + (tt + 1) * 128, :], in_=KVt[tt][:], sbt=KVt[tt], reads=[KVt[tt]])
                for cg in range(8):
                    W = WZ[iz % 2]
                    iz += 1
                    C.dma("sp", out=W[:], in_=wz[cg, 0], sbt=W, writes=[W])
                    for tt in range(ntt):
                        ps = PS[2 + (cg * ntt + tt) % 2]
                        for k in range(KC):
                            C.op("pe", "matmul", reads=[W, H], writes=[ps], out=ps[:, :512], lhsT=H[:, k, 1 + tt * 128:1 + (tt + 1) * 128],
                                 rhs=W[:, k, :], start=(k == 0), stop=(k == KC - 1))
                        C.op("act", "activation", reads=[ps], writes=[ZS[tt]], out=ZS[tt][:, cg * 512:(cg + 1) * 512], in_=ps[:, :512], func=AF.Silu)
                for tt in range(ntt):
                    C.dma("pool", out=self.gdn_z[t0 + tt * 128:t0 + (tt + 1) * 128, :], in_=ZS[tt][:], sbt=ZS[tt], reads=[ZS[tt]])
                for tt in range(ntt):
                    ps = PS[tt % 2]
                    for k in range(KC):
                        C.op("pe", "matmul", reads=[WAB, H], writes=[ps], out=ps[:, :128], lhsT=H[:, k, 1 + tt * 128:1 + (tt + 1) * 128],
                             rhs=WAB[:, k, :], start=(k == 0), stop=(k == KC - 1))
                    pv = ps[:, 0:128].rearrange("p (d w h) -> p d w h", d=2, w=2)
                    ta_, gb = tab[tt % 2], GBs[tt % 2]
                    C.op("dve", "tensor_tensor", reads=[ps, DTB], writes=[ta_], out=ta_[:].rearrange("p (d h) -> p d h", d=2), in0=pv[:, :, 0, :],
                         in1=DTB[:].rearrange("p (d h) -> p d h", d=2), op=ALU.add)
                    C.op("act", "activation", reads=[ta_], writes=[ta_], out=ta_[:], in_=ta_[:], func=AF.Exp)
                    C.op("act", "activation", reads=[ta_, oneb], writes=[ta_], out=ta_[:], in_=ta_[:], func=AF.Ln, bias=oneb[:, 0:1], scale=1.0)
                    C.op("dve", "tensor_tensor", reads=[ta_, NRATE], writes=[gb], out=gb[:, 0:64], in0=ta_[:], in1=NRATE[:], op=ALU.mult)
                    C.op("act", "activation", reads=[ps], writes=[gb], out=gb[:, 64:128].rearrange("p (d h) -> p d h", d=2), in_=pv[:, :, 1, :],
                         func=AF.Sigmoid)
                    C.dma("pool", out=self.gdn_gb[t0 + tt * 128:t0 + (tt + 1) * 128, :], in_=gb[:], sbt=gb, reads=[gb])
            C.barrier()

        dbg = int(os.environ.get("GDN_DBG", "9"))
        with contextlib.ExitStack() as st:
            self.ensure_eps(st)
            ONORM = C.sb(st, "ONORM", [128, 128], F32)
            with contextlib.ExitStack() as tmp:
                self.bcast_row(ONORM, tmp, "onorm", self.gdn_out_norm[0:1, :], 128)
                C.barrier()
            CONS = C.sb(st, "CONS", [64, 8, 64], F32, dma=True)
            C.dma("sp", out=CONS[:], in_=self.gdn_consts[:, :, :], sbt=CONS, writes=[CONS])
            S = C.sb(st, "S", [128, 4096], F32)
            Sb = C.sb(st, "Sb", [128, 4096], BF16)
            QKB = [C.sb(st, "QKB", [128, 32, 256], BF16, dma=True) for _ in range(2)]
            KVc = [C.sb(st, "KVc", [64, 6144], BF16, dma=True) for _ in range(2)]
            GBc = [C.sb(st, "GBc", [64, 128], F32, dma=True) for _ in range(2)]
            OC = C.sb(st, "OC", [64, 4096], F32, dma=True)
            OF = C.sb(st, "OF", [64, 2048], F32, dma=True)
            ZC = C.sb(st, "ZC", [64, 2048], F32, dma=True)
            OTst = C.sb(st, "OTst", [128, 32, 128], BF16, dma=True)
            EG = C.sb(st, "EG", [128, 96], F32)
            GAM = C.sb(st, "GAM", [64, 32], F32)
            NEGEG = C.sb(st, "NEGEG", [64, 32], F32)
            NEGB = C.sb(st, "NEGB", [64, 32], F32)
            RS = C.sb(st, "RS", [64, 32], F32)
            W2 = lambda nm, dt=F32: [C.sb(st, nm, [64, 8, 64], dt) for _ in range(2)]
            GTRI, DT, PTb = W2("GTRI"), W2("DT"), W2("PT", BF16)
            CDT = BF16 if GDN_NEUMANN_BF16 else F32
            CK = [W2("CKa", CDT), W2("CKb", CDT)]
            CKT = [W2("CKTa", CDT), W2("CKTb", CDT)]
            RR = [W2("RRa", CDT), W2("RRb", CDT)]
            W4 = lambda nm, dt=F32, p=64: [C.sb(st, nm, [p, 512], dt) for _ in range(2)]
            TKS, VP, UB, O1, KD = W4("TKS"), W4("VP", CDT), W4("UB", BF16), W4("O1"), W4("KD", BF16)
            TS = W4("TS", F32, 128)
            ident64 = self.ident[0:64, 0:64]
            ones64 = self.ones[0:64, 0:64]
            idn_t = self.identb if GDN_NEUMANN_BF16 else self.ident
            idn64 = idn_t[0:64, 0:64]
            iblk = 0
            ig = 0
            for dr in range(2):
                U, SUF, MNEG, ST01 = CONS[:, dr, :], CONS[:, 2 + dr, :], CONS[:, 4 + dr, :], CONS[:, 6 + dr, :]
                C.op("dve", "memset", writes=[S], ap=S[:], constant=0.0)
                C.op("pool", "memset", writes=[Sb], ap=Sb[:], constant=0.0)
                qkb = None
                for ci in range(NCK if dbg >= 1 else 0):
                    c = ci if dr == 0 else NCK - 1 - ci
                    if qkb is None or (c % 4 == (0 if dr == 0 else 3)):
                        qkb = QKB[iblk % 2]
                        iblk += 1
                        b0 = (c // 4) * 256
                        C.dma("sp", out=qkb[:], in_=self.gdn_qk.rearrange("(c p) t -> p c t", p=128)[:, :, b0:b0 + 256], sbt=qkb, writes=[qkb])
                    cc = slice((c % 4) * 64, (c % 4) * 64 + 64)
                    kv, gbc = KVc[ci % 2], GBc[ci % 2]
                    C.dma("sp", out=kv[:], in_=self.gdn_kv[c * 64:(c + 1) * 64, :], sbt=kv, writes=[kv])
                    C.dma("sp", out=gbc[:], in_=self.gdn_gb[c * 64:(c + 1) * 64, :], sbt=gbc, writes=[gbc])
                    g = gbc[:, dr * 32:(dr + 1) * 32]
                    beta = gbc[:, 64 + dr * 32:64 + (dr + 1) * 32]
                    p0 = PS[0]
                    C.op("pe", "matmul", reads=[CONS, gbc], writes=[p0], out=p0[0:64, 0:32], lhsT=U, rhs=g, start=True, stop=True)
                    C.op("pe", "matmul", reads=[CONS, gbc], writes=[p0], out=p0[0:64, 32:64], lhsT=SUF, rhs=g, start=True, stop=True)
                    C.op("pe", "matmul", reads=[self.ones, gbc], writes=[p0], out=p0[:, 64:96], lhsT=self.ones[0:64, :], rhs=g, start=True, stop=True)
                    C.op("act", "activation", reads=[p0], writes=[EG], out=EG[0:64, 0:64], in_=p0[0:64, 0:64], func=AF.Exp)
                    C.op("act", "activation", reads=[p0], writes=[EG], out=EG[:, 64:96], in_=p0[:, 64:96], func=AF.Exp)
                    C.op("dve", "tensor_copy", reads=[p0], writes=[GAM], out=GAM[:], in_=p0[0:64, 0:32])
                    C.op("dve", "tensor_scalar", reads=[EG], writes=[NEGEG], out=NEGEG[:], in0=EG[0:64, 0:32], scalar1=-1.0, scalar2=None, op0=ALU.mult)
                    C.op("dve", "tensor_scalar", reads=[gbc], writes=[NEGB], out=NEGB[:], in0=beta, scalar1=-1.0, scalar2=None, op0=ALU.mult)
                    v8 = lambda p: p[0:64, 0:512].rearrange("p (h i) -> p h i", h=8)
                    r4 = lambda t: t[:].rearrange("p (a r) i -> p a r i", r=2)
                    rfin = {}

                    def sn(hg, i2, pk, pA, pB, pCc, kv=kv, gbc=gbc, qkb=qkb, cc=cc, g=g):
                        h0, qk0 = 8 * hg, 4 * hg
                        p1 = pk
                        for a in range(4):
                            kT = qkb[:, 16 + qk0 + a, cc]
                            C.op("pe", "matmul", reads=[qkb], writes=[p1], out=p1[0:64, a * 64:(a + 1) * 64], lhsT=kT, rhs=kT, start=True, stop=True)
                        for a in range(4):
                            C.op("pe", "matmul", reads=[qkb], writes=[p1], out=p1[0:64, 256 + a * 64:256 + (a + 1) * 64], lhsT=qkb[:, 16 + qk0 + a, cc],
                                 rhs=qkb[:, qk0 + a, cc], start=True, stop=True)
                        gtri, dt_, es, bb, ct0, ptb = GTRI[i2], DT[i2], GTRI[i2], CK[0][i2], CKT[0][i2], PTb[i2]
                        C.op("dve", "tensor_tensor", reads=[CONS, gbc], writes=[gtri], out=gtri[:], in0=bc(U, 1, [64, 8, 64]),
                             in1=bc(g[:, h0:h0 + 8], 2, [64, 8, 64]), op=ALU.mult)
                        p2 = pA
                        for hh in range(8):
                            C.op("pe", "matmul", reads=[self.ones, gtri], writes=[p2], out=p2[0:64, hh * 64:(hh + 1) * 64], lhsT=ones64, rhs=gtri[:, hh, :],
                                 start=True, stop=True)
                        yield
                        C.op("dve", "tensor_tensor", reads=[p2, GAM], writes=[dt_], out=dt_[:], in0=v8(p2), in1=bc(GAM[:, h0:h0 + 8], 2, [64, 8, 64]),
                             op=ALU.subtract)
                        C.op("pool", "tensor_tensor", reads=[dt_, CONS], writes=[dt_], out=dt_[:], in0=dt_[:], in1=bc(MNEG, 1, [64, 8, 64]), op=ALU.add)
                        C.op("act", "activation", reads=[dt_], writes=[dt_], out=dt_[:], in_=dt_[:], func=AF.Exp)
                        C.op("pool", "tensor_tensor", reads=[dt_, CONS], writes=[es], out=es[:], in0=dt_[:], in1=bc(ST01, 1, [64, 8, 64]), op=ALU.mult)
                        kkv = p1[0:64, 0:256].rearrange("p (a i) -> p a i", a=4).unsqueeze(2).to_broadcast([64, 4, 2, 64])
                        kqv = p1[0:64, 256:512].rearrange("p (a i) -> p a i", a=4).unsqueeze(2).to_broadcast([64, 4, 2, 64])
                        C.op("dve", "tensor_tensor", reads=[p1, es], writes=[bb], out=r4(bb), in0=kkv, in1=r4(es), op=ALU.mult)
                        C.op("dve", "tensor_tensor", reads=[bb, NEGB], writes=[bb], out=bb[:], in0=bb[:], in1=bc(NEGB[:, h0:h0 + 8], 2, [64, 8, 64]),
                             op=ALU.mult)
                        C.op("dve", "tensor_tensor", reads=[p1, dt_], writes=[ptb], out=r4(ptb), in0=kqv, in1=r4(dt_), op=ALU.mult)
                        p3 = pB
                        for hh in range(8):
                            C.op("pe", "matmul", reads=[bb, idn_t], writes=[p3], out=p3[0:64, hh * 64:(hh + 1) * 64], lhsT=bb[:, hh, :],
                                 rhs=idn64, start=True, stop=True)
                        if GDN_ACTEV:
                            C.op("act", "activation", reads=[p3], writes=[ct0], out=ct0[:], in_=v8(p3), func=AF.Copy)
                        else:
                            C.op("dve", "tensor_copy", reads=[p3], writes=[ct0], out=ct0[:], in_=v8(p3))
                        ck, ckt = bb, ct0
                        rr = RR[0][i2]
                        C.op("pool", "tensor_tensor", reads=[bb, self.ident], writes=[rr], out=rr[:], in0=bb[:], in1=bc(ident64, 1, [64, 8, 64]), op=ALU.add)
                        yield
                        for kk in range(1, 6):
                            cn, cnt, rn_ = CK[kk % 2][i2], CKT[kk % 2][i2], RR[kk % 2][i2]
                            pa, pb, pc = pA, pB, pCc
                            for hh in range(8):
                                C.op("pe", "matmul", reads=[ck, ckt], writes=[pa], out=pa[0:64, hh * 64:(hh + 1) * 64], lhsT=ck[:, hh, :], rhs=ckt[:, hh, :],
                                     start=True, stop=True)
                            if GDN_ACTEV:
                                C.op("act", "activation", reads=[pa], writes=[cnt], out=cnt[:], in_=v8(pa), func=AF.Copy)
                            else:
                                C.op("dve", "tensor_copy", reads=[pa], writes=[cnt], out=cnt[:], in_=v8(pa))
                            if kk < 5:
                                for hh in range(8):
                                    C.op("pe", "matmul", reads=[ck, ckt], writes=[pb], out=pb[0:64, hh * 64:(hh + 1) * 64], lhsT=ckt[:, hh, :],
                                         rhs=ck[:, hh, :], start=True, stop=True)
                                C.op("dve", "tensor_copy", reads=[pb], writes=[cn], out=cn[:], in_=v8(pb))
                            yield
                            for hh in range(8):
                                C.op("pe", "matmul", reads=[cnt, rr], writes=[pc], out=pc[0:64, hh * 64:(hh + 1) * 64], lhsT=cnt[:, hh, :], rhs=rr[:, hh, :],
                                     start=True, stop=True)
                            C.op("dve", "tensor_tensor", reads=[pc, rr], writes=[rn_], out=rn_[:], in0=rr[:], in1=v8(pc), op=ALU.add)
                            yield
                            ck, ckt, rr = cn, cnt, rn_
                        rfin[hg] = rr

                    def val(hg, i2, kv=kv, gbc=gbc, qkb=qkb, cc=cc, beta=beta):
                        h0 = 8 * hg
                        rr, ptb = rfin[hg], PTb[i2]
                        for hv in range(2):
                            hd0 = h0 + 4 * hv
                            tks, vp, ub, o1, kd, ts = TKS[hv], VP[hv], UB[hv], O1[hv], KD[hv], TS[hv]
                            v4 = lambda p, np_=64: p[0:np_, 0:512].rearrange("p (a d) -> p a d", a=4)
                            p5, p6, p7 = PS[5], PS[6], PS[7]
                            for a in range(4):
                                hd = hd0 + a
                                C.op("pe", "matmul", reads=[qkb, Sb], writes=[p5], out=p5[0:64, a * 128:(a + 1) * 128], lhsT=qkb[:, 16 + hd // 2, cc],
                                     rhs=Sb[:, hd * 128:(hd + 1) * 128], start=True, stop=True)
                            for a in range(4):
                                hd = hd0 + a
                                C.op("pe", "matmul", reads=[qkb, Sb], writes=[p7], out=p7[0:64, a * 128:(a + 1) * 128], lhsT=qkb[:, hd // 2, cc],
                                     rhs=Sb[:, hd * 128:(hd + 1) * 128], start=True, stop=True)
                            C.op("dve", "tensor_tensor", reads=[p5, NEGEG], writes=[tks], out=v4(tks), in0=v4(p5), in1=bc(NEGEG[:, hd0:hd0 + 4], 2, [64, 4, 128]),
                                 op=ALU.mult)
                            C.op("pool", "tensor_tensor", reads=[tks, kv], writes=[vp], out=vp[:], in0=tks[:], in1=kv[:, 2048 + hd0 * 128:2048 + (hd0 + 4) * 128],
                                 op=ALU.add)
                            C.op("dve", "tensor_tensor", reads=[p7, EG], writes=[o1], out=v4(o1), in0=v4(p7), in1=bc(EG[0:64, hd0:hd0 + 4], 2, [64, 4, 128]),
                                 op=ALU.mult)
                            for a in range(4):
                                C.op("pe", "matmul", reads=[rr, vp], writes=[p6], out=p6[0:64, a * 128:(a + 1) * 128], lhsT=rr[:, 4 * hv + a, :],
                                     rhs=vp[:, a * 128:(a + 1) * 128], start=True, stop=True)
                            C.op("dve", "tensor_tensor", reads=[p6, gbc], writes=[ub], out=v4(ub), in0=v4(p6), in1=bc(beta[:, hd0:hd0 + 4], 2, [64, 4, 128]),
                                 op=ALU.mult)
                            for a in range(4):
                                C.op("pe", "matmul", reads=[ptb, ub], writes=[p5], out=p5[0:64, a * 128:(a + 1) * 128], lhsT=ptb[:, 4 * hv + a, :],
                                     rhs=ub[:, a * 128:(a + 1) * 128], start=True, stop=True)
                            C.op("dve", "tensor_tensor", reads=[p5, o1], writes=[OC], out=OC[:, hd0 * 128:(hd0 + 4) * 128], in0=o1[:], in1=p5[0:64, 0:512],
                                 op=ALU.add)
                            kq0 = hd0 // 2
                            ktok = kv[:, kq0 * 128:(kq0 + 2) * 128].rearrange("p (a d) -> p a d", a=2).unsqueeze(2).to_broadcast([64, 2, 2, 128])
                            C.op("dve", "tensor_tensor", reads=[kv, EG], writes=[kd], out=kd[:].rearrange("p (a r d) -> p a r d", a=2, r=2), in0=ktok,
                                 in1=bc(EG[0:64, 32 + hd0:32 + hd0 + 4], 2, [64, 4, 128]).rearrange("p (a r) d -> p a r d", a=2), op=ALU.mult)
                            for a in range(4):
                                C.op("pe", "matmul", reads=[kd, ub], writes=[p6], out=p6[:, a * 128:(a + 1) * 128], lhsT=kd[:, a * 128:(a + 1) * 128],
                                     rhs=ub[:, a * 128:(a + 1) * 128], start=True, stop=True)
                            ssl = S[:, hd0 * 128:(hd0 + 4) * 128]
                            C.op("pool", "tensor_tensor", reads=[S, EG], writes=[ts], out=v4(ts, 128), in0=ssl.rearrange("p (a d) -> p a d", a=4),
                                 in1=bc(EG[:, 64 + hd0:64 + hd0 + 4], 2, [128, 4, 128]), op=ALU.mult)
                            C.op("dve", "tensor_tensor", reads=[ts, p6], writes=[S], out=ssl, in0=ts[:], in1=p6[:, 0:512], op=ALU.add)
                            C.op("pool", "tensor_copy", reads=[S], writes=[Sb], out=Sb[:, hd0 * 128:(hd0 + 4) * 128], in_=ssl)

                    for hp in range(2):
                        hgA, hgB = 2 * hp, 2 * hp + 1
                        gens = [sn(hgA, 0, PS[1], PS[2], PS[3], PS[4]), sn(hgB, 1, PS[0], PS[5], PS[6], PS[7])]
                        if not GDN_ZIP:
                            for gn_ in gens:
                                for _ in gn_:
                                    pass
                            gens = []
                        while gens:
                            for gn_ in list(gens):
                                try:
                                    next(gn_)
                                except StopIteration:
                                    gens.remove(gn_)
                        val(hgA, 0)
                        val(hgB, 1)
                    if dbg < 6:
                        continue
                    if dr == 0:
                        C.dma("pool", out=self.gdn_of[c * 64:(c + 1) * 64, :], in_=OC[:], sbt=OC, reads=[OC])
                    else:
                        for hf in range(2):
                            cs_ = slice(hf * 2048, (hf + 1) * 2048)
                            C.dma("sp", out=OF[:], in_=self.gdn_of[c * 64:(c + 1) * 64, cs_], sbt=OF, writes=[OF])
                            C.dma("sp", out=ZC[:], in_=self.gdn_z[c * 64:(c + 1) * 64, cs_], sbt=ZC, writes=[ZC])
                            och = OC[:, cs_]
                            o3 = och.rearrange("p (h d) -> p h d", h=16)
                            rs = RS[:, hf * 16:(hf + 1) * 16]
                            C.op("pool", "tensor_tensor", reads=[OC, OF], writes=[OC], out=och, in0=och, in1=OF[:], op=ALU.add)
                            C.op("dve", "tensor_tensor", reads=[OC], writes=[OF], out=OF[:], in0=och, in1=och, op=ALU.mult)
                            C.op("dve", "tensor_reduce", reads=[OF], writes=[RS], out=rs, in_=OF[:].rearrange("p (h d) -> p h d", h=16), axis=AX.X, op=ALU.add)
                            C.op("act", "activation", reads=[RS], writes=[RS], out=rs, in_=rs, func=AF.Sqrt, scale=1.0 / 128, bias=self.epsb[0:64, 0:1])
                            C.op("dve", "reciprocal", reads=[RS], writes=[RS], out=rs, in_=rs)
                            C.op("dve", "tensor_tensor", reads=[OC, RS], writes=[OC], out=o3, in0=o3, in1=bc(rs, 2, [64, 16, 128]), op=ALU.mult)
                            C.op("pool", "tensor_tensor", reads=[OC, ONORM], writes=[OC], out=o3, in0=o3, in1=bc(ONORM[0:64, :], 1, [64, 16, 128]), op=ALU.mult)
                            C.op("dve", "tensor_tensor", reads=[OC, ZC], writes=[OC], out=och, in0=och, in1=ZC[:], op=ALU.mult)
                        for h8 in range(4):
                            pt = PS[4 + h8 % 2]
                            for hh in range(8):
                                hd = h8 * 8 + hh
                                C.op("pe", "transpose", reads=[OC, self.ident], writes=[pt], out=pt[:, hh * 64:(hh + 1) * 64], in_=OC[:, hd * 128:(hd + 1) * 128],
                                     identity=ident64)
                            dst = OTst[:, h8 * 8:(h8 + 1) * 8, (c % 2) * 64:(c % 2) * 64 + 64]
                            srcv = pt[:, 0:512].rearrange("p (h t) -> p h t", h=8)
                            C.op("dve", "tensor_copy", reads=[pt], writes=[OTst], out=dst, in_=srcv)
                        if c % 2 == 0:
                            b0 = (c // 2) * 128
                            C.dma("pool", out=self.gdn_o.rearrange("(c p) t -> p c t", p=128)[:, :, b0:b0 + 128], in_=OTst[:], sbt=OTst, reads=[OTst])
                C.barrier()
        self.out_proj(xsrc, xdst, self.gdn_o, 32, wo, BT=256)


    def mix_s5(self, l, s, xsrc, xdst):
        C, T, PS = self.C, self.T, self.PS
        L = 128
        NCH = T // L
        PI = float(np.pi)
        with contextlib.ExitStack() as st:
            self.ensure_eps(st)
            XB = [C.sb(st, "XB", [128, KC, 512], F32, dma=True) for _ in range(2)]
            HF = [C.sb(st, "HF", [128, KC, 512], F32, dma=True) for _ in range(2)]
            sqb = [C.sb(st, "sq", [128, 512], F32) for _ in range(4)]
            rstd = C.sb(st, "rstd", [128, 512], F32)
            ib = 0
            for t0 in range(0, T, 512):
                gn = min(512, T - t0)
                X, Hf = XB[ib % 2], HF[ib % 2]
                ib += 1
                self.load_xblock(X, xsrc, t0, gn, 0)
                self.rms_to_h(X, None, gn, "norm_mix", l, sqb, rstd, PS[4], out_f32=Hf)
                C.dma("pool", out=self.s5_h.rearrange("(c p) t -> p c t", p=128)[:, :, t0:t0 + gn], in_=Hf[:, :, :gn], sbt=Hf, reads=[Hf])
            C.barrier()

        def rev(t, a, b):
            return t[:, b - 1:a - 1:-1] if a > 0 else t[:, b - 1::-1]

        with contextlib.ExitStack() as st:
            IOTA = C.sb(st, "IOTA", [128, L], F32, dma=True)
            C.dma("sp", out=IOTA[:], in_=self.iota_in[:, :], sbt=IOTA, writes=[IOTA])
            LAM = C.sb(st, "LAM", [128, 2, 3, 64], F32, dma=True)
            C.dma("sp", out=LAM[:], in_=self.s5_lam[:, :, :, :], sbt=LAM, writes=[LAM])
            MUL = C.sb(st, "MUL", [128, 4, L], F32)
            C.op("dve", "memset", writes=[MUL], ap=MUL[:], constant=1.0)
            C.op("dve", "memset", writes=[MUL], ap=MUL[:, :, 0:1], constant=0.0)
            HB = [C.sb(st, "HB", [128, T], F32, dma=True) for _ in range(2)]
            YT = C.sb(st, "YT", [128, T], F32)
            YO = C.sb(st, "YO", [128, T], BF16, dma=True)
            Bt = [C.sb(st, "Bt", [128, 2, 4, 128], F32, dma=True) for _ in range(2)]
            Ct = [C.sb(st, "Ct", [128, 2, 4, 128], F32, dma=True) for _ in range(2)]
            tabs = [[C.sb(st, "tab", [128, 4, L], F32) for _ in range(4)] for _ in range(2)]
            CTt, SNt, GMt, GIt = [C.sb(st, "trig", [128, 4, L], F32) for _ in range(4)]
            ang = [C.sb(st, "ang", [128, L], F32) for _ in range(2)]
            angi = C.sb(st, "angi", [128, L], mybir.dt.int32)
            angf = C.sb(st, "angf", [128, L], F32)
            sm = C.sb(st, "s5sm", [128, 16, 4], F32)
            wk = [[C.sb(st, "wk", [128, 4, L], F32) for _ in range(2)] for _ in range(10)]
            ta = [C.sb(st, "ta", [128, 512], F32) for _ in range(2)]
            tb = [C.sb(st, "tb", [128, 512], F32) for _ in range(2)]
            tc_ = [C.sb(st, "tc", [128, 512], F32) for _ in range(2)]
            it = 0
            ifd = 0
            for fc in range(KC):
                hb = HB[fc % 2]
                C.dma("sp", out=hb[:], in_=self.s5_h[fc * 128:(fc + 1) * 128, :], sbt=hb, writes=[hb])
                for dr in range(2):
                    bt, ct = Bt[ifd % 2], Ct[ifd % 2]
                    T1r, T1i, T2r, T2i = tabs[ifd % 2]
                    ifd += 1
                    C.dma("sp", out=bt[:], in_=self.s5_B[dr, fc], sbt=bt, writes=[bt])
                    C.dma("sp", out=ct[:], in_=self.s5_C[dr, fc], sbt=ct, writes=[ct])
                    C.op("act", "activation", reads=[ct], writes=[ct], out=ct[:, 1], in_=ct[:, 1], func=AF.Copy, scale=-1.0)
                    are = LAM[:, dr, 0, 4 * fc:4 * fc + 4]
                    aim = LAM[:, dr, 1, 4 * fc:4 * fc + 4]
                    ldt = LAM[:, dr, 2, 4 * fc:4 * fc + 4]
                    S = lambda i: sm[:, i, :]
                    C.op("act", "activation", reads=[LAM], writes=[sm], out=S(0), in_=ldt, func=AF.Exp)
                    C.op("dve", "tensor_tensor", reads=[LAM, sm], writes=[sm], out=S(1), in0=are, in1=S(0), op=ALU.mult)
                    C.op("dve", "tensor_scalar", reads=[sm], writes=[sm], out=S(2), in0=S(1), scalar1=-1.0, scalar2=None, op0=ALU.mult)
                    C.op("dve", "tensor_tensor", reads=[LAM, sm], writes=[sm], out=S(3), in0=aim, in1=S(0), op=ALU.mult)
                    for q4 in range(4):
                        for (dst, off) in ((SNt, 0.0), (CTt, 0.5 * PI)):
                            a, a0 = ang[0], ang[1]
                            C1 = 6.28125
                            C2 = 2 * PI - C1
                            C.op("dve", "tensor_scalar", reads=[IOTA, sm], writes=[a0], out=a0[:], in0=IOTA[:], scalar1=sm[:, 3, q4:q4 + 1],
                                 scalar2=off, op0=ALU.mult, op1=ALU.add)
                            C.op("dve", "tensor_scalar", reads=[a0], writes=[angi], out=angi[:], in0=a0[:], scalar1=1.0 / (2 * PI), scalar2=None,
                                 op0=ALU.mult)
                            C.op("dve", "tensor_copy", reads=[angi], writes=[angf], out=angf[:], in_=angi[:])
                            C.op("dve", "scalar_tensor_tensor", reads=[angf, a0], writes=[a], out=a[:], in0=angf[:], scalar=-C1, in1=a0[:],
                                 op0=ALU.mult, op1=ALU.add)
                            C.op("dve", "scalar_tensor_tensor", reads=[angf, a], writes=[a], out=a[:], in0=angf[:], scalar=-C2, in1=a[:],
                                 op0=ALU.mult, op1=ALU.add)
                            C.op("dve", "tensor_scalar", reads=[a], writes=[a0], out=a0[:], in0=a[:], scalar1=PI, scalar2=-2 * PI, op0=ALU.is_gt,
                                 op1=ALU.mult)
                            C.op("dve", "tensor_tensor", reads=[a, a0], writes=[a], out=a[:], in0=a[:], in1=a0[:], op=ALU.add)
                            C.op("dve", "tensor_scalar", reads=[a], writes=[a0], out=a0[:], in0=a[:], scalar1=-PI, scalar2=2 * PI, op0=ALU.is_lt,
                                 op1=ALU.mult)
                            C.op("dve", "tensor_tensor", reads=[a, a0], writes=[a], out=a[:], in0=a[:], in1=a0[:], op=ALU.add)
                            C.op("act", "activation", reads=[a], writes=[dst], out=dst[:, q4, :], in_=a[:], func=AF.Sin)
                        C.op("act", "activation", reads=[IOTA, sm], writes=[GMt], out=GMt[:, q4, :], in_=IOTA[:], func=AF.Exp,
                             scale=sm[:, 1, q4:q4 + 1])
                        C.op("act", "activation", reads=[IOTA, sm], writes=[GIt], out=GIt[:, q4, :], in_=IOTA[:], func=AF.Exp,
                             scale=sm[:, 2, q4:q4 + 1])
                    rho, c1, s1 = GMt[:, :, 0], CTt[:, :, 0], SNt[:, :, 0]
                    tt_ = lambda o, a_, b_, op_, rd: C.op("dve", "tensor_tensor", reads=rd, writes=[sm], out=o, in0=a_, in1=b_, op=op_)
                    tt_(S(4), rho, c1, ALU.mult, [GMt, CTt])
                    C.op("dve", "tensor_scalar", reads=[sm], writes=[sm], out=S(4), in0=S(4), scalar1=-1.0, scalar2=None, op0=ALU.add)
                    tt_(S(5), rho, s1, ALU.mult, [GMt, SNt])
                    tt_(S(6), are, are, ALU.mult, [LAM])
                    tt_(S(7), aim, aim, ALU.mult, [LAM])
                    tt_(S(6), S(6), S(7), ALU.add, [sm])
                    C.op("dve", "reciprocal", reads=[sm], writes=[sm], out=S(6), in_=S(6))
                    tt_(S(7), S(4), are, ALU.mult, [sm, LAM])
                    tt_(S(8), S(5), aim, ALU.mult, [sm, LAM])
                    tt_(S(7), S(7), S(8), ALU.add, [sm])
                    tt_(S(9), S(7), S(6), ALU.mult, [sm])
                    tt_(S(7), S(5), are, ALU.mult, [sm, LAM])
                    tt_(S(8), S(4), aim, ALU.mult, [sm, LAM])
                    tt_(S(7), S(7), S(8), ALU.subtract, [sm])
                    tt_(S(10), S(7), S(6), ALU.mult, [sm])
                    C.op("dve", "tensor_scalar", reads=[sm], writes=[sm], out=S(11), in0=S(9), scalar1=-1.0, scalar2=None, op0=ALU.mult)
                    for q4 in range(4):
                        u = ang[q4 % 2]
                        C.op("dve", "tensor_scalar", reads=[CTt, sm], writes=[u], out=u[:], in0=CTt[:, q4, :], scalar1=sm[:, 9, q4:q4 + 1],
                             scalar2=None, op0=ALU.mult)
                        C.op("dve", "scalar_tensor_tensor", reads=[SNt, sm, u], writes=[u], out=u[:], in0=SNt[:, q4, :],
                             scalar=sm[:, 10, q4:q4 + 1], in1=u[:], op0=ALU.mult, op1=ALU.add)
                        C.op("dve", "tensor_tensor", reads=[u, GIt], writes=[T1r], out=T1r[:, q4, :], in0=u[:], in1=GIt[:, q4, :], op=ALU.mult)
                        C.op("dve", "tensor_scalar", reads=[CTt, sm], writes=[u], out=u[:], in0=CTt[:, q4, :], scalar1=sm[:, 10, q4:q4 + 1],
                             scalar2=None, op0=ALU.mult)
                        C.op("dve", "scalar_tensor_tensor", reads=[SNt, sm, u], writes=[u], out=u[:], in0=SNt[:, q4, :],
                             scalar=sm[:, 11, q4:q4 + 1], in1=u[:], op0=ALU.mult, op1=ALU.add)
                        C.op("dve", "tensor_tensor", reads=[u, GIt], writes=[T1i], out=T1i[:, q4, :], in0=u[:], in1=GIt[:, q4, :], op=ALU.mult)
                    C.op("dve", "tensor_tensor", reads=[GMt, CTt], writes=[T2r], out=T2r[:], in0=GMt[:], in1=CTt[:], op=ALU.mult)
                    C.op("dve", "tensor_tensor", reads=[GMt, SNt], writes=[T2i], out=T2i[:], in0=GMt[:], in1=SNt[:], op=ALU.mult)
                    xprev = None
                    for ci in range(NCH):
                        cb = ci if dr == 0 else NCH - 1 - ci
                        hcols = hb[:, cb * L:(cb + 1) * L] if dr == 0 else rev(hb, cb * L, (cb + 1) * L)
                        pr, pi_ = PS[it % 2], PS[2 + it % 2]
                        py = PS[4 + it % 2]
                        W = [wk[i][it % 2] for i in range(10)]
                        it += 1
                        for q4 in range(4):
                            C.op("pe", "matmul", reads=[bt, hb], writes=[pr], out=pr[:, q4 * L:(q4 + 1) * L], lhsT=bt[:, 0, q4, :], rhs=hcols,
                                 start=True, stop=True)
                        for q4 in range(4):
                            C.op("pe", "matmul", reads=[bt, hb], writes=[pi_], out=pi_[:, q4 * L:(q4 + 1) * L], lhsT=bt[:, 1, q4, :], rhs=hcols,
                                 start=True, stop=True)
                        v3 = lambda p: p[:, 0:4 * L].rearrange("p (q t) -> p q t", q=4)
                        m1, m2, m3, m4, wr, wi, sr, si, xr, xi = W
                        C.op("dve", "tensor_tensor", reads=[T1r, pr], writes=[m1], out=m1[:], in0=T1r[:], in1=v3(pr), op=ALU.mult)
                        C.op("dve", "tensor_tensor", reads=[T1i, pi_], writes=[m2], out=m2[:], in0=T1i[:], in1=v3(pi_), op=ALU.mult)
                        C.op("dve", "tensor_tensor", reads=[T1r, pi_], writes=[m3], out=m3[:], in0=T1r[:], in1=v3(pi_), op=ALU.mult)
                        C.op("dve", "tensor_tensor", reads=[T1i, pr], writes=[m4], out=m4[:], in0=T1i[:], in1=v3(pr), op=ALU.mult)
                        C.op("pool", "tensor_tensor", reads=[m1, m2], writes=[wr], out=wr[:], in0=m1[:], in1=m2[:], op=ALU.subtract)
                        C.op("pool", "tensor_tensor", reads=[m3, m4], writes=[wi], out=wi[:], in0=m3[:], in1=m4[:], op=ALU.add)
                        if xprev is not None:
                            C.op("dve", "tensor_tensor", reads=[wr, xprev[0]], writes=[wr], out=wr[:, :, 0:1], in0=wr[:, :, 0:1],
                                 in1=xprev[0][:, :, L - 1:L], op=ALU.add)
                            C.op("dve", "tensor_tensor", reads=[wi, xprev[1]], writes=[wi], out=wi[:, :, 0:1], in0=wi[:, :, 0:1],
                                 in1=xprev[1][:, :, L - 1:L], op=ALU.add)
                        f2 = lambda t: t[:].rearrange("p q t -> p (q t)")
                        C.op("dve", "tensor_tensor_scan", reads=[MUL, wr], writes=[sr], out=f2(sr), data0=f2(MUL), data1=f2(wr), initial=0.0,
                             op0=ALU.mult, op1=ALU.add)
                        C.op("dve", "tensor_tensor_scan", reads=[MUL, wi], writes=[si], out=f2(si), data0=f2(MUL), data1=f2(wi), initial=0.0,
                             op0=ALU.mult, op1=ALU.add)
                        C.op("pool", "tensor_tensor", reads=[T2r, sr], writes=[m1], out=m1[:], in0=T2r[:], in1=sr[:], op=ALU.mult)
                        C.op("pool", "tensor_tensor", reads=[T2i, si], writes=[m2], out=m2[:], in0=T2i[:], in1=si[:], op=ALU.mult)
                        C.op("dve", "tensor_tensor", reads=[T2r, si], writes=[m3], out=m3[:], in0=T2r[:], in1=si[:], op=ALU.mult)
                        C.op("dve", "tensor_tensor", reads=[T2i, sr], writes=[m4], out=m4[:], in0=T2i[:], in1=sr[:], op=ALU.mult)
                        C.op("pool", "tensor_tensor", reads=[m1, m2], writes=[xr], out=xr[:], in0=m1[:], in1=m2[:], op=ALU.subtract)
                        C.op("dve", "tensor_tensor", reads=[m3, m4], writes=[xi], out=xi[:], in0=m3[:], in1=m4[:], op=ALU.add)
                        xprev = (xr, xi)
                        for q4 in range(4):
                            C.op("pe", "matmul", reads=[ct, xr], writes=[py], out=py[:, :L], lhsT=ct[:, 0, q4, :], rhs=xr[:, q4, :],
                                 start=(q4 == 0), stop=False)
                        for q4 in range(4):
                            C.op("pe", "matmul", reads=[ct, xi], writes=[py], out=py[:, :L], lhsT=ct[:, 1, q4, :], rhs=xi[:, q4, :],
                                 start=False, stop=(q4 == 3))
                        if dr == 0:
                            C.op("act", "activation", reads=[py], writes=[YT], out=YT[:, cb * L:(cb + 1) * L], in_=py[:, :L], func=AF.Copy)
                        else:
                            yv = rev(YT, cb * L, (cb + 1) * L)
                            C.op("dve", "tensor_tensor", reads=[YT, py], writes=[YT], out=yv, in0=yv, in1=py[:, :L], op=ALU.add)
                for g0 in range(0, T, 512):
                    gn = min(512, T - g0)
                    i2 = (g0 // 512) % 2
                    C.op("dve", "scalar_tensor_tensor", reads=[hb, YT, self.small], writes=[tc_[i2]], out=tc_[i2][:, :gn], in0=hb[:, g0:g0 + gn],
                         scalar=self.sm("s5_d", fc), in1=YT[:, g0:g0 + gn], op0=ALU.mult, op1=ALU.add)
                    self.gelu_tanh(YO, YO[:, g0:g0 + gn], tc_[i2], tc_[i2][:, :gn], ta[i2], tb[i2], gn)
                C.dma("pool", out=self.s5_y[fc * 128:(fc + 1) * 128, :], in_=YO[:], sbt=YO, reads=[YO])
            C.barrier()
        wg = self.ws["s5_glu"]
        with contextlib.ExitStack() as st:
            XB = [C.sb(st, "XBo", [128, KC, 512], F32, dma=True) for _ in range(2)]
            AB = [C.sb(st, "ABo", [128, KC, 512], BF16, dma=True) for _ in range(2)]
            WA = [C.sb(st, "WA", [128, KC, 256], BF16, dma=True) for _ in range(2)]
            WB = [C.sb(st, "WB", [128, KC, 256], BF16, dma=True) for _ in range(2)]
            sgb = [C.sb(st, "sg", [128, 512], F32) for _ in range(2)]
            g0b = [C.sb(st, "g0", [128, 512], F32) for _ in range(2)]
            io = ib = 0
            for t0 in range(0, T, 512):
                gn = min(512, T - t0)
                X, Ab = XB[ib % 2], AB[ib % 2]
                ib += 1
                self.load_xblock(X, xsrc, t0, gn, 0)
                C.dma("sp", out=Ab[:, :, :gn], in_=self.s5_y.rearrange("(c p) t -> p c t", p=128)[:, :, t0:t0 + gn], sbt=Ab, writes=[Ab])
                for ot in range(8):
                    Wa, Wb = WA[io % 2], WB[io % 2]
                    io += 1
                    C.dma("sp", out=Wa[:], in_=wg[ot, 0], sbt=Wa, writes=[Wa])
                    C.dma("sp", out=Wb[:], in_=wg[8 + ot, 0], sbt=Wb, writes=[Wb])
                    for oo in range(2):
                        oc = ot * 2 + oo
                        pa, pb = PS[oc % 2], PS[2 + oc % 2]
                        for k in range(KC):
                            C.op("pe", "matmul", reads=[Wa, Ab], writes=[pa], out=pa[:, :gn], lhsT=Wa[:, k, oo * 128:(oo + 1) * 128],
                                 rhs=Ab[:, k, :gn], start=(k == 0), stop=(k == KC - 1))
                        for k in range(KC):
                            C.op("pe", "matmul", reads=[Wb, Ab], writes=[pb], out=pb[:, :gn], lhsT=Wb[:, k, oo * 128:(oo + 1) * 128],
                                 rhs=Ab[:, k, :gn], start=(k == 0), stop=(k == KC - 1))
                        sg, g0 = sgb[oc % 2], g0b[oc % 2]
                        C.op("act", "activation", reads=[pb], writes=[sg], out=sg[:, :gn], in_=pb[:, :gn], func=AF.Sigmoid)
                        C.op("dve", "tensor_tensor", reads=[sg, pa], writes=[g0], out=g0[:, :gn], in0=sg[:, :gn], in1=pa[:, :gn], op=ALU.mult)
                        C.op("pool", "tensor_tensor", reads=[X, g0], writes=[X], out=X[:, oc, :gn], in0=X[:, oc, :gn], in1=g0[:, :gn], op=ALU.add)
                dst = xdst.rearrange("(c p) t -> p c t", p=128)[:, :, t0:t0 + gn]
                C.dma("pool", out=dst, in_=X[:, :, :gn], sbt=X, reads=[X])
            C.barrier()

    def small_layout(self):
        def add(name, n):
            self.small_off[name] = self.small_n
            self.small_n += n
        add("norm_mix", DEPTH * KC)
        add("norm_ffn", DEPTH * KC)
        add("norm_ple", DEPTH * KC)
        add("final_norm", KC)
        add("conv_w", DEPTH * 3 * NFC)
        add("conv_b", DEPTH * NFC)
        add("s5_d", KC)
        add("gdn_cw", 3 * 64)

    def sm(self, name, idx):
        o = self.small_off[name] + idx
        return self.small[:, o:o + 1]

    def prologue(self, x_in):
        C, T = self.C, self.T
        with contextlib.ExitStack() as st:
            xtok = [C.sb(st, "xtok", [128, D], F32, dma=True) for _ in range(3)]
            stage = [C.sb(st, "stage", [128, KC, 512], F32, dma=True) for _ in range(2)]
            it = 0
            for s in range(self.NS):
                for g0 in range(0, T, 512):
                    gn = min(512, T - g0)
                    sg = stage[(g0 // 512) % 2]
                    for tt in range(0, gn, 128):
                        xt = xtok[it % 3]
                        it += 1
                        C.dma("sp", out=xt[:], in_=x_in[s, g0 + tt:g0 + tt + 128, :], sbt=xt, writes=[xt])
                        for c4 in range(0, KC, 4):
                            pst = self.PS[(c4 // 4) % 2 + 2 * ((tt // 128) % 2)]
                            for c in range(c4, c4 + 4):
                                C.op("pe", "transpose", reads=[xt, self.ident], writes=[pst],
                                     out=pst[:, (c - c4) * 128:(c - c4 + 1) * 128], in_=xt[:, c * 128:(c + 1) * 128],
                                     identity=self.ident[:])
                            eng = "dve" if (c4 // 4) % 2 == 0 else "act"
                            src = pst[:, 0:512].rearrange("p (c t) -> p c t", c=4)
                            if eng == "dve":
                                C.op("dve", "tensor_copy", reads=[pst], writes=[sg], out=sg[:, c4:c4 + 4, tt:tt + 128], in_=src)
                            else:
                                C.op("act", "activation", reads=[pst], writes=[sg], out=sg[:, c4:c4 + 4, tt:tt + 128], in_=src,
                                     func=AF.Copy)
                    dst = self.xT[0][s].rearrange("(c p) t -> p c t", p=128)[:, :, g0:g0 + gn]
                    C.dma("pool", out=dst, in_=sg[:, :, 0:gn], sbt=sg, reads=[sg])
            C.barrier()

    def rms_to_h(self, XB, H, nb, gain_name, l, sqb, rstd, psb, out_f32=None):
        C = self.C
        for c in range(KC):
            sq = sqb[c % len(sqb)]
            if c % 2 == 0:
                C.op("dve", "tensor_tensor", reads=[XB], writes=[sq], out=sq[:, :nb], in0=XB[:, c, :nb], in1=XB[:, c, :nb], op=ALU.mult)
            else:
                C.op("act", "activation", reads=[XB], writes=[sq], out=sq[:, :nb], in_=XB[:, c, :nb], func=AF.Square)
            C.op("pe", "matmul", reads=[sq, self.ones], writes=[psb], out=psb[:, :nb], lhsT=self.ones[:], rhs=sq[:, :nb],
                 start=(c == 0), stop=(c == KC - 1))
        C.op("act", "activation", reads=[psb], writes=[rstd], out=rstd[:, :nb], in_=psb[:, :nb], func=AF.Sqrt,
             scale=1.0 / D, bias=self.epsb[:, 0:1])
        C.op("dve", "reciprocal", reads=[rstd], writes=[rstd], out=rstd[:, :nb], in_=rstd[:, :nb])
        for c in range(KC):
            dst = H if out_f32 is None else out_f32
            C.op("dve", "scalar_tensor_tensor", reads=[XB, rstd, self.small], writes=[dst], out=dst[:, c, :nb], in0=XB[:, c, :nb],
                 scalar=self.sm(gain_name, l * KC + c), in1=rstd[:, :nb], op0=ALU.mult, op1=ALU.mult)

    def ensure_eps(self, st):
        C = self.C
        self.epsb = C.sb(st, "epsb", [128, 1], F32)
        C.op("dve", "memset", writes=[self.epsb], ap=self.epsb[:], constant=EPS)

    def load_xblock(self, XB, xsrc, t0, nbi, halo):
        C, T = self.C, self.T
        lo, hi = t0 - halo, t0 + nbi + halo
        clo, chi = max(lo, 0), min(hi, T)
        if clo > lo:
            C.op("dve", "memset", writes=[XB], ap=XB[:, :, 0:clo - lo], constant=0.0)
        if chi < hi:
            C.op("dve", "memset", writes=[XB], ap=XB[:, :, chi - lo:hi - lo], constant=0.0)
        src = xsrc.rearrange("(c p) t -> p c t", p=128)[:, :, clo:chi]
        C.dma("sp", out=XB[:, :, clo - lo:chi - lo], in_=src, sbt=XB, writes=[XB])

    def ffn_ple(self, l, s, xsrc, xdst, p_l):
        C, T = self.C, self.T
        PS = self.PS
        wgu, wdn, wpg, wpp = self.ws[("gu", l)], self.ws[("dn", l)], self.ws[("pg", l)], self.ws[("pp", l)]
        with contextlib.ExitStack() as st:
            self.ensure_eps(st)
            XB = C.sb(st, "XB", [128, KC, 512], F32, dma=True)
            H = C.sb(st, "H", [128, KC, 512], BF16)
            A = C.sb(st, "A", [128, 22, 512], BF16)
            WGU = [C.sb(st, "WGU", [128, 2, KC, 256], BF16, dma=True) for _ in range(2)]
            WD = [C.sb(st, "WD", [128, 11, 512], BF16, dma=True) for _ in range(3)]
            WPG = [C.sb(st, "WPG", [128, KC, 256], BF16, dma=True) for _ in range(2)]
            WPP = C.sb(st, "WPP", [128, 2, D], BF16, dma=True)
            sqb = [C.sb(st, "sq", [128, 512], F32) for _ in range(4)]
            rstd = C.sb(st, "rstd", [128, 512], F32)
            g0b = [C.sb(st, "g0", [128, 512], F32) for _ in range(2)]
            g1b = [C.sb(st, "g1", [128, 512], F32) for _ in range(2)]
            sgb = [C.sb(st, "sg", [128, 512], F32) for _ in range(2)]
            ptok = [C.sb(st, "ptok", [128, PLE], F32, dma=True) for _ in range(2)]
            PT = C.sb(st, "PT", [128, 2, 512], BF16)
            C.dma("sp", out=WPP[:], in_=wpp[0, 0], sbt=WPP, writes=[WPP])
            iw = 0
            idn = 0
            ipg = 0
            for (t0, nbi) in blocks_of(T):
                nb = nbi + 2
                self.load_xblock(XB, xsrc, t0, nbi, 1)
                self.rms_to_h(XB, H, nb, "norm_ffn", l, sqb, rstd, PS[4])
                for hf in range(2):
                    for jp in range(11):
                        jpg = hf * 11 + jp
                        W = WGU[iw % 2]
                        iw += 1
                        C.dma("sp", out=W[:, 0], in_=wgu[jpg, 0], sbt=W, writes=[W])
                        C.dma("sp", out=W[:, 1], in_=wgu[22 + jpg, 0], sbt=W, writes=[W])
                        for jj in range(2):
                            j = jpg * 2 + jj
                            ja = jp * 2 + jj
                            pg, pu = PS[j % 2], PS[2 + j % 2]
                            for k in range(KC):
                                C.op("pe", "matmul", reads=[W, H], writes=[pg], out=pg[:, :nb], lhsT=W[:, 0, k, jj * 128:(jj + 1) * 128],
                                     rhs=H[:, k, :nb], start=(k == 0), stop=(k == KC - 1))
                            for k in range(KC):
                                C.op("pe", "matmul", reads=[W, H], writes=[pu], out=pu[:, :nb], lhsT=W[:, 1, k, jj * 128:(jj + 1) * 128],
                                     rhs=H[:, k, :nb], start=(k == 0), stop=(k == KC - 1))
                            g0, g1, sg = g0b[j % 2], g1b[j % 2], sgb[j % 2]
                            cw = lambda kk: self.sm("conv_w", (l * 3 + kk) * NFC + j)
                            C.op("act", "activation", reads=[pg, self.small], writes=[g0], out=g0[:, :nbi], in_=pg[:, 1:1 + nbi],
                                 func=AF.Identity, scale=cw(1))
                            C.op("dve", "scalar_tensor_tensor", reads=[pg, g0, self.small], writes=[g1], out=g1[:, :nbi],
                                 in0=pg[:, 0:nbi], scalar=cw(0), in1=g0[:, :nbi], op0=ALU.mult, op1=ALU.add)
                            C.op("dve", "scalar_tensor_tensor", reads=[pg, g1, self.small], writes=[g0], out=g0[:, :nbi],
                                 in0=pg[:, 2:2 + nbi], scalar=cw(2), in1=g1[:, :nbi], op0=ALU.mult, op1=ALU.add)
                            C.op("act", "activation", reads=[g0, self.small], writes=[sg], out=sg[:, :nbi], in_=g0[:, :nbi],
                                 func=AF.Silu, bias=self.sm("conv_b", l * NFC + j), scale=1.0)
                            C.op("dve", "tensor_tensor", reads=[sg, pu], writes=[A], out=A[:, ja, :nbi], in0=sg[:, :nbi],
                                 in1=pu[:, 1:1 + nbi], op=ALU.mult)
                    for q in range(4):
                        for g in range(2):
                            Wd = WD[idn % 3]
                            idn += 1
                            C.dma("sp", out=Wd[:], in_=wdn[q, hf * 2 + g], sbt=Wd, writes=[Wd])
                            for jj in range(11):
                                ja = g * 11 + jj
                                for i in range(4):
                                    C.op("pe", "matmul", reads=[Wd, A], writes=[PS[4 + i]], out=PS[4 + i][:, :nbi],
                                         lhsT=Wd[:, jj, i * 128:(i + 1) * 128], rhs=A[:, ja, :nbi],
                                         start=(ja == 0), stop=(ja == 21))
                        for i in range(4):
                            c = q * 4 + i
                            C.op("dve", "tensor_tensor", reads=[XB, PS[4 + i]], writes=[XB], out=XB[:, c, 1:1 + nbi],
                                 in0=XB[:, c, 1:1 + nbi], in1=PS[4 + i][:, :nbi], op=ALU.add)
                for tt in range(0, nbi, 128):
                    tn = min(128, nbi - tt)
                    pt = ptok[(tt // 128) % 2]
                    C.dma("sp", out=pt[:tn, :], in_=p_l[t0 + tt:t0 + tt + tn, :], sbt=pt, writes=[pt])
                    pst = PS[(tt // 128) % 2]
                    for e in range(2):
                        C.op("pe", "transpose", reads=[pt, self.ident], writes=[pst], out=pst[:, e * 128:e * 128 + tn],
                             in_=pt[:tn, e * 128:(e + 1) * 128], identity=self.ident[:tn, :tn])
                    C.op("act", "activation", reads=[pst], writes=[PT], out=PT[:, :, tt:tt + tn],
                         in_=pst[:, 0:256].rearrange("p (e t) -> p e t", e=2)[:, :, :tn], func=AF.Copy)
                self.rms_to_h(XB, H, nb, "norm_ple", l, sqb, rstd, PS[4])
                for og in range(8):
                    Wg = WPG[ipg % 2]
                    ipg += 1
                    C.dma("sp", out=Wg[:], in_=wpg[og, 0], sbt=Wg, writes=[Wg])
                    for oo in range(2):
                        oc = og * 2 + oo
                        pgate, pproj = PS[oc % 2], PS[2 + oc % 2]
                        for k in range(KC):
                            C.op("pe", "matmul", reads=[Wg, H], writes=[pgate], out=pgate[:, :nbi], lhsT=Wg[:, k, oo * 128:(oo + 1) * 128],
                                 rhs=H[:, k, 1:1 + nbi], start=(k == 0), stop=(k == KC - 1))
                        for e in range(2):
                            C.op("pe", "matmul", reads=[WPP, PT], writes=[pproj], out=pproj[:, :nbi], lhsT=WPP[:, e, oc * 128:(oc + 1) * 128],
                                 rhs=PT[:, e, :nbi], start=(e == 0), stop=(e == 1))
                        sg, g0 = sgb[oc % 2], g0b[oc % 2]
                        C.op("act", "activation", reads=[pgate], writes=[sg], out=sg[:, :nbi], in_=pgate[:, :nbi], func=AF.Sigmoid)
                        C.op("dve", "tensor_tensor", reads=[sg, pproj], writes=[g0], out=g0[:, :nbi], in0=sg[:, :nbi], in1=pproj[:, :nbi],
                             op=ALU.mult)
                        C.op("pool", "tensor_tensor", reads=[XB, g0], writes=[XB], out=XB[:, oc, 1:1 + nbi], in0=XB[:, oc, 1:1 + nbi],
                             in1=g0[:, :nbi], op=ALU.add)
                dst = xdst.rearrange("(c p) t -> p c t", p=128)[:, :, t0:t0 + nbi]
                C.dma("pool", out=dst, in_=XB[:, :, 1:1 + nbi], sbt=XB, reads=[XB])
            C.barrier()

    def epilogue(self, xT, y_out):
        C, T = self.C, self.T
        PS = self.PS
        with contextlib.ExitStack() as st:
            self.ensure_eps(st)
            XB = [C.sb(st, "XBe", [128, KC, 512], F32, dma=True) for _ in range(2)]
            Y = C.sb(st, "Ye", [128, KC, 512], F32)
            sqb = [C.sb(st, "sq", [128, 512], F32) for _ in range(4)]
            rstd = C.sb(st, "rstd", [128, 512], F32)
            otok = [C.sb(st, "otok", [128, D], F32, dma=True) for _ in range(2)]
            ib = 0
            io = 0
            for s in range(self.NS):
                for g0 in range(0, T, 512):
                    gn = min(512, T - g0)
                    X = XB[ib % 2]
                    ib += 1
                    self.load_xblock(X, xT[s], g0, gn, 0)
                    self.rms_to_h(X, None, gn, "final_norm", 0, sqb, rstd, PS[4], out_f32=Y)
                    for tt in range(0, gn, 128):
                        ot = otok[io % 2]
                        io += 1
                        for c4 in range(0, KC, 4):
                            pst = PS[(c4 // 4) % 4]
                            for c in range(c4, c4 + 4):
                                C.op("pe", "transpose", reads=[Y, self.ident], writes=[pst], out=pst[:, (c - c4) * 128:(c - c4 + 1) * 128],
                                     in_=Y[:, c, tt:tt + 128], identity=self.ident[:])
                            if (c4 // 4) % 2 == 0:
                                C.op("dve", "tensor_copy", reads=[pst], writes=[ot], out=ot[:, c4 * 128:(c4 + 4) * 128], in_=pst[:, 0:512])
                            else:
                                C.op("act", "activation", reads=[pst], writes=[ot], out=ot[:, c4 * 128:(c4 + 4) * 128], in_=pst[:, 0:512],
                                     func=AF.Copy)
                        C.dma("pool", out=y_out[s, g0 + tt:g0 + tt + 128, :], in_=ot[:], sbt=ot, reads=[ot])
            C.barrier()


def pack_small(P, inputs):
    sp = np.zeros((128, P.small_n), np.float32)

    def put(name, arr2d):
        o = P.small_off[name]
        sp[:, o:o + arr2d.shape[0]] = arr2d.T

    def chunked(a, nch):
        return np.ascontiguousarray(a).reshape(-1, 128)

    put("norm_mix", chunked(inputs["norm_mix"], KC))
    put("norm_ffn", chunked(inputs["norm_ffn"], KC))
    put("norm_ple", chunked(inputs["norm_ple"], KC))
    put("final_norm", chunked(inputs["final_norm"], KC))
    put("conv_w", chunked(inputs["ffn_conv_w"], NFC))
    put("conv_b", chunked(inputs["ffn_conv_b"], NFC))
    if "s5_d" in inputs:
        put("s5_d", chunked(inputs["s5_d"], KC))
    if "gdn_conv_w" in inputs:
        put("gdn_cw", chunked(inputs["gdn_conv_w"], 64))
    return sp


def na_tables(rpb):
    NEG = -1.0e4
    par = np.arange(2)[:, None, None, None]
    j = np.arange(64)[None, :, None, None]
    m = np.arange(16)[None, None, :, None]
    c = np.arange(64)[None, None, None, :]
    dr = 7 - m + par
    dc = j - c
    g = rpb[:, np.clip(dr + 7, 0, 14), np.clip(dc + 15, 0, 30)]
    g = np.ascontiguousarray(np.broadcast_to(g, (rpb.shape[0], 2, 64, 16, 64))).reshape(rpb.shape[0], 128, 16, 64)
    cs = np.clip(c - 8, 0, 48)
    colv = (j >= cs) & (j < cs + 16)
    v_full = colv & (dr >= -7) & (dr <= 7)
    v_int = colv & (dr >= -4) & (dr <= 3)
    mask = np.stack([np.where(v_int, 0.0, NEG), np.where(v_full, 0.0, NEG)], 0)
    mask = np.ascontiguousarray(np.broadcast_to(mask, (2, 2, 64, 16, 64))).reshape(2, 128, 16, 64).transpose(1, 0, 2, 3)
    return g.astype(np.float32), np.ascontiguousarray(mask).astype(np.float32)


def s5_tables(inputs):
    lam = np.zeros((128, 2, 3, 64), np.float32)
    Bp = np.zeros((2, 16, 128, 2, 4, 128), np.float32)
    Cp = np.zeros((2, 16, 128, 2, 4, 128), np.float32)
    for d in range(2):
        lam[:, d, 0, :] = inputs["s5_a_re"][0, d].reshape(64, 128).T
        lam[:, d, 1, :] = inputs["s5_a_im"][0, d].reshape(64, 128).T
        lam[:, d, 2, :] = np.repeat(inputs["s5_log_dt"][0, d], 64).reshape(64, 128).T
        for ri, (bn, cn) in enumerate((("s5_b_re", "s5_c_re"), ("s5_b_im", "s5_c_im"))):
            b = inputs[bn][0, d]
            c = inputs[cn][0, d]
            for fc in range(16):
                for gi in range(8):
                    g = 8 * fc + gi
                    q4, half = gi // 2, gi % 2
                    Bp[d, fc, gi * 16:(gi + 1) * 16, ri, q4, half * 64:(half + 1) * 64] = b[g].T
                    Cp[d, fc, half * 64:(half + 1) * 64, ri, q4, gi * 16:(gi + 1) * 16] = c[g].T
    return lam, Bp, Cp


def gdn_consts():
    NEG = -1.0e5
    s = np.arange(64)[:, None]
    i = np.arange(64)[None, :]
    cs = np.zeros((64, 8, 64), np.float32)
    cs[:, 0] = s <= i
    cs[:, 1] = s >= i
    cs[:, 2] = s > i
    cs[:, 3] = s < i
    cs[:, 4] = np.where(i >= s, 0.0, NEG)
    cs[:, 5] = np.where(i <= s, 0.0, NEG)
    cs[:, 6] = i > s
    cs[:, 7] = i < s
    return cs


def run_prog(P, nc, inputs, slot_x, slot_p, n_cores):
    sp = pack_small(P, inputs)
    ident = np.eye(128, dtype=np.float32)
    in_maps = []
    for c in range(n_cores):
        m = {"x_in": slot_x[c], "p_in": slot_p[c], "smallp": sp, "ident": ident}
        if "gdn_consts" in P.din:
            m["gdn_consts"] = gdn_consts()
        if "s5_lam" in P.din:
            m["s5_lam"], m["s5_B"], m["s5_C"] = s5_tables(inputs)
            m["iota128"] = np.ascontiguousarray(np.broadcast_to(np.arange(1, 129, dtype=np.float32), (128, 128)))
        if "na_g" in P.din:
            m["na_g"], m["na_mask"] = na_tables(np.asarray(inputs["na_rpb"][0]))
        for k in P.din:
            if k not in m:
                if "." in k:
                    nm, l = k.split(".")
                    m[k] = np.ascontiguousarray(inputs[nm][int(l)])
                else:
                    m[k] = np.ascontiguousarray(inputs[k])
        in_maps.append(m)
    res = run_bass_kernel_spmd(nc, in_maps, core_ids=list(range(n_cores)))
    return [r["y_out"] for r in res.results]


def kernel(**inputs):
    T, NS = 4096, 2
    P = Prog(T, NS, list(range(DEPTH)))
    nc = P.build()
    xp, xs = inputs["x_prompt"], inputs["x_sample"]
    pp, psm = inputs["p_prompt"], inputs["p_sample"]
    slot_x, slot_p = [], []
    for c in range(8):
        j = c if c < 2 else 0
        slot_x.append(np.stack([xs[c], xp[j]], 0))
        slot_p.append(np.stack([psm[:, c], pp[:, j]], 1))
    outs = run_prog(P, nc, inputs, slot_x, slot_p, 8)
    y_sample = np.stack([outs[c][0] for c in range(8)], 0)
    y_prompt = np.stack([outs[0][1], outs[1][1]], 0)
    return (y_prompt, y_sample)
```

```python
import contextlib
import os
import numpy as np
import ml_dtypes
import concourse.bass as bass
import concourse.mybir as mybir
from concourse.bass_utils import run_bass_kernel_spmd

F32 = mybir.dt.float32
BF16 = mybir.dt.bfloat16
AF = mybir.ActivationFunctionType
ALU = mybir.AluOpType
AX = mybir.AxisListType

D = 2048
KC = 16
DFF = 5632
NFC = 44
PLE = 256
DEPTH = 4
EPS = 1e-6
SAME_ENGINE_SYNC = True
GDN_DEFER = True
GDN_ACTEV = False
GDN_ZIP = True
GDN_NEUMANN_BF16 = False


class DSem:
    def __init__(self, sem):
        self.sem = sem
        self.total = 0


class Buf:
    def __init__(self, name):
        self.name = name
        self.w = None
        self.r = {}
        self.dsem = None


class TT:
    def __init__(self, t, b):
        self.t = t
        self.b = b

    def __getitem__(self, idx):
        return self.t[idx]


class Ctx:
    ENGS = ["pe", "dve", "act", "pool", "sp"]
    CENGS = ["pe", "dve", "act", "pool"]

    def __init__(self, nc, stack, n_dsem=90):
        self.nc = nc
        self.ops = {e: [] for e in self.ENGS}
        self.csem = {e: stack.enter_context(nc.semaphore("c_" + e)) for e in self.CENGS}
        self.cnt = {e: 0 for e in self.CENGS}
        self.seen = {e: {} for e in self.ENGS}
        self.dsems = [DSem(stack.enter_context(nc.semaphore("d%d" % i))) for i in range(n_dsem)]
        self.free_ds = list(self.dsems)
        self.nuid = 0

    def sb(self, stack, name, shape, dtype, dma=False):
        self.nuid += 1
        t = stack.enter_context(self.nc.sbuf_tensor("%s_%d" % (name, self.nuid), list(shape), dtype))
        b = Buf(name)
        if dma:
            b.dsem = self.free_ds.pop()
            stack.callback(self.free_ds.append, b.dsem)
        return TT(t, b)

    def ps(self, stack, name, shape, dtype):
        self.nuid += 1
        t = stack.enter_context(self.nc.psum_tensor("%s_%d" % (name, self.nuid), list(shape), dtype))
        return TT(t, Buf(name))

    def _tok_key(self, tok):
        return (tok[0], tok[1] if tok[0] == "c" else id(tok[1]))

    def _collect(self, reads, writes):
        toks = []
        for b in reads:
            if b.w is not None:
                toks.append(b.w)
        for b in writes:
            if b.w is not None:
                toks.append(b.w)
            toks.extend(b.r.values())
        return toks

    def _waits(self, eng, toks):
        res = {}
        for tok in toks:
            if tok[0] == "c":
                e2, v = tok[1], tok[2]
                if e2 == eng and (eng == "pe" or not SAME_ENGINE_SYNC):
                    continue
                sh = self.csem[e2]
            else:
                v = tok[1].total
                sh = tok[1].sem
            key = self._tok_key(tok)
            if self.seen[eng].get(key, 0) >= v:
                continue
            if key in res and res[key][1] >= v:
                continue
            res[key] = (sh, v)
        for key, (sh, v) in res.items():
            self.seen[eng][key] = v
        return list(res.values())

    def _commit(self, tok, reads, writes):
        key = self._tok_key(tok)
        for b in reads:
            b.r[key] = tok
        for b in writes:
            b.w = tok
            b.r = {}

    def op(self, eng, name, reads=(), writes=(), **kw):
        reads = [x.b if isinstance(x, TT) else x for x in reads]
        writes = [x.b if isinstance(x, TT) else x for x in writes]
        waits = self._waits(eng, self._collect(reads, writes))
        self.cnt[eng] += 1
        sem = self.csem[eng]

        def run(e, name=name, kw=kw, waits=waits, sem=sem):
            for sh, v in waits:
                e.wait_ge(sh, v)
            getattr(e, name)(**kw).then_inc(sem, 1)

        self.ops[eng].append(run)
        self._commit(("c", eng, self.cnt[eng]), reads, writes)

    def dma(self, q, out, in_, sbt, reads=(), writes=(), **kw):
        reads = [x.b if isinstance(x, TT) else x for x in reads]
        writes = [x.b if isinstance(x, TT) else x for x in writes]
        waits = self._waits(q, self._collect(reads, writes))
        ds = sbt.b.dsem if isinstance(sbt, TT) else sbt
        ds.total += 16

        def run(e, out=out, in_=in_, kw=kw, waits=waits, ds=ds):
            for sh, v in waits:
                e.wait_ge(sh, v)
            e.dma_start(out=out, in_=in_, **kw).then_inc(ds.sem, 16)

        self.ops[q].append(run)
        self._commit(("d", ds), reads, writes)

    def barrier(self):
        for e in self.ENGS:
            waits = []
            for e2 in self.CENGS:
                if e2 == e:
                    continue
                v = self.cnt[e2]
                key = ("c", e2)
                if v > self.seen[e].get(key, 0):
                    self.seen[e][key] = v
                    waits.append((self.csem[e2], v))
            for ds in self.dsems:
                key = ("d", id(ds))
                if ds.total > self.seen[e].get(key, 0):
                    self.seen[e][key] = ds.total
                    waits.append((ds.sem, ds.total))
            if waits:
                def run(eh, waits=waits):
                    for sh, v in waits:
                        eh.wait_ge(sh, v)
                self.ops[e].append(run)

    def emit(self):
        nc = self.nc
        with nc.Block() as block:
            @block.tensor
            def _(e):
                for f in self.ops["pe"]:
                    f(e)

            @block.vector
            def _(e):
                for f in self.ops["dve"]:
                    f(e)

            @block.scalar
            def _(e):
                for f in self.ops["act"]:
                    f(e)

            @block.gpsimd
            def _(e):
                for f in self.ops["pool"]:
                    f(e)

            @block.sync
            def _(e):
                for f in self.ops["sp"]:
                    f(e)


def blocks_of(T, nbi_max=456):
    nblk = -(-T // nbi_max)
    base = -(-T // nblk)
    out = []
    t = 0
    while t < T:
        n = min(base, T - t)
        out.append((t, n))
        t += n
    return out


class Prog:
    def __init__(self, T, nslot, layers, mixers=True, do_ffn=True):
        self.do_ffn = do_ffn
        self.T = T
        self.NS = nslot
        self.layers = layers
        self.mixers = mixers
        self.nc = bass.Bass("TRN2", target_bir_lowering=False)
        self.din = {}
        self.small_off = {}
        self.small_n = 0

    def inp(self, name, shape, dtype=F32):
        ap = self.nc.dram_tensor(name, list(shape), dtype, kind="ExternalInput").ap()
        self.din[name] = ap
        return ap

    def scratch(self, name, shape, dtype):
        return self.nc.dram_tensor(name, list(shape), dtype, kind="Internal").ap()

    def cast_weight(self, C, name, src2d, K, N, ct, kg=None):
        kcs = K // 128
        kg = kg or kcs
        nt, ng = N // ct, kcs // kg
        ws = self.scratch(name, [nt, ng, 128, kg, ct], BF16)
        for i in range(nt):
            for g in range(ng):
                src = src2d[g * kg * 128:(g + 1) * kg * 128, i * ct:(i + 1) * ct].rearrange("(k p) c -> p k c", p=128)
                C.dma("pool", out=ws[i, g], in_=src, sbt=self.cast_ds)
        return ws

    def build(self):
        nc, T, NS = self.nc, self.T, self.NS
        x_in = self.inp("x_in", [NS, T, D])
        p_in = self.inp("p_in", [DEPTH, NS, T, PLE])
        y_out = nc.dram_tensor("y_out", [NS, T, D], F32, kind="ExternalOutput").ap()
        w_gu = {l: self.inp("ffn_w_gu.%d" % l, [D, 2 * DFF]) for l in self.layers}
        w_dn = {l: self.inp("ffn_w_down.%d" % l, [DFF, D]) for l in self.layers}
        w_pg = {l: self.inp("ple_w_gate.%d" % l, [D, D]) for l in self.layers}
        w_pp = {l: self.inp("ple_w_proj.%d" % l, [PLE, D]) for l in self.layers}
        self.declare_mixer_inputs()
        self.small_layout()
        smallp = self.inp("smallp", [128, self.small_n])
        ident_in = self.inp("ident", [128, 128])
        self.xT = [[self.scratch("xT_%d_%d" % (a, s), [D, T], F32) for s in range(NS)] for a in range(2)]

        with contextlib.ExitStack() as gstack:
            C = Ctx(nc, gstack)
            self.C = C
            self.cast_ds = C.free_ds.pop()
            self.PS = [C.ps(gstack, "ps%d" % i, [128, 512], F32) for i in range(8)]
            self.small = C.sb(gstack, "small", [128, self.small_n], F32, dma=True)
            C.dma("sp", out=self.small[:], in_=smallp[:, :], sbt=self.small, writes=[self.small])
            self.ident = C.sb(gstack, "ident", [128, 128], F32, dma=True)
            C.dma("sp", out=self.ident[:], in_=ident_in[:, :], sbt=self.ident, writes=[self.ident])
            self.ones = C.sb(gstack, "ones", [128, 128], F32)
            C.op("dve", "memset", writes=[self.ones], ap=self.ones[:], constant=1.0)
            self.onesb = C.sb(gstack, "onesb", [128, 128], BF16)
            C.op("dve", "memset", writes=[self.onesb], ap=self.onesb[:], constant=1.0)
            self.identb = C.sb(gstack, "identb", [128, 128], BF16)
            C.op("dve", "tensor_copy", reads=[self.ident], writes=[self.identb], out=self.identb[:], in_=self.ident[:])

            self.ws = {}
            for l in self.layers:
                if self.mixers:
                    self.cast_mixer_weights(l)
                if self.do_ffn:
                    self.ws[("gu", l)] = self.cast_weight(C, "ws_gu%d" % l, w_gu[l], D, 2 * DFF, 256)
                    self.ws[("dn", l)] = self.cast_weight(C, "ws_dn%d" % l, w_dn[l], DFF, D, 512, kg=11)
                    self.ws[("pg", l)] = self.cast_weight(C, "ws_pg%d" % l, w_pg[l], D, D, 256)
                    self.ws[("pp", l)] = self.cast_weight(C, "ws_pp%d" % l, w_pp[l], PLE, D, 2048)
            C.barrier()

            self.prologue(x_in)
            cur = 0
            for l in self.layers:
                if self.mixers:
                    for s in range(NS):
                        [self.mix_na, self.mix_sg, self.mix_gdn, self.mix_s5][l % 4](l, s, self.xT[cur][s], self.xT[1 - cur][s])
                    cur = 1 - cur
                if self.do_ffn:
                    for s in range(NS):
                        self.ffn_ple(l, s, self.xT[cur][s], self.xT[1 - cur][s], p_in[l, s])
                    cur = 1 - cur
            self.epilogue(self.xT[cur], y_out)
            C.barrier()
            C.emit()
        return nc

    def kinds(self):
        return sorted(set(l % 4 for l in self.layers)) if self.mixers else []

    def declare_mixer_inputs(self):
        ks = self.kinds()
        if 0 in ks:
            self.na_w_qkv = self.inp("na_w_qkv", [1, D, 3 * D])
            self.na_w_o = self.inp("na_w_o", [1, D, D])
            self.na_g = self.inp("na_g", [16, 128, 16, 64])
            self.na_mask = self.inp("na_mask", [128, 2, 16, 64])
            self.na_qk = self.scratch("na_qk", [2 * D, self.T], BF16)
            self.na_v = self.scratch("na_v", [self.T, D], BF16)
            self.na_o = self.scratch("na_o", [D, self.T], BF16)
        if 2 in ks:
            T = self.T
            self.gdn_w_in = self.inp("gdn_w_in", [1, D, 12416])
            self.gdn_w_o = self.inp("gdn_w_o", [1, 4096, D])
            self.gdn_a_log = self.inp("gdn_a_log", [1, 2, 32])
            self.gdn_dt_bias = self.inp("gdn_dt_bias", [1, 2, 32])
            self.gdn_out_norm = self.inp("gdn_out_norm", [1, 128])
            self.gdn_consts = self.inp("gdn_consts", [64, 8, 64])
            self.gdn_qk = self.scratch("gdn_qk", [4096, T], BF16)
            self.gdn_kv = self.scratch("gdn_kv", [T, 6144], BF16)
            self.gdn_z = self.scratch("gdn_z", [T, 4096], F32)
            self.gdn_gb = self.scratch("gdn_gb", [T, 128], F32)
            self.gdn_of = self.scratch("gdn_of", [T, 4096], F32)
            self.gdn_o = self.scratch("gdn_o", [4096, T], BF16)
        if 3 in ks:
            self.s5_w_glu = self.inp("s5_w_glu", [1, D, 2 * D])
            self.s5_lam = self.inp("s5_lam", [128, 2, 3, 64])
            self.s5_B = self.inp("s5_B", [2, 16, 128, 2, 4, 128])
            self.s5_C = self.inp("s5_C", [2, 16, 128, 2, 4, 128])
            self.iota_in = self.inp("iota128", [128, 128])
            self.s5_h = self.scratch("s5_h", [D, self.T], F32)
            self.s5_y = self.scratch("s5_y", [D, self.T], BF16)
        if 1 in ks:
            self.sg_w_in = self.inp("sg_w_in", [1, D, 2 * D])
            self.sg_norm = self.inp("sg_norm", [1, D])
            self.sg_w_s = self.inp("sg_w_s", [1, 16, 128, 128])
            self.sg_b_s = self.inp("sg_b_s", [1, 16, 128])
            self.sg_w_o = self.inp("sg_w_o", [1, D, D])

    def cast_mixer_weights(self, l):
        C = self.C
        k = l % 4
        if k == 0:
            self.ws["na_qk"] = self.cast_weight(C, "ws_naqk", self.na_w_qkv[0][:, 0:2 * D], D, 2 * D, 256)
            self.ws["na_v"] = self.cast_weight(C, "ws_nav", self.na_w_qkv[0][:, 2 * D:3 * D], D, D, 512)
            self.ws["na_o"] = self.cast_weight(C, "ws_nao", self.na_w_o[0], D, D, 256)
        if k == 2:
            self.ws["gdn_qkv"] = self.cast_weight(C, "ws_gqkv", self.gdn_w_in[0][:, 0:8192], D, 8192, 256)
            self.ws["gdn_z"] = self.cast_weight(C, "ws_gz", self.gdn_w_in[0][:, 8192:12288], D, 4096, 512)
            self.ws["gdn_ab"] = self.cast_weight(C, "ws_gab", self.gdn_w_in[0][:, 12288:12416], D, 128, 128)
            self.ws["gdn_o"] = self.cast_weight(C, "ws_go", self.gdn_w_o[0], 4096, D, 256)
        if k == 3:
            self.ws["s5_glu"] = self.cast_weight(C, "ws_s5glu", self.s5_w_glu[0], D, 2 * D, 256)
        if k == 1:
            self.ws["sg_u"] = self.cast_weight(C, "ws_sgu", self.sg_w_in[0][:, 0:D], D, D, 256)
            self.ws["sg_v"] = self.cast_weight(C, "ws_sgv", self.sg_w_in[0][:, D:2 * D], D, D, 512)
            self.ws["sg_o"] = self.cast_weight(C, "ws_sgo", self.sg_w_o[0], D, D, 256)

    def gelu_tanh(self, dst, dst_ap, src, src_ap, ta, tb, n):
        C = self.C
        C.op("act", "activation", reads=[src], writes=[ta], out=ta[:, :n], in_=src_ap, func=AF.Square)
        C.op("dve", "tensor_scalar", reads=[ta], writes=[ta], out=ta[:, :n], in0=ta[:, :n], scalar1=0.044715, scalar2=1.0,
             op0=ALU.mult, op1=ALU.add)
        C.op("dve", "tensor_tensor", reads=[ta, src], writes=[ta], out=ta[:, :n], in0=ta[:, :n], in1=src_ap, op=ALU.mult)
        C.op("act", "activation", reads=[ta], writes=[tb], out=tb[:, :n], in_=ta[:, :n], func=AF.Sigmoid, scale=1.5957691216057308)
        C.op("dve", "tensor_tensor", reads=[tb, src], writes=[dst], out=dst_ap, in0=tb[:, :n], in1=src_ap, op=ALU.mult)

    def bcast_row(self, out, tmp, name, src_row_ap, n):
        C = self.C
        row = C.sb(tmp, name + "_row", [1, n], F32, dma=True)
        C.dma("sp", out=row[:], in_=src_row_ap, sbt=row, writes=[row])
        for q in range(0, n, 512):
            qn = min(512, n - q)
            ps = self.PS[(q // 512) % 4]
            C.op("pe", "matmul", reads=[row, self.ones], writes=[ps], out=ps[:, :qn], lhsT=self.ones[0:1, 0:128], rhs=row[0:1, q:q + qn],
                 start=True, stop=True)
            C.op("dve", "tensor_copy", reads=[ps], writes=[out], out=out[:, q:q + qn], in_=ps[:, :qn])
        return out

    def mix_sg(self, l, s, xsrc, xdst):
        C, T, PS = self.C, self.T, self.PS
        BT = 256
        wu, wv, wo = self.ws["sg_u"], self.ws["sg_v"], self.ws["sg_o"]
        with contextlib.ExitStack() as st:
            WST = C.sb(st, "WST", [128, 16, 128], BF16)
            BHI = C.sb(st, "BHI", [1, D], BF16)
            BLO = C.sb(st, "BLO", [1, D], BF16)
            SGN = C.sb(st, "SGN", [128, D], F32)
            with contextlib.ExitStack() as tmp:
                self.bcast_row(SGN, tmp, "SGN", self.sg_norm[0:1, :], D)
                wsl = C.sb(tmp, "wsl", [128, 16, 128], F32, dma=True)
                C.dma("sp", out=wsl[:], in_=self.sg_w_s[0].rearrange("g t s -> t g s"), sbt=wsl, writes=[wsl])
                for g4 in range(4):
                    ps = PS[g4]
                    for gi in range(4):
                        g = g4 * 4 + gi
                        C.op("pe", "transpose", reads=[wsl, self.ident], writes=[ps], out=ps[:, gi * 128:(gi + 1) * 128], in_=wsl[:, g, :],
                             identity=self.ident[:])
                    C.op("dve", "tensor_copy", reads=[ps], writes=[WST], out=WST[:, g4 * 4:(g4 + 1) * 4, :],
                         in_=ps[:, 0:512].rearrange("p (g t) -> p g t", g=4))
                brow = C.sb(tmp, "brow", [1, D], F32, dma=True)
                C.dma("sp", out=brow[:], in_=self.sg_b_s[0:1].rearrange("o g t -> o (g t)"), sbt=brow, writes=[brow])
                bt = C.sb(tmp, "btmp", [1, D], F32)
                C.op("dve", "tensor_copy", reads=[brow], writes=[BHI], out=BHI[:], in_=brow[:])
                C.op("dve", "tensor_tensor", reads=[brow, BHI], writes=[bt], out=bt[:], in0=brow[:], in1=BHI[:], op=ALU.subtract)
                C.op("dve", "tensor_copy", reads=[bt], writes=[BLO], out=BLO[:], in_=bt[:])
                C.barrier()
            self.ensure_eps(st)
            XB = C.sb(st, "XB", [128, KC, BT], F32, dma=True)
            H = C.sb(st, "H", [128, KC, BT], BF16)
            UT = C.sb(st, "UT", [128, KC, BT], F32)
            GT = C.sb(st, "GT", [128, KC, BT], BF16)
            VT = [C.sb(st, "VT", [128, D], F32) for _ in range(BT // 128)]
            VN = [C.sb(st, "VN", [128, D], BF16) for _ in range(BT // 128)]
            WU = [C.sb(st, "WU", [128, KC, 256], BF16, dma=True) for _ in range(2)]
            WV = [C.sb(st, "WV", [128, KC, 512], BF16, dma=True) for _ in range(2)]
            WO = [C.sb(st, "WO", [128, KC, 256], BF16, dma=True) for _ in range(2)]
            ta = [C.sb(st, "ta", [128, 512], F32) for _ in range(2)]
            tb = [C.sb(st, "tb", [128, 512], F32) for _ in range(2)]
            sqb = [C.sb(st, "sq", [128, 512], F32) for _ in range(4)]
            rstd = C.sb(st, "rstd", [128, 512], F32)
            ssv = C.sb(st, "ssv", [128, 2], F32)
            iu = iv = io = ig = 0
            for t0 in range(0, T, BT):
                gn = min(BT, T - t0)
                ntt = gn // 128
                self.load_xblock(XB, xsrc, t0, gn, 0)
                self.rms_to_h(XB, H, gn, "norm_mix", l, sqb, rstd, PS[4])
                for jt in range(8):
                    W = WU[iu % 2]
                    iu += 1
                    C.dma("sp", out=W[:], in_=wu[jt, 0], sbt=W, writes=[W])
                    for jj in range(2):
                        j = jt * 2 + jj
                        ps = PS[j % 2]
                        for k in range(KC):
                            C.op("pe", "matmul", reads=[W, H], writes=[ps], out=ps[:, :gn], lhsT=W[:, k, jj * 128:(jj + 1) * 128],
                                 rhs=H[:, k, :gn], start=(k == 0), stop=(k == KC - 1))
                        self.gelu_tanh(UT, UT[:, j, :gn], ps, ps[:, :gn], ta[ig % 2], tb[ig % 2], gn)
                        ig += 1
                for cg in range(4):
                    W = WV[iv % 2]
                    iv += 1
                    C.dma("sp", out=W[:], in_=wv[cg, 0], sbt=W, writes=[W])
                    for tt in range(ntt):
                        ps = PS[2 + (cg * ntt + tt) % 2]
                        for k in range(KC):
                            C.op("pe", "matmul", reads=[W, H], writes=[ps], out=ps[:, :512], lhsT=H[:, k, tt * 128:(tt + 1) * 128],
                                 rhs=W[:, k, :], start=(k == 0), stop=(k == KC - 1))
                        self.gelu_tanh(VT[tt], VT[tt][:, cg * 512:(cg + 1) * 512], ps, ps[:, :512], ta[ig % 2], tb[ig % 2], 512)
                        ig += 1
                for tt in range(ntt):
                    C.op("act", "activation", reads=[VT[tt]], writes=[VN[tt], ssv], out=VN[tt][:], in_=VT[tt][:], func=AF.Square,
                         accum_out=ssv[:, tt:tt + 1])
                    C.op("act", "activation", reads=[ssv], writes=[ssv], out=ssv[:, tt:tt + 1], in_=ssv[:, tt:tt + 1], func=AF.Sqrt,
                         scale=1.0 / D, bias=self.epsb[:, 0:1])
                    C.op("dve", "reciprocal", reads=[ssv], writes=[ssv], out=ssv[:, tt:tt + 1], in_=ssv[:, tt:tt + 1])
                    C.op("dve", "scalar_tensor_tensor", reads=[VT[tt], ssv, SGN], writes=[VN[tt]], out=VN[tt][:], in0=VT[tt][:],
                         scalar=ssv[:, tt:tt + 1], in1=SGN[:], op0=ALU.mult, op1=ALU.mult)
                for tt in range(ntt):
                    for g4 in range(4):
                        ps = PS[4 + (tt * 4 + g4) % 4]
                        for gi in range(4):
                            g = g4 * 4 + gi
                            o = ps[:, gi * 128:(gi + 1) * 128]
                            C.op("pe", "matmul", reads=[VN[tt], WST], writes=[ps], out=o, lhsT=VN[tt][:, g * 128:(g + 1) * 128],
                                 rhs=WST[:, g, :], start=True, stop=False)
                            C.op("pe", "matmul", reads=[BHI, self.onesb], writes=[ps], out=o, lhsT=self.onesb[0:1, 0:128],
                                 rhs=BHI[0:1, g * 128:(g + 1) * 128], start=False, stop=False)
                            C.op("pe", "matmul", reads=[BLO, self.onesb], writes=[ps], out=o, lhsT=self.onesb[0:1, 0:128],
                                 rhs=BLO[0:1, g * 128:(g + 1) * 128], start=False, stop=True)
                        C.op("dve", "tensor_tensor", reads=[UT, ps], writes=[GT], out=GT[:, g4 * 4:(g4 + 1) * 4, tt * 128:(tt + 1) * 128],
                             in0=UT[:, g4 * 4:(g4 + 1) * 4, tt * 128:(tt + 1) * 128],
                             in1=ps[:, 0:512].rearrange("p (g t) -> p g t", g=4), op=ALU.mult)
                for ot in range(8):
                    W = WO[io % 2]
                    io += 1
                    C.dma("sp", out=W[:], in_=wo[ot, 0], sbt=W, writes=[W])
                    for oo in range(2):
                        oc = ot * 2 + oo
                        ps = PS[oc % 2]
                        for k in range(KC):
                            C.op("pe", "matmul", reads=[W, GT], writes=[ps], out=ps[:, :gn], lhsT=W[:, k, oo * 128:(oo + 1) * 128],
                                 rhs=GT[:, k, :gn], start=(k == 0), stop=(k == KC - 1))
                        C.op("dve", "tensor_tensor", reads=[XB, ps], writes=[XB], out=XB[:, oc, :gn], in0=XB[:, oc, :gn], in1=ps[:, :gn],
                             op=ALU.add)
                dst = xdst.rearrange("(c p) t -> p c t", p=128)[:, :, t0:t0 + gn]
                C.dma("pool", out=dst, in_=XB[:, :, :gn], sbt=XB, reads=[XB])
            C.barrier()

    def out_proj(self, xsrc, xdst, act_dram, kcs, wo, BT=512):
        C, T, PS = self.C, self.T, self.PS
        with contextlib.ExitStack() as st:
            XB = [C.sb(st, "XBo", [128, KC, BT], F32, dma=True) for _ in range(2)]
            AB = [C.sb(st, "ABo", [128, kcs, BT], BF16, dma=True) for _ in range(2)]
            WO = [C.sb(st, "WOo", [128, kcs, 256], BF16, dma=True) for _ in range(2)]
            io = ib = 0
            for t0 in range(0, T, BT):
                gn = min(BT, T - t0)
                X, Ab = XB[ib % 2], AB[ib % 2]
                ib += 1
                self.load_xblock(X, xsrc, t0, gn, 0)
                C.dma("sp", out=Ab[:, :, :gn], in_=act_dram.rearrange("(c p) t -> p c t", p=128)[:, :, t0:t0 + gn], sbt=Ab, writes=[Ab])
                for ot in range(8):
                    W = WO[io % 2]
                    io += 1
                    C.dma("sp", out=W[:], in_=wo[ot, 0], sbt=W, writes=[W])
                    for oo in range(2):
                        oc = ot * 2 + oo
                        ps = PS[oc % 4]
                        for k in range(kcs):
                            C.op("pe", "matmul", reads=[W, Ab], writes=[ps], out=ps[:, :gn], lhsT=W[:, k, oo * 128:(oo + 1) * 128],
                                 rhs=Ab[:, k, :gn], start=(k == 0), stop=(k == kcs - 1))
                        C.op("dve", "tensor_tensor", reads=[X, ps], writes=[X], out=X[:, oc, :gn], in0=X[:, oc, :gn], in1=ps[:, :gn],
                             op=ALU.add)
                dst = xdst.rearrange("(c p) t -> p c t", p=128)[:, :, t0:t0 + gn]
                C.dma("pool", out=dst, in_=X[:, :, :gn], sbt=X, reads=[X])
            C.barrier()

    def mix_na(self, l, s, xsrc, xdst):
        C, T, PS = self.C, self.T, self.PS
        BT = 512
        R = T // 64
        wqk, wv, wo = self.ws["na_qk"], self.ws["na_v"], self.ws["na_o"]
        with contextlib.ExitStack() as st:
            self.ensure_eps(st)
            XB = C.sb(st, "XB", [128, KC, BT], F32, dma=True)
            H = C.sb(st, "H", [128, KC, BT], BF16)
            QK = C.sb(st, "QK", [128, 32, BT], BF16, dma=True)
            VS = [C.sb(st, "VS", [128, D], BF16, dma=True) for _ in range(BT // 128)]
            WQ = [C.sb(st, "WQ", [128, KC, 256], BF16, dma=True) for _ in range(2)]
            WV = [C.sb(st, "WV", [128, KC, 512], BF16, dma=True) for _ in range(2)]
            sqb = [C.sb(st, "sq", [128, 512], F32) for _ in range(4)]
            rstd = C.sb(st, "rstd", [128, 512], F32)
            iq = iv = ie = 0
            for t0 in range(0, T, BT):
                gn = min(BT, T - t0)
                ntt = gn // 128
                self.load_xblock(XB, xsrc, t0, gn, 0)
                self.rms_to_h(XB, H, gn, "norm_mix", l, sqb, rstd, PS[4])
                for jt in range(16):
                    W = WQ[iq % 2]
                    iq += 1
                    C.dma("sp", out=W[:], in_=wqk[jt, 0], sbt=W, writes=[W])
                    for jj in range(2):
                        j = jt * 2 + jj
                        ps = PS[j % 2]
                        for k in range(KC):
                            C.op("pe", "matmul", reads=[W, H], writes=[ps], out=ps[:, :gn], lhsT=W[:, k, jj * 128:(jj + 1) * 128],
                                 rhs=H[:, k, :gn], start=(k == 0), stop=(k == KC - 1))
                        if ie % 2 == 0:
                            C.op("dve", "tensor_copy", reads=[ps], writes=[QK], out=QK[:, j, :gn], in_=ps[:, :gn])
                        else:
                            C.op("act", "activation", reads=[ps], writes=[QK], out=QK[:, j, :gn], in_=ps[:, :gn], func=AF.Copy)
                        ie += 1
                C.dma("pool", out=self.na_qk.rearrange("(c p) t -> p c t", p=128)[:, :, t0:t0 + gn], in_=QK[:, :, :gn], sbt=QK, reads=[QK])
                for cg in range(4):
                    W = WV[iv % 2]
                    iv += 1
                    C.dma("sp", out=W[:], in_=wv[cg, 0], sbt=W, writes=[W])
                    for tt in range(ntt):
                        ps = PS[2 + (cg * ntt + tt) % 2]
                        for k in range(KC):
                            C.op("pe", "matmul", reads=[W, H], writes=[ps], out=ps[:, :512], lhsT=H[:, k, tt * 128:(tt + 1) * 128],
                                 rhs=W[:, k, :], start=(k == 0), stop=(k == KC - 1))
                        if ie % 2 == 0:
                            C.op("dve", "tensor_copy", reads=[ps], writes=[VS[tt]], out=VS[tt][:, cg * 512:(cg + 1) * 512], in_=ps[:, :512])
                        else:
                            C.op("act", "activation", reads=[ps], writes=[VS[tt]], out=VS[tt][:, cg * 512:(cg + 1) * 512], in_=ps[:, :512],
                                 func=AF.Copy)
                        ie += 1
                for tt in range(ntt):
                    C.dma("pool", out=self.na_v[t0 + tt * 128:t0 + (tt + 1) * 128, :], in_=VS[tt][:], sbt=VS[tt], reads=[VS[tt]])
            C.barrier()
        scale = 128 ** -0.5

        def win(r):
            r0 = min(max(r - 4, 0), R - 8)
            return range(r0, r0 + 8)

        with contextlib.ExitStack() as st:
            MASK = C.sb(st, "MASK", [128, 2, 16, 64], F32, dma=True)
            C.dma("sp", out=MASK[:], in_=self.na_mask[:, :, :, :], sbt=MASK, writes=[MASK])
            QH = [C.sb(st, "QH", [128, T], BF16, dma=True) for _ in range(2)]
            KH = [C.sb(st, "KH", [128, T], BF16, dma=True) for _ in range(2)]
            VH = [C.sb(st, "VH", [128, T // 128, 128], BF16, dma=True) for _ in range(2)]
            GH = [C.sb(st, "GH", [128, 16, 64], F32, dma=True) for _ in range(2)]
            TBL = [C.sb(st, "TBL", [128, 2, 16, 64], F32) for _ in range(2)]
            OTH = [C.sb(st, "OTH", [128, T], BF16, dma=True) for _ in range(2)]
            sc = [C.sb(st, "sc", [128, 256], F32) for _ in range(3)]
            PT = [C.sb(st, "PT", [128, 256], BF16) for _ in range(3)]
            rden = [C.sb(st, "rden", [128, 256], F32) for _ in range(2)]
            it = 0
            for hd in range(16):
                qh, kh, vh, gh, tbl, oth = QH[hd % 2], KH[hd % 2], VH[hd % 2], GH[hd % 2], TBL[hd % 2], OTH[hd % 2]
                C.dma("sp", out=qh[:], in_=self.na_qk[hd * 128:(hd + 1) * 128, :], sbt=qh, writes=[qh])
                C.dma("sp", out=kh[:], in_=self.na_qk[D + hd * 128:D + (hd + 1) * 128, :], sbt=kh, writes=[kh])
                C.dma("sp", out=vh[:], in_=self.na_v[:, hd * 128:(hd + 1) * 128].rearrange("(n p) d -> p n d", p=128), sbt=vh, writes=[vh])
                C.dma("sp", out=gh[:], in_=self.na_g[hd], sbt=gh, writes=[gh])
                for kd in range(2):
                    C.op("pool", "tensor_tensor", reads=[gh, MASK], writes=[tbl], out=tbl[:, kd], in0=gh[:], in1=MASK[:, kd], op=ALU.add)
                for gq in range(R // 4):
                    rows = list(range(4 * gq, 4 * gq + 4))
                    chunks = sorted(set(a // 2 for r in rows for a in win(r)))
                    kind = [1 if (r < 4 or r >= R - 3) else 0 for r in rows]
                    runs = []
                    for ri, r in enumerate(rows):
                        if runs and runs[-1][2] == kind[ri]:
                            runs[-1][1] += 1
                        else:
                            runs.append([ri, 1, kind[ri]])
                    po, pd = PS[4 + gq % 2], PS[6 + gq % 2]
                    for ci, i in enumerate(chunks):
                        pss = PS[it % 4]
                        scb, ptb = sc[it % 3], PT[it % 3]
                        it += 1
                        C.op("pe", "matmul", reads=[kh, qh], writes=[pss], out=pss[:, :256], lhsT=kh[:, i * 128:(i + 1) * 128],
                             rhs=qh[:, gq * 256:(gq + 1) * 256], start=True, stop=True)
                        for (ri, nr, kd) in runs:
                            m0 = rows[ri] - 2 * i + 7
                            assert 0 <= m0 and m0 + nr <= 16, (m0, nr)
                            C.op("dve", "scalar_tensor_tensor", reads=[pss, tbl], writes=[scb], out=scb[:, ri * 64:(ri + nr) * 64],
                                 in0=pss[:, ri * 64:(ri + nr) * 64], scalar=scale,
                                 in1=tbl[:, kd, m0:m0 + nr, :].rearrange("p m c -> p (m c)"), op0=ALU.mult, op1=ALU.add)
                        C.op("act", "activation", reads=[scb], writes=[ptb], out=ptb[:], in_=scb[:], func=AF.Exp)
                        C.op("pe", "matmul", reads=[vh, ptb], writes=[po], out=po[:, :256], lhsT=vh[:, i, :], rhs=ptb[:],
                             start=(ci == 0), stop=(ci == len(chunks) - 1))
                        C.op("pe", "matmul", reads=[self.onesb, ptb], writes=[pd], out=pd[:, :256], lhsT=self.onesb[:], rhs=ptb[:],
                             start=(ci == 0), stop=(ci == len(chunks) - 1))
                    rd = rden[gq % 2]
                    C.op("dve", "reciprocal", reads=[pd], writes=[rd], out=rd[:], in_=pd[:, :256])
                    C.op("dve", "tensor_tensor", reads=[po, rd], writes=[oth], out=oth[:, gq * 256:(gq + 1) * 256], in0=po[:, :256], in1=rd[:],
                         op=ALU.mult)
                C.dma("pool", out=self.na_o[hd * 128:(hd + 1) * 128, :], in_=oth[:], sbt=oth, reads=[oth])
            C.barrier()
        self.out_proj(xsrc, xdst, self.na_o, KC, wo)

    def mix_gdn(self, l, s, xsrc, xdst):
        C, T, PS = self.C, self.T, self.PS
        BTG = 256
        L = 64
        NCK = T // L
        wqkv, wz, wab, wo = self.ws["gdn_qkv"], self.ws["gdn_z"], self.ws["gdn_ab"], self.ws["gdn_o"]

        def bc(ap, axis, shape):
            return ap.unsqueeze(axis).to_broadcast(list(shape))

        with contextlib.ExitStack() as st:
            DTB = C.sb(st, "DTB", [128, 64], F32)
            NRATE = C.sb(st, "NRATE", [128, 64], F32)
            with contextlib.ExitStack() as tmp:
                self.bcast_row(DTB, tmp, "dtb", self.gdn_dt_bias[0:1].rearrange("o d h -> o (d h)"), 64)
                self.bcast_row(NRATE, tmp, "alog", self.gdn_a_log[0:1].rearrange("o d h -> o (d h)"), 64)
                C.op("act", "activation", reads=[NRATE], writes=[NRATE], out=NRATE[:], in_=NRATE[:], func=AF.Exp)
                C.op("dve", "tensor_scalar", reads=[NRATE], writes=[NRATE], out=NRATE[:], in0=NRATE[:], scalar1=-1.0, scalar2=None, op0=ALU.mult)
                C.barrier()
            self.ensure_eps(st)
            oneb = C.sb(st, "oneb", [128, 1], F32)
            C.op("dve", "memset", writes=[oneb], ap=oneb[:], constant=1.0)
            XB = C.sb(st, "XB", [128, KC, BTG + 4], F32, dma=True)
            H = C.sb(st, "H", [128, KC, BTG + 4], BF16)
            WQ = [C.sb(st, "WQ", [128, KC, 256], BF16, dma=True) for _ in range(2)]
            WZ = [C.sb(st, "WZ", [128, KC, 512], BF16, dma=True) for _ in range(2)]
            WAB = C.sb(st, "WAB", [128, KC, 128], BF16, dma=True)
            C.dma("sp", out=WAB[:], in_=wab[0, 0], sbt=WAB, writes=[WAB])
            QKst = C.sb(st, "QKst", [128, 32, BTG], BF16, dma=True)
            KVt = [C.sb(st, "KVt", [128, 6144], BF16, dma=True) for _ in range(2)]
            ZS = [C.sb(st, "ZS", [128, 4096], F32, dma=True) for _ in range(2)]
            GBs = [C.sb(st, "GBs", [128, 128], F32, dma=True) for _ in range(2)]
            g0b = [C.sb(st, "g0", [128, BTG], F32) for _ in range(2)]
            g1b = [C.sb(st, "g1", [128, BTG], F32) for _ in range(2)]
            vlb = [C.sb(st, "vl", [128, BTG], F32) for _ in range(2)]
            sqb2 = [C.sb(st, "sq2", [128, BTG], F32) for _ in range(2)]
            rnb = [C.sb(st, "rn", [128, BTG], F32) for _ in range(2)]
            sqb = [C.sb(st, "sq", [128, 512], F32) for _ in range(4)]
            rstd = C.sb(st, "rstd", [128, 512], F32)
            tab = [C.sb(st, "tab", [128, 64], F32) for _ in range(2)]
            iq = iz = 0
            for t0 in range(0, T, BTG):
                nbi = min(BTG, T - t0)
                nb = nbi + 2
                ntt = nbi // 128
                self.load_xblock(XB, xsrc, t0, nbi, 1)
                self.rms_to_h(XB, H, nb, "norm_mix", l, sqb, rstd, PS[4])
                pend_back = None
                for jt in range(32):
                    W = WQ[iq % 2]
                    iq += 1
                    C.dma("sp", out=W[:], in_=wqkv[jt, 0], sbt=W, writes=[W])
                    for jj in range(2):
                        j = jt * 2 + jj
                        ps = PS[j % 2]
                        for k in range(KC):
                            C.op("pe", "matmul", reads=[W, H], writes=[ps], out=ps[:, :nb], lhsT=W[:, k, jj * 128:(jj + 1) * 128],
                                 rhs=H[:, k, :nb], start=(k == 0), stop=(k == KC - 1))
                        g0, g1, vl = g0b[j % 2], g1b[j % 2], vlb[j % 2]
                        cw = lambda kk, j=j: self.sm("gdn_cw", kk * 64 + j)
                        C.op("act", "activation", reads=[ps, self.small], writes=[g0], out=g0[:, :nbi], in_=ps[:, 1:1 + nbi], func=AF.Identity,
                             scale=cw(1))
                        C.op("dve", "scalar_tensor_tensor", reads=[ps, g0, self.small], writes=[g1], out=g1[:, :nbi], in0=ps[:, 0:nbi],
                             scalar=cw(0), in1=g0[:, :nbi], op0=ALU.mult, op1=ALU.add)
                        C.op("dve", "scalar_tensor_tensor", reads=[ps, g1, self.small], writes=[g0], out=g0[:, :nbi], in0=ps[:, 2:2 + nbi],
                             scalar=cw(2), in1=g1[:, :nbi], op0=ALU.mult, op1=ALU.add)
                        C.op("act", "activation", reads=[g0], writes=[vl], out=vl[:, :nbi], in_=g0[:, :nbi], func=AF.Silu)
                        if j < 32:
                            sq = sqb2[j % 2]
                            C.op("pool", "tensor_tensor", reads=[vl], writes=[sq], out=sq[:, :nbi], in0=vl[:, :nbi], in1=vl[:, :nbi], op=ALU.mult)

                        def back(j=j, vl=vl, nbi=nbi, ntt=ntt):
                            if j < 32:
                                sq, rn = sqb2[j % 2], rnb[j % 2]
                                pn = PS[4 + j % 2]
                                C.op("pe", "matmul", reads=[sq, self.ones], writes=[pn], out=pn[:, :nbi], lhsT=self.ones[:], rhs=sq[:, :nbi],
                                     start=True, stop=True)
                                C.op("act", "activation", reads=[pn], writes=[rn], out=rn[:, :nbi], in_=pn[:, :nbi], func=AF.Sqrt, scale=1.0,
                                     bias=self.epsb[:, 0:1])
                                C.op("dve", "reciprocal", reads=[rn], writes=[rn], out=rn[:, :nbi], in_=rn[:, :nbi])
                                if j < 16:
                                    C.op("dve", "scalar_tensor_tensor", reads=[vl, rn], writes=[QKst], out=QKst[:, j, :nbi], in0=vl[:, :nbi],
                                         scalar=128 ** -0.5, in1=rn[:, :nbi], op0=ALU.mult, op1=ALU.mult)
                                else:
                                    C.op("dve", "tensor_tensor", reads=[vl, rn], writes=[vl], out=vl[:, :nbi], in0=vl[:, :nbi], in1=rn[:, :nbi],
                                         op=ALU.mult)
                                    C.op("act", "activation", reads=[vl], writes=[QKst], out=QKst[:, j, :nbi], in_=vl[:, :nbi], func=AF.Copy)
                            if j >= 16:
                                for tt in range(ntt):
                                    pt = PS[6 + tt % 2]
                                    C.op("pe", "transpose", reads=[vl, self.ident], writes=[pt], out=pt[:, 0:128], in_=vl[:, tt * 128:(tt + 1) * 128],
                                         identity=self.ident[:])
                                    C.op("act" if tt % 2 == 0 else "dve", "activation" if tt % 2 == 0 else "tensor_copy", reads=[pt], writes=[KVt[tt]],
                                         out=KVt[tt][:, (j - 16) * 128:(j - 15) * 128], in_=pt[:, 0:128], **({"func": AF.Copy} if tt % 2 == 0 else {}))

                        if not GDN_DEFER:
                            back()
                            continue
                        if pend_back is not None:
                            pend_back()
                        pend_back = back
                if pend_back is not None:
                    pend_back()
                    pend_back = None
                C.dma("pool", out=self.gdn_qk.rearrange("(c p) t -> p c t", p=128)[:, :, t0:t0 + nbi], in_=QKst[:, :, :nbi], sbt=QKst, reads=[QKst])
                for tt in range(ntt):
                    C.dma("pool", out=self.gdn_kv[t0 + tt * 128:t0 + (tt + 1) * 128, :], in_=KVt[tt][:], sbt=KVt[tt], reads=[KVt[tt]])
                for cg in range(8):
                    W = WZ[iz % 2]
                    iz += 1
                    C.dma("sp", out=W[:], in_=wz[cg, 0], sbt=W, writes=[W])
                    for tt in range(ntt):
                        ps = PS[2 + (cg * ntt + tt) % 2]
                        for k in range(KC):
                            C.op("pe", "matmul", reads=[W, H], writes=[ps], out=ps[:, :512], lhsT=H[:, k, 1 + tt * 128:1 + (tt + 1) * 128],
                                 rhs=W[:, k, :], start=(k == 0), stop=(k == KC - 1))
                        C.op("act", "activation", reads=[ps], writes=[ZS[tt]], out=ZS[tt][:, cg * 512:(cg + 1) * 512], in_=ps[:, :512], func=AF.Silu)
                for tt in range(ntt):
                    C.dma("pool", out=self.gdn_z[t0 + tt * 128:t0 + (tt + 1) * 128, :], in_=ZS[tt][:], sbt=ZS[tt], reads=[ZS[tt]])
                for tt in range(ntt):
                    ps = PS[tt % 2]
                    for k in range(KC):
                        C.op("pe", "matmul", reads=[WAB, H], writes=[ps], out=ps[:, :128], lhsT=H[:, k, 1 + tt * 128:1 + (tt + 1) * 128],
                             rhs=WAB[:, k, :], start=(k == 0), stop=(k == KC - 1))
                    pv = ps[:, 0:128].rearrange("p (d w h) -> p d w h", d=2, w=2)
                    ta_, gb = tab[tt % 2], GBs[tt % 2]
                    C.op("dve", "tensor_tensor", reads=[ps, DTB], writes=[ta_], out=ta_[:].rearrange("p (d h) -> p d h", d=2), in0=pv[:, :, 0, :],
                         in1=DTB[:].rearrange("p (d h) -> p d h", d=2), op=ALU.add)
                    C.op("act", "activation", reads=[ta_], writes=[ta_], out=ta_[:], in_=ta_[:], func=AF.Exp)
                    C.op("act", "activation", reads=[ta_, oneb], writes=[ta_], out=ta_[:], in_=ta_[:], func=AF.Ln, bias=oneb[:, 0:1], scale=1.0)
                    C.op("dve", "tensor_tensor", reads=[ta_, NRATE], writes=[gb], out=gb[:, 0:64], in0=ta_[:], in1=NRATE[:], op=ALU.mult)
                    C.op("act", "activation", reads=[ps], writes=[gb], out=gb[:, 64:128].rearrange("p (d h) -> p d h", d=2), in_=pv[:, :, 1, :],
                         func=AF.Sigmoid)
                    C.dma("pool", out=self.gdn_gb[t0 + tt * 128:t0 + (tt + 1) * 128, :], in_=gb[:], sbt=gb, reads=[gb])
            C.barrier()

        dbg = int(os.environ.get("GDN_DBG", "9"))
        with contextlib.ExitStack() as st:
            self.ensure_eps(st)
            ONORM = C.sb(st, "ONORM", [128, 128], F32)
            with contextlib.ExitStack() as tmp:
                self.bcast_row(ONORM, tmp, "onorm", self.gdn_out_norm[0:1, :], 128)
                C.barrier()
            CONS = C.sb(st, "CONS", [64, 8, 64], F32, dma=True)
            C.dma("sp", out=CONS[:], in_=self.gdn_consts[:, :, :], sbt=CONS, writes=[CONS])
            S = C.sb(st, "S", [128, 4096], F32)
            Sb = C.sb(st, "Sb", [128, 4096], BF16)
            QKB = [C.sb(st, "QKB", [128, 32, 256], BF16, dma=True) for _ in range(2)]
            KVc = [C.sb(st, "KVc", [64, 6144], BF16, dma=True) for _ in range(2)]
            GBc = [C.sb(st, "GBc", [64, 128], F32, dma=True) for _ in range(2)]
            OC = C.sb(st, "OC", [64, 4096], F32, dma=True)
            OF = C.sb(st, "OF", [64, 2048], F32, dma=True)
            ZC = C.sb(st, "ZC", [64, 2048], F32, dma=True)
            OTst = C.sb(st, "OTst", [128, 32, 128], BF16, dma=True)
            EG = C.sb(st, "EG", [128, 96], F32)
            GAM = C.sb(st, "GAM", [64, 32], F32)
            NEGEG = C.sb(st, "NEGEG", [64, 32], F32)
            NEGB = C.sb(st, "NEGB", [64, 32], F32)
            RS = C.sb(st, "RS", [64, 32], F32)
            W2 = lambda nm, dt=F32: [C.sb(st, nm, [64, 8, 64], dt) for _ in range(2)]
            GTRI, DT, PTb = W2("GTRI"), W2("DT"), W2("PT", BF16)
            CDT = BF16 if GDN_NEUMANN_BF16 else F32
            CK = [W2("CKa", CDT), W2("CKb", CDT)]
            CKT = [W2("CKTa", CDT), W2("CKTb", CDT)]
            RR = [W2("RRa", CDT), W2("RRb", CDT)]
            W4 = lambda nm, dt=F32, p=64: [C.sb(st, nm, [p, 512], dt) for _ in range(2)]
            TKS, VP, UB, O1, KD = W4("TKS"), W4("VP", CDT), W4("UB", BF16), W4("O1"), W4("KD", BF16)
            TS = W4("TS", F32, 128)
            ident64 = self.ident[0:64, 0:64]
            ones64 = self.ones[0:64, 0:64]
            idn_t = self.identb if GDN_NEUMANN_BF16 else self.ident
            idn64 = idn_t[0:64, 0:64]
            iblk = 0
            ig = 0
            for dr in range(2):
                U, SUF, MNEG, ST01 = CONS[:, dr, :], CONS[:, 2 + dr, :], CONS[:, 4 + dr, :], CONS[:, 6 + dr, :]
                C.op("dve", "memset", writes=[S], ap=S[:], constant=0.0)
                C.op("pool", "memset", writes=[Sb], ap=Sb[:], constant=0.0)
                qkb = None
                for ci in range(NCK if dbg >= 1 else 0):
                    c = ci if dr == 0 else NCK - 1 - ci
                    if qkb is None or (c % 4 == (0 if dr == 0 else 3)):
                        qkb = QKB[iblk % 2]
                        iblk += 1
                        b0 = (c // 4) * 256
                        C.dma("sp", out=qkb[:], in_=self.gdn_qk.rearrange("(c p) t -> p c t", p=128)[:, :, b0:b0 + 256], sbt=qkb, writes=[qkb])
                    cc = slice((c % 4) * 64, (c % 4) * 64 + 64)
                    kv, gbc = KVc[ci % 2], GBc[ci % 2]
                    C.dma("sp", out=kv[:], in_=self.gdn_kv[c * 64:(c + 1) * 64, :], sbt=kv, writes=[kv])
                    C.dma("sp", out=gbc[:], in_=self.gdn_gb[c * 64:(c + 1) * 64, :], sbt=gbc, writes=[gbc])
                    g = gbc[:, dr * 32:(dr + 1) * 32]
                    beta = gbc[:, 64 + dr * 32:64 + (dr + 1) * 32]
                    p0 = PS[0]
                    C.op("pe", "matmul", reads=[CONS, gbc], writes=[p0], out=p0[0:64, 0:32], lhsT=U, rhs=g, start=True, stop=True)
                    C.op("pe", "matmul", reads=[CONS, gbc], writes=[p0], out=p0[0:64, 32:64], lhsT=SUF, rhs=g, start=True, stop=True)
                    C.op("pe", "matmul", reads=[self.ones, gbc], writes=[p0], out=p0[:, 64:96], lhsT=self.ones[0:64, :], rhs=g, start=True, stop=True)
                    C.op("act", "activation", reads=[p0], writes=[EG], out=EG[0:64, 0:64], in_=p0[0:64, 0:64], func=AF.Exp)
                    C.op("act", "activation", reads=[p0], writes=[EG], out=EG[:, 64:96], in_=p0[:, 64:96], func=AF.Exp)
                    C.op("dve", "tensor_copy", reads=[p0], writes=[GAM], out=GAM[:], in_=p0[0:64, 0:32])
                    C.op("dve", "tensor_scalar", reads=[EG], writes=[NEGEG], out=NEGEG[:], in0=EG[0:64, 0:32], scalar1=-1.0, scalar2=None, op0=ALU.mult)
                    C.op("dve", "tensor_scalar", reads=[gbc], writes=[NEGB], out=NEGB[:], in0=beta, scalar1=-1.0, scalar2=None, op0=ALU.mult)
                    v8 = lambda p: p[0:64, 0:512].rearrange("p (h i) -> p h i", h=8)
                    r4 = lambda t: t[:].rearrange("p (a r) i -> p a r i", r=2)
                    rfin = {}

                    def sn(hg, i2, pk, pA, pB, pCc, kv=kv, gbc=gbc, qkb=qkb, cc=cc, g=g):
                        h0, qk0 = 8 * hg, 4 * hg
                        p1 = pk
                        for a in range(4):
                            kT = qkb[:, 16 + qk0 + a, cc]
                            C.op("pe", "matmul", reads=[qkb], writes=[p1], out=p1[0:64, a * 64:(a + 1) * 64], lhsT=kT, rhs=kT, start=True, stop=True)
                        for a in range(4):
                            C.op("pe", "matmul", reads=[qkb], writes=[p1], out=p1[0:64, 256 + a * 64:256 + (a + 1) * 64], lhsT=qkb[:, 16 + qk0 + a, cc],
                                 rhs=qkb[:, qk0 + a, cc], start=True, stop=True)
                        gtri, dt_, es, bb, ct0, ptb = GTRI[i2], DT[i2], GTRI[i2], CK[0][i2], CKT[0][i2], PTb[i2]
                        C.op("dve", "tensor_tensor", reads=[CONS, gbc], writes=[gtri], out=gtri[:], in0=bc(U, 1, [64, 8, 64]),
                             in1=bc(g[:, h0:h0 + 8], 2, [64, 8, 64]), op=ALU.mult)
                        p2 = pA
                        for hh in range(8):
                            C.op("pe", "matmul", reads=[self.ones, gtri], writes=[p2], out=p2[0:64, hh * 64:(hh + 1) * 64], lhsT=ones64, rhs=gtri[:, hh, :],
                                 start=True, stop=True)
                        yield
                        C.op("dve", "tensor_tensor", reads=[p2, GAM], writes=[dt_], out=dt_[:], in0=v8(p2), in1=bc(GAM[:, h0:h0 + 8], 2, [64, 8, 64]),
                             op=ALU.subtract)
                        C.op("pool", "tensor_tensor", reads=[dt_, CONS], writes=[dt_], out=dt_[:], in0=dt_[:], in1=bc(MNEG, 1, [64, 8, 64]), op=ALU.add)
                        C.op("act", "activation", reads=[dt_], writes=[dt_], out=dt_[:], in_=dt_[:], func=AF.Exp)
                        C.op("pool", "tensor_tensor", reads=[dt_, CONS], writes=[es], out=es[:], in0=dt_[:], in1=bc(ST01, 1, [64, 8, 64]), op=ALU.mult)
                        kkv = p1[0:64, 0:256].rearrange("p (a i) -> p a i", a=4).unsqueeze(2).to_broadcast([64, 4, 2, 64])
                        kqv = p1[0:64, 256:512].rearrange("p (a i) -> p a i", a=4).unsqueeze(2).to_broadcast([64, 4, 2, 64])
                        C.op("dve", "tensor_tensor", reads=[p1, es], writes=[bb], out=r4(bb), in0=kkv, in1=r4(es), op=ALU.mult)
                        C.op("dve", "tensor_tensor", reads=[bb, NEGB], writes=[bb], out=bb[:], in0=bb[:], in1=bc(NEGB[:, h0:h0 + 8], 2, [64, 8, 64]),
                             op=ALU.mult)
                        C.op("dve", "tensor_tensor", reads=[p1, dt_], writes=[ptb], out=r4(ptb), in0=kqv, in1=r4(dt_), op=ALU.mult)
                        p3 = pB
                        for hh in range(8):
                            C.op("pe", "matmul", reads=[bb, idn_t], writes=[p3], out=p3[0:64, hh * 64:(hh + 1) * 64], lhsT=bb[:, hh, :],
                                 rhs=idn64, start=True, stop=True)
                        if GDN_ACTEV:
                            C.op("act", "activation", reads=[p3], writes=[ct0], out=ct0[:], in_=v8(p3), func=AF.Copy)
                        else:
                            C.op("dve", "tensor_copy", reads=[p3], writes=[ct0], out=ct0[:], in_=v8(p3))
                        ck, ckt = bb, ct0
                        rr = RR[0][i2]
                        C.op("pool", "tensor_tensor", reads=[bb, self.ident], writes=[rr], out=rr[:], in0=bb[:], in1=bc(ident64, 1, [64, 8, 64]), op=ALU.add)
                        yield
                        for kk in range(1, 6):
                            cn, cnt, rn_ = CK[kk % 2][i2], CKT[kk % 2][i2], RR[kk % 2][i2]
                            pa, pb, pc = pA, pB, pCc
                            for hh in range(8):
                                C.op("pe", "matmul", reads=[ck, ckt], writes=[pa], out=pa[0:64, hh * 64:(hh + 1) * 64], lhsT=ck[:, hh, :], rhs=ckt[:, hh, :],
                                     start=True, stop=True)
                            if GDN_ACTEV:
                                C.op("act", "activation", reads=[pa], writes=[cnt], out=cnt[:], in_=v8(pa), func=AF.Copy)
                            else:
                                C.op("dve", "tensor_copy", reads=[pa], writes=[cnt], out=cnt[:], in_=v8(pa))
                            if kk < 5:
                                for hh in range(8):
                                    C.op("pe", "matmul", reads=[ck, ckt], writes=[pb], out=pb[0:64, hh * 64:(hh + 1) * 64], lhsT=ckt[:, hh, :],
                                         rhs=ck[:, hh, :], start=True, stop=True)
                                C.op("dve", "tensor_copy", reads=[pb], writes=[cn], out=cn[:], in_=v8(pb))
                            yield
                            for hh in range(8):
                                C.op("pe", "matmul", reads=[cnt, rr], writes=[pc], out=pc[0:64, hh * 64:(hh + 1) * 64], lhsT=cnt[:, hh, :], rhs=rr[:, hh, :],
                                     start=True, stop=True)
                            C.op("dve", "tensor_tensor", reads=[pc, rr], writes=[rn_], out=rn_[:], in0=rr[:], in1=v8(pc), op=ALU.add)
                            yield
                            ck, ckt, rr = cn, cnt, rn_
                        rfin[hg] = rr

                    def val(hg, i2, kv=kv, gbc=gbc, qkb=qkb, cc=cc, beta=beta):
                        h0 = 8 * hg
                        rr, ptb = rfin[hg], PTb[i2]
                        for hv in range(2):
                            hd0 = h0 + 4 * hv
                            tks, vp, ub, o1, kd, ts = TKS[hv], VP[hv], UB[hv], O1[hv], KD[hv], TS[hv]
                            v4 = lambda p, np_=64: p[0:np_, 0:512].rearrange("p (a d) -> p a d", a=4)
                            p5, p6, p7 = PS[5], PS[6], PS[7]
                            for a in range(4):
                                hd = hd0 + a
                                C.op("pe", "matmul", reads=[qkb, Sb], writes=[p5], out=p5[0:64, a * 128:(a + 1) * 128], lhsT=qkb[:, 16 + hd // 2, cc],
                                     rhs=Sb[:, hd * 128:(hd + 1) * 128], start=True, stop=True)
                            for a in range(4):
                                hd = hd0 + a
                                C.op("pe", "matmul", reads=[qkb, Sb], writes=[p7], out=p7[0:64, a * 128:(a + 1) * 128], lhsT=qkb[:, hd // 2, cc],
                                     rhs=Sb[:, hd * 128:(hd + 1) * 128], start=True, stop=True)
                            C.op("dve", "tensor_tensor", reads=[p5, NEGEG], writes=[tks], out=v4(tks), in0=v4(p5), in1=bc(NEGEG[:, hd0:hd0 + 4], 2, [64, 4, 128]),
                                 op=ALU.mult)
                            C.op("pool", "tensor_tensor", reads=[tks, kv], writes=[vp], out=vp[:], in0=tks[:], in1=kv[:, 2048 + hd0 * 128:2048 + (hd0 + 4) * 128],
                                 op=ALU.add)
                            C.op("dve", "tensor_tensor", reads=[p7, EG], writes=[o1], out=v4(o1), in0=v4(p7), in1=bc(EG[0:64, hd0:hd0 + 4], 2, [64, 4, 128]),
                                 op=ALU.mult)
                            for a in range(4):
                                C.op("pe", "matmul", reads=[rr, vp], writes=[p6], out=p6[0:64, a * 128:(a + 1) * 128], lhsT=rr[:, 4 * hv + a, :],
                                     rhs=vp[:, a * 128:(a + 1) * 128], start=True, stop=True)
                            C.op("dve", "tensor_tensor", reads=[p6, gbc], writes=[ub], out=v4(ub), in0=v4(p6), in1=bc(beta[:, hd0:hd0 + 4], 2, [64, 4, 128]),
                                 op=ALU.mult)
                            for a in range(4):
                                C.op("pe", "matmul", reads=[ptb, ub], writes=[p5], out=p5[0:64, a * 128:(a + 1) * 128], lhsT=ptb[:, 4 * hv + a, :],
                                     rhs=ub[:, a * 128:(a + 1) * 128], start=True, stop=True)
                            C.op("dve", "tensor_tensor", reads=[p5, o1], writes=[OC], out=OC[:, hd0 * 128:(hd0 + 4) * 128], in0=o1[:], in1=p5[0:64, 0:512],
                                 op=ALU.add)
                            kq0 = hd0 // 2
                            ktok = kv[:, kq0 * 128:(kq0 + 2) * 128].rearrange("p (a d) -> p a d", a=2).unsqueeze(2).to_broadcast([64, 2, 2, 128])
                            C.op("dve", "tensor_tensor", reads=[kv, EG], writes=[kd], out=kd[:].rearrange("p (a r d) -> p a r d", a=2, r=2), in0=ktok,
                                 in1=bc(EG[0:64, 32 + hd0:32 + hd0 + 4], 2, [64, 4, 128]).rearrange("p (a r) d -> p a r d", a=2), op=ALU.mult)
                            for a in range(4):
                                C.op("pe", "matmul", reads=[kd, ub], writes=[p6], out=p6[:, a * 128:(a + 1) * 128], lhsT=kd[:, a * 128:(a + 1) * 128],
                                     rhs=ub[:, a * 128:(a + 1) * 128], start=True, stop=True)
                            ssl = S[:, hd0 * 128:(hd0 + 4) * 128]
                            C.op("pool", "tensor_tensor", reads=[S, EG], writes=[ts], out=v4(ts, 128), in0=ssl.rearrange("p (a d) -> p a d", a=4),
                                 in1=bc(EG[:, 64 + hd0:64 + hd0 + 4], 2, [128, 4, 128]), op=ALU.mult)
                            C.op("dve", "tensor_tensor", reads=[ts, p6], writes=[S], out=ssl, in0=ts[:], in1=p6[:, 0:512], op=ALU.add)
                            C.op("pool", "tensor_copy", reads=[S], writes=[Sb], out=Sb[:, hd0 * 128:(hd0 + 4) * 128], in_=ssl)

                    for hp in range(2):
                        hgA, hgB = 2 * hp, 2 * hp + 1
                        gens = [sn(hgA, 0, PS[1], PS[2], PS[3], PS[4]), sn(hgB, 1, PS[0], PS[5], PS[6], PS[7])]
                        if not GDN_ZIP:
                            for gn_ in gens:
                                for _ in gn_:
                                    pass
                            gens = []
                        while gens:
                            for gn_ in list(gens):
                                try:
                                    next(gn_)
                                except StopIteration:
                                    gens.remove(gn_)
                        val(hgA, 0)
                        val(hgB, 1)
                    if dbg < 6:
                        continue
                    if dr == 0:
                        C.dma("pool", out=self.gdn_of[c * 64:(c + 1) * 64, :], in_=OC[:], sbt=OC, reads=[OC])
                    else:
                        for hf in range(2):
                            cs_ = slice(hf * 2048, (hf + 1) * 2048)
                            C.dma("sp", out=OF[:], in_=self.gdn_of[c * 64:(c + 1) * 64, cs_], sbt=OF, writes=[OF])
                            C.dma("sp", out=ZC[:], in_=self.gdn_z[c * 64:(c + 1) * 64, cs_], sbt=ZC, writes=[ZC])
                            och = OC[:, cs_]
                            o3 = och.rearrange("p (h d) -> p h d", h=16)
                            rs = RS[:, hf * 16:(hf + 1) * 16]
                            C.op("pool", "tensor_tensor", reads=[OC, OF], writes=[OC], out=och, in0=och, in1=OF[:], op=ALU.add)
                            C.op("dve", "tensor_tensor", reads=[OC], writes=[OF], out=OF[:], in0=och, in1=och, op=ALU.mult)
                            C.op("dve", "tensor_reduce", reads=[OF], writes=[RS], out=rs, in_=OF[:].rearrange("p (h d) -> p h d", h=16), axis=AX.X, op=ALU.add)
                            C.op("act", "activation", reads=[RS], writes=[RS], out=rs, in_=rs, func=AF.Sqrt, scale=1.0 / 128, bias=self.epsb[0:64, 0:1])
                            C.op("dve", "reciprocal", reads=[RS], writes=[RS], out=rs, in_=rs)
                            C.op("dve", "tensor_tensor", reads=[OC, RS], writes=[OC], out=o3, in0=o3, in1=bc(rs, 2, [64, 16, 128]), op=ALU.mult)
                            C.op("pool", "tensor_tensor", reads=[OC, ONORM], writes=[OC], out=o3, in0=o3, in1=bc(ONORM[0:64, :], 1, [64, 16, 128]), op=ALU.mult)
                            C.op("dve", "tensor_tensor", reads=[OC, ZC], writes=[OC], out=och, in0=och, in1=ZC[:], op=ALU.mult)
                        for h8 in range(4):
                            pt = PS[4 + h8 % 2]
                            for hh in range(8):
                                hd = h8 * 8 + hh
                                C.op("pe", "transpose", reads=[OC, self.ident], writes=[pt], out=pt[:, hh * 64:(hh + 1) * 64], in_=OC[:, hd * 128:(hd + 1) * 128],
                                     identity=ident64)
                            dst = OTst[:, h8 * 8:(h8 + 1) * 8, (c % 2) * 64:(c % 2) * 64 + 64]
                            srcv = pt[:, 0:512].rearrange("p (h t) -> p h t", h=8)
                            C.op("dve", "tensor_copy", reads=[pt], writes=[OTst], out=dst, in_=srcv)
                        if c % 2 == 0:
                            b0 = (c // 2) * 128
                            C.dma("pool", out=self.gdn_o.rearrange("(c p) t -> p c t", p=128)[:, :, b0:b0 + 128], in_=OTst[:], sbt=OTst, reads=[OTst])
                C.barrier()
        self.out_proj(xsrc, xdst, self.gdn_o, 32, wo, BT=256)


    def mix_s5(self, l, s, xsrc, xdst):
        C, T, PS = self.C, self.T, self.PS
        L = 128
        NCH = T // L
        PI = float(np.pi)
        with contextlib.ExitStack() as st:
            self.ensure_eps(st)
            XB = [C.sb(st, "XB", [128, KC, 512], F32, dma=True) for _ in range(2)]
            HF = [C.sb(st, "HF", [128, KC, 512], F32, dma=True) for _ in range(2)]
            sqb = [C.sb(st, "sq", [128, 512], F32) for _ in range(4)]
            rstd = C.sb(st, "rstd", [128, 512], F32)
            ib = 0
            for t0 in range(0, T, 512):
                gn = min(512, T - t0)
                X, Hf = XB[ib % 2], HF[ib % 2]
                ib += 1
                self.load_xblock(X, xsrc, t0, gn, 0)
                self.rms_to_h(X, None, gn, "norm_mix", l, sqb, rstd, PS[4], out_f32=Hf)
                C.dma("pool", out=self.s5_h.rearrange("(c p) t -> p c t", p=128)[:, :, t0:t0 + gn], in_=Hf[:, :, :gn], sbt=Hf, reads=[Hf])
            C.barrier()

        def rev(t, a, b):
            return t[:, b - 1:a - 1:-1] if a > 0 else t[:, b - 1::-1]

        with contextlib.ExitStack() as st:
            IOTA = C.sb(st, "IOTA", [128, L], F32, dma=True)
            C.dma("sp", out=IOTA[:], in_=self.iota_in[:, :], sbt=IOTA, writes=[IOTA])
            LAM = C.sb(st, "LAM", [128, 2, 3, 64], F32, dma=True)
            C.dma("sp", out=LAM[:], in_=self.s5_lam[:, :, :, :], sbt=LAM, writes=[LAM])
            MUL = C.sb(st, "MUL", [128, 4, L], F32)
            C.op("dve", "memset", writes=[MUL], ap=MUL[:], constant=1.0)
            C.op("dve", "memset", writes=[MUL], ap=MUL[:, :, 0:1], constant=0.0)
            HB = [C.sb(st, "HB", [128, T], F32, dma=True) for _ in range(2)]
            YT = C.sb(st, "YT", [128, T], F32)
            YT2 = C.sb(st, "YT2", [128, T], F32)
            YO = C.sb(st, "YO", [128, T], BF16, dma=True)
            Bt = [C.sb(st, "Bt", [128, 2, 4, 128], F32, dma=True) for _ in range(2)]
            Ct = [C.sb(st, "Ct", [128, 2, 4, 128], F32, dma=True) for _ in range(2)]
            tabs = [[C.sb(st, "tab", [128, 4, L], F32) for _ in range(4)] for _ in range(2)]
            CTt, SNt, GMt, GIt = [C.sb(st, "trig", [128, 4, L], F32) for _ in range(4)]
            ang = [C.sb(st, "ang", [128, L], F32) for _ in range(2)]
            angi = C.sb(st, "angi", [128, L], mybir.dt.int32)
            angf = C.sb(st, "angf", [128, L], F32)
            sm = C.sb(st, "s5sm", [128, 16, 4], F32)
            wk = [[C.sb(st, "wk", [128, 4, L], F32) for _ in range(2)] for _ in range(10)]
            ta = [C.sb(st, "ta", [128, 512], F32) for _ in range(2)]
            tb = [C.sb(st, "tb", [128, 512], F32) for _ in range(2)]
            tc_ = [C.sb(st, "tc", [128, 512], F32) for _ in range(2)]
            it = 0
            ifd = 0
            for fc in range(KC):
                hb = HB[fc % 2]
                C.dma("sp", out=hb[:], in_=self.s5_h[fc * 128:(fc + 1) * 128, :], sbt=hb, writes=[hb])
                streams = []
                for dr in range(2):
                    bt, ct = Bt[ifd % 2], Ct[ifd % 2]
                    T1r, T1i, T2r, T2i = tabs[ifd % 2]
                    ifd += 1
                    C.dma("sp", out=bt[:], in_=self.s5_B[dr, fc], sbt=bt, writes=[bt])
                    C.dma("sp", out=ct[:], in_=self.s5_C[dr, fc], sbt=ct, writes=[ct])
                    C.op("act", "activation", reads=[ct], writes=[ct], out=ct[:, 1], in_=ct[:, 1], func=AF.Copy, scale=-1.0)
                    are = LAM[:, dr, 0, 4 * fc:4 * fc + 4]
                    aim = LAM[:, dr, 1, 4 * fc:4 * fc + 4]
                    ldt = LAM[:, dr, 2, 4 * fc:4 * fc + 4]
                    S = lambda i: sm[:, i, :]
                    C.op("act", "activation", reads=[LAM], writes=[sm], out=S(0), in_=ldt, func=AF.Exp)
                    C.op("dve", "tensor_tensor", reads=[LAM, sm], writes=[sm], out=S(1), in0=are, in1=S(0), op=ALU.mult)
                    C.op("dve", "tensor_scalar", reads=[sm], writes=[sm], out=S(2), in0=S(1), scalar1=-1.0, scalar2=None, op0=ALU.mult)
                    C.op("dve", "tensor_tensor", reads=[LAM, sm], writes=[sm], out=S(3), in0=aim, in1=S(0), op=ALU.mult)
                    for q4 in range(4):
                        for (dst, off) in ((SNt, 0.0), (CTt, 0.5 * PI)):
                            a, a0 = ang[0], ang[1]
                            C1 = 6.28125
                            C2 = 2 * PI - C1
                            C.op("dve", "tensor_scalar", reads=[IOTA, sm], writes=[a0], out=a0[:], in0=IOTA[:], scalar1=sm[:, 3, q4:q4 + 1],
                                 scalar2=off, op0=ALU.mult, op1=ALU.add)
                            C.op("dve", "tensor_scalar", reads=[a0], writes=[angi], out=angi[:], in0=a0[:], scalar1=1.0 / (2 * PI), scalar2=None,
                                 op0=ALU.mult)
                            C.op("dve", "tensor_copy", reads=[angi], writes=[angf], out=angf[:], in_=angi[:])
                            C.op("dve", "scalar_tensor_tensor", reads=[angf, a0], writes=[a], out=a[:], in0=angf[:], scalar=-C1, in1=a0[:],
                                 op0=ALU.mult, op1=ALU.add)
                            C.op("dve", "scalar_tensor_tensor", reads=[angf, a], writes=[a], out=a[:], in0=angf[:], scalar=-C2, in1=a[:],
                                 op0=ALU.mult, op1=ALU.add)
                            C.op("dve", "tensor_scalar", reads=[a], writes=[a0], out=a0[:], in0=a[:], scalar1=PI, scalar2=-2 * PI, op0=ALU.is_gt,
                                 op1=ALU.mult)
                            C.op("dve", "tensor_tensor", reads=[a, a0], writes=[a], out=a[:], in0=a[:], in1=a0[:], op=ALU.add)
                            C.op("dve", "tensor_scalar", reads=[a], writes=[a0], out=a0[:], in0=a[:], scalar1=-PI, scalar2=2 * PI, op0=ALU.is_lt,
                                 op1=ALU.mult)
                            C.op("dve", "tensor_tensor", reads=[a, a0], writes=[a], out=a[:], in0=a[:], in1=a0[:], op=ALU.add)
                            C.op("act", "activation", reads=[a], writes=[dst], out=dst[:, q4, :], in_=a[:], func=AF.Sin)
                        C.op("act", "activation", reads=[IOTA, sm], writes=[GMt], out=GMt[:, q4, :], in_=IOTA[:], func=AF.Exp,
                             scale=sm[:, 1, q4:q4 + 1])
                        C.op("act", "activation", reads=[IOTA, sm], writes=[GIt], out=GIt[:, q4, :], in_=IOTA[:], func=AF.Exp,
                             scale=sm[:, 2, q4:q4 + 1])
                    rho, c1, s1 = GMt[:, :, 0], CTt[:, :, 0], SNt[:, :, 0]
                    tt_ = lambda o, a_, b_, op_, rd: C.op("dve", "tensor_tensor", reads=rd, writes=[sm], out=o, in0=a_, in1=b_, op=op_)
                    tt_(S(4), rho, c1, ALU.mult, [GMt, CTt])
                    C.op("dve", "tensor_scalar", reads=[sm], writes=[sm], out=S(4), in0=S(4), scalar1=-1.0, scalar2=None, op0=ALU.add)
                    tt_(S(5), rho, s1, ALU.mult, [GMt, SNt])
                    tt_(S(6), are, are, ALU.mult, [LAM])
                    tt_(S(7), aim, aim, ALU.mult, [LAM])
                    tt_(S(6), S(6), S(7), ALU.add, [sm])
                    C.op("dve", "reciprocal", reads=[sm], writes=[sm], out=S(6), in_=S(6))
                    tt_(S(7), S(4), are, ALU.mult, [sm, LAM])
                    tt_(S(8), S(5), aim, ALU.mult, [sm, LAM])
                    tt_(S(7), S(7), S(8), ALU.add, [sm])
                    tt_(S(9), S(7), S(6), ALU.mult, [sm])
                    tt_(S(7), S(5), are, ALU.mult, [sm, LAM])
                    tt_(S(8), S(4), aim, ALU.mult, [sm, LAM])
                    tt_(S(7), S(7), S(8), ALU.subtract, [sm])
                    tt_(S(10), S(7), S(6), ALU.mult, [sm])
                    C.op("dve", "tensor_scalar", reads=[sm], writes=[sm], out=S(11), in0=S(9), scalar1=-1.0, scalar2=None, op0=ALU.mult)
                    for q4 in range(4):
                        u = ang[q4 % 2]
                        C.op("dve", "tensor_scalar", reads=[CTt, sm], writes=[u], out=u[:], in0=CTt[:, q4, :], scalar1=sm[:, 9, q4:q4 + 1],
                             scalar2=None, op0=ALU.mult)
                        C.op("dve", "scalar_tensor_tensor", reads=[SNt, sm, u], writes=[u], out=u[:], in0=SNt[:, q4, :],
                             scalar=sm[:, 10, q4:q4 + 1], in1=u[:], op0=ALU.mult, op1=ALU.add)
                        C.op("dve", "tensor_tensor", reads=[u, GIt], writes=[T1r], out=T1r[:, q4, :], in0=u[:], in1=GIt[:, q4, :], op=ALU.mult)
                        C.op("dve", "tensor_scalar", reads=[CTt, sm], writes=[u], out=u[:], in0=CTt[:, q4, :], scalar1=sm[:, 10, q4:q4 + 1],
                             scalar2=None, op0=ALU.mult)
                        C.op("dve", "scalar_tensor_tensor", reads=[SNt, sm, u], writes=[u], out=u[:], in0=SNt[:, q4, :],
                             scalar=sm[:, 11, q4:q4 + 1], in1=u[:], op0=ALU.mult, op1=ALU.add)
                        C.op("dve", "tensor_tensor", reads=[u, GIt], writes=[T1i], out=T1i[:, q4, :], in0=u[:], in1=GIt[:, q4, :], op=ALU.mult)
                    C.op("dve", "tensor_tensor", reads=[GMt, CTt], writes=[T2r], out=T2r[:], in0=GMt[:], in1=CTt[:], op=ALU.mult)
                    C.op("dve", "tensor_tensor", reads=[GMt, SNt], writes=[T2i], out=T2i[:], in0=GMt[:], in1=SNt[:], op=ALU.mult)
                    def chunks(dr=dr, bt=bt, ct=ct, T1r=T1r, T1i=T1i, T2r=T2r, T2i=T2i, hb=hb):
                        par = dr
                        YD = YT if dr == 0 else YT2
                        pr, pi_, py = PS[par], PS[2 + par], PS[4 + par]
                        m1, m2, m3, m4, wr, wi, sr, si, xr, xi = [wk[i][par] for i in range(10)]
                        v3 = lambda p: p[:, 0:4 * L].rearrange("p (q t) -> p q t", q=4)
                        f2 = lambda t: t[:].rearrange("p q t -> p (q t)")
                        for ci in range(NCH):
                            cb = ci if dr == 0 else NCH - 1 - ci
                            hcols = hb[:, cb * L:(cb + 1) * L] if dr == 0 else rev(hb, cb * L, (cb + 1) * L)
                            for q4 in range(4):
                                C.op("pe", "matmul", reads=[bt, hb], writes=[pr], out=pr[:, q4 * L:(q4 + 1) * L], lhsT=bt[:, 0, q4, :], rhs=hcols,
                                     start=True, stop=True)
                            for q4 in range(4):
                                C.op("pe", "matmul", reads=[bt, hb], writes=[pi_], out=pi_[:, q4 * L:(q4 + 1) * L], lhsT=bt[:, 1, q4, :], rhs=hcols,
                                     start=True, stop=True)
                            yield
                            C.op("dve", "tensor_tensor", reads=[T1r, pr], writes=[m1], out=m1[:], in0=T1r[:], in1=v3(pr), op=ALU.mult)
                            C.op("dve", "tensor_tensor", reads=[T1i, pi_], writes=[m2], out=m2[:], in0=T1i[:], in1=v3(pi_), op=ALU.mult)
                            C.op("dve", "tensor_tensor", reads=[T1r, pi_], writes=[m3], out=m3[:], in0=T1r[:], in1=v3(pi_), op=ALU.mult)
                            C.op("dve", "tensor_tensor", reads=[T1i, pr], writes=[m4], out=m4[:], in0=T1i[:], in1=v3(pr), op=ALU.mult)
                            C.op("pool", "tensor_tensor", reads=[m1, m2], writes=[wr], out=wr[:], in0=m1[:], in1=m2[:], op=ALU.subtract)
                            C.op("pool", "tensor_tensor", reads=[m3, m4], writes=[wi], out=wi[:], in0=m3[:], in1=m4[:], op=ALU.add)
                            yield
                            if ci > 0:
                                C.op("dve", "tensor_tensor", reads=[wr, xr], writes=[wr], out=wr[:, :, 0:1], in0=wr[:, :, 0:1],
                                     in1=xr[:, :, L - 1:L], op=ALU.add)
                                C.op("dve", "tensor_tensor", reads=[wi, xi], writes=[wi], out=wi[:, :, 0:1], in0=wi[:, :, 0:1],
                                     in1=xi[:, :, L - 1:L], op=ALU.add)
                            C.op("dve", "tensor_tensor_scan", reads=[MUL, wr], writes=[sr], out=f2(sr), data0=f2(MUL), data1=f2(wr), initial=0.0,
                                 op0=ALU.mult, op1=ALU.add)
                            C.op("dve", "tensor_tensor_scan", reads=[MUL, wi], writes=[si], out=f2(si), data0=f2(MUL), data1=f2(wi), initial=0.0,
                                 op0=ALU.mult, op1=ALU.add)
                            yield
                            C.op("pool", "tensor_tensor", reads=[T2r, sr], writes=[m1], out=m1[:], in0=T2r[:], in1=sr[:], op=ALU.mult)
                            C.op("pool", "tensor_tensor", reads=[T2i, si], writes=[m2], out=m2[:], in0=T2i[:], in1=si[:], op=ALU.mult)
                            C.op("dve", "tensor_tensor", reads=[T2r, si], writes=[m3], out=m3[:], in0=T2r[:], in1=si[:], op=ALU.mult)
                            C.op("dve", "tensor_tensor", reads=[T2i, sr], writes=[m4], out=m4[:], in0=T2i[:], in1=sr[:], op=ALU.mult)
                            C.op("pool", "tensor_tensor", reads=[m1, m2], writes=[xr], out=xr[:], in0=m1[:], in1=m2[:], op=ALU.subtract)
                            C.op("dve", "tensor_tensor", reads=[m3, m4], writes=[xi], out=xi[:], in0=m3[:], in1=m4[:], op=ALU.add)
                            yield
                            for q4 in range(4):
                                C.op("pe", "matmul", reads=[ct, xr], writes=[py], out=py[:, :L], lhsT=ct[:, 0, q4, :], rhs=xr[:, q4, :],
                                     start=(q4 == 0), stop=False)
                            for q4 in range(4):
                                C.op("pe", "matmul", reads=[ct, xi], writes=[py], out=py[:, :L], lhsT=ct[:, 1, q4, :], rhs=xi[:, q4, :],
                                     start=False, stop=(q4 == 3))
                            if dr == 0:
                                C.op("act", "activation", reads=[py], writes=[YD], out=YD[:, cb * L:(cb + 1) * L], in_=py[:, :L], func=AF.Copy)
                            else:
                                C.op("dve", "tensor_copy", reads=[py], writes=[YD], out=rev(YD, cb * L, (cb + 1) * L), in_=py[:, :L])
                            yield

                    streams.append(chunks())
                while streams:
                    for gn_ in list(streams):
                        try:
                            next(gn_)
                        except StopIteration:
                            streams.remove(gn_)
                for g0 in range(0, T, 512):
                    gn = min(512, T - g0)
                    i2 = (g0 // 512) % 2
                    C.op("dve", "scalar_tensor_tensor", reads=[hb, YT, self.small], writes=[tc_[i2]], out=tc_[i2][:, :gn], in0=hb[:, g0:g0 + gn],
                         scalar=self.sm("s5_d", fc), in1=YT[:, g0:g0 + gn], op0=ALU.mult, op1=ALU.add)
                    C.op("dve", "tensor_tensor", reads=[tc_[i2], YT2], writes=[tc_[i2]], out=tc_[i2][:, :gn], in0=tc_[i2][:, :gn],
                         in1=YT2[:, g0:g0 + gn], op=ALU.add)
                    self.gelu_tanh(YO, YO[:, g0:g0 + gn], tc_[i2], tc_[i2][:, :gn], ta[i2], tb[i2], gn)
                C.dma("pool", out=self.s5_y[fc * 128:(fc + 1) * 128, :], in_=YO[:], sbt=YO, reads=[YO])
            C.barrier()
        wg = self.ws["s5_glu"]
        with contextlib.ExitStack() as st:
            XB = [C.sb(st, "XBo", [128, KC, 512], F32, dma=True) for _ in range(2)]
            AB = [C.sb(st, "ABo", [128, KC, 512], BF16, dma=True) for _ in range(2)]
            WA = [C.sb(st, "WA", [128, KC, 256], BF16, dma=True) for _ in range(2)]
            WB = [C.sb(st, "WB", [128, KC, 256], BF16, dma=True) for _ in range(2)]
            sgb = [C.sb(st, "sg", [128, 512], F32) for _ in range(2)]
            g0b = [C.sb(st, "g0", [128, 512], F32) for _ in range(2)]
            io = ib = 0
            for t0 in range(0, T, 512):
                gn = min(512, T - t0)
                X, Ab = XB[ib % 2], AB[ib % 2]
                ib += 1
                self.load_xblock(X, xsrc, t0, gn, 0)
                C.dma("sp", out=Ab[:, :, :gn], in_=self.s5_y.rearrange("(c p) t -> p c t", p=128)[:, :, t0:t0 + gn], sbt=Ab, writes=[Ab])
                for ot in range(8):
                    Wa, Wb = WA[io % 2], WB[io % 2]
                    io += 1
                    C.dma("sp", out=Wa[:], in_=wg[ot, 0], sbt=Wa, writes=[Wa])
                    C.dma("sp", out=Wb[:], in_=wg[8 + ot, 0], sbt=Wb, writes=[Wb])
                    for oo in range(2):
                        oc = ot * 2 + oo
                        pa, pb = PS[oc % 2], PS[2 + oc % 2]
                        for k in range(KC):
                            C.op("pe", "matmul", reads=[Wa, Ab], writes=[pa], out=pa[:, :gn], lhsT=Wa[:, k, oo * 128:(oo + 1) * 128],
                                 rhs=Ab[:, k, :gn], start=(k == 0), stop=(k == KC - 1))
                        for k in range(KC):
                            C.op("pe", "matmul", reads=[Wb, Ab], writes=[pb], out=pb[:, :gn], lhsT=Wb[:, k, oo * 128:(oo + 1) * 128],
                                 rhs=Ab[:, k, :gn], start=(k == 0), stop=(k == KC - 1))
                        sg, g0 = sgb[oc % 2], g0b[oc % 2]
                        C.op("act", "activation", reads=[pb], writes=[sg], out=sg[:, :gn], in_=pb[:, :gn], func=AF.Sigmoid)
                        C.op("dve", "tensor_tensor", reads=[sg, pa], writes=[g0], out=g0[:, :gn], in0=sg[:, :gn], in1=pa[:, :gn], op=ALU.mult)
                        C.op("pool", "tensor_tensor", reads=[X, g0], writes=[X], out=X[:, oc, :gn], in0=X[:, oc, :gn], in1=g0[:, :gn], op=ALU.add)
                dst = xdst.rearrange("(c p) t -> p c t", p=128)[:, :, t0:t0 + gn]
                C.dma("pool", out=dst, in_=X[:, :, :gn], sbt=X, reads=[X])
            C.barrier()

    def small_layout(self):
        def add(name, n):
            self.small_off[name] = self.small_n
            self.small_n += n
        add("norm_mix", DEPTH * KC)
        add("norm_ffn", DEPTH * KC)
        add("norm_ple", DEPTH * KC)
        add("final_norm", KC)
        add("conv_w", DEPTH * 3 * NFC)
        add("conv_b", DEPTH * NFC)
        add("s5_d", KC)
        add("gdn_cw", 3 * 64)

    def sm(self, name, idx):
        o = self.small_off[name] + idx
        return self.small[:, o:o + 1]

    def prologue(self, x_in):
        C, T = self.C, self.T
        with contextlib.ExitStack() as st:
            xtok = [C.sb(st, "xtok", [128, D], F32, dma=True) for _ in range(3)]
            stage = [C.sb(st, "stage", [128, KC, 512], F32, dma=True) for _ in range(2)]
            it = 0
            for s in range(self.NS):
                for g0 in range(0, T, 512):
                    gn = min(512, T - g0)
                    sg = stage[(g0 // 512) % 2]
                    for tt in range(0, gn, 128):
                        xt = xtok[it % 3]
                        it += 1
                        C.dma("sp", out=xt[:], in_=x_in[s, g0 + tt:g0 + tt + 128, :], sbt=xt, writes=[xt])
                        for c4 in range(0, KC, 4):
                            pst = self.PS[(c4 // 4) % 2 + 2 * ((tt // 128) % 2)]
                            for c in range(c4, c4 + 4):
                                C.op("pe", "transpose", reads=[xt, self.ident], writes=[pst],
                                     out=pst[:, (c - c4) * 128:(c - c4 + 1) * 128], in_=xt[:, c * 128:(c + 1) * 128],
                                     identity=self.ident[:])
                            eng = "dve" if (c4 // 4) % 2 == 0 else "act"
                            src = pst[:, 0:512].rearrange("p (c t) -> p c t", c=4)
                            if eng == "dve":
                                C.op("dve", "tensor_copy", reads=[pst], writes=[sg], out=sg[:, c4:c4 + 4, tt:tt + 128], in_=src)
                            else:
                                C.op("act", "activation", reads=[pst], writes=[sg], out=sg[:, c4:c4 + 4, tt:tt + 128], in_=src,
                                     func=AF.Copy)
                    dst = self.xT[0][s].rearrange("(c p) t -> p c t", p=128)[:, :, g0:g0 + gn]
                    C.dma("pool", out=dst, in_=sg[:, :, 0:gn], sbt=sg, reads=[sg])
            C.barrier()

    def rms_to_h(self, XB, H, nb, gain_name, l, sqb, rstd, psb, out_f32=None):
        C = self.C
        for c in range(KC):
            sq = sqb[c % len(sqb)]
            if c % 2 == 0:
                C.op("dve", "tensor_tensor", reads=[XB], writes=[sq], out=sq[:, :nb], in0=XB[:, c, :nb], in1=XB[:, c, :nb], op=ALU.mult)
            else:
                C.op("act", "activation", reads=[XB], writes=[sq], out=sq[:, :nb], in_=XB[:, c, :nb], func=AF.Square)
            C.op("pe", "matmul", reads=[sq, self.ones], writes=[psb], out=psb[:, :nb], lhsT=self.ones[:], rhs=sq[:, :nb],
                 start=(c == 0), stop=(c == KC - 1))
        C.op("act", "activation", reads=[psb], writes=[rstd], out=rstd[:, :nb], in_=psb[:, :nb], func=AF.Sqrt,
             scale=1.0 / D, bias=self.epsb[:, 0:1])
        C.op("dve", "reciprocal", reads=[rstd], writes=[rstd], out=rstd[:, :nb], in_=rstd[:, :nb])
        for c in range(KC):
            dst = H if out_f32 is None else out_f32
            C.op("dve", "scalar_tensor_tensor", reads=[XB, rstd, self.small], writes=[dst], out=dst[:, c, :nb], in0=XB[:, c, :nb],
                 scalar=self.sm(gain_name, l * KC + c), in1=rstd[:, :nb], op0=ALU.mult, op1=ALU.mult)

    def ensure_eps(self, st):
        C = self.C
        self.epsb = C.sb(st, "epsb", [128, 1], F32)
        C.op("dve", "memset", writes=[self.epsb], ap=self.epsb[:], constant=EPS)

    def load_xblock(self, XB, xsrc, t0, nbi, halo):
        C, T = self.C, self.T
        lo, hi = t0 - halo, t0 + nbi + halo
        clo, chi = max(lo, 0), min(hi, T)
        if clo > lo:
            C.op("dve", "memset", writes=[XB], ap=XB[:, :, 0:clo - lo], constant=0.0)
        if chi < hi:
            C.op("dve", "memset", writes=[XB], ap=XB[:, :, chi - lo:hi - lo], constant=0.0)
        src = xsrc.rearrange("(c p) t -> p c t", p=128)[:, :, clo:chi]
        C.dma("sp", out=XB[:, :, clo - lo:chi - lo], in_=src, sbt=XB, writes=[XB])

    def ffn_ple(self, l, s, xsrc, xdst, p_l):
        C, T = self.C, self.T
        PS = self.PS
        wgu, wdn, wpg, wpp = self.ws[("gu", l)], self.ws[("dn", l)], self.ws[("pg", l)], self.ws[("pp", l)]
        with contextlib.ExitStack() as st:
            self.ensure_eps(st)
            XB = C.sb(st, "XB", [128, KC, 512], F32, dma=True)
            H = C.sb(st, "H", [128, KC, 512], BF16)
            A = C.sb(st, "A", [128, 22, 512], BF16)
            WGU = [C.sb(st, "WGU", [128, 2, KC, 256], BF16, dma=True) for _ in range(2)]
            WD = [C.sb(st, "WD", [128, 11, 512], BF16, dma=True) for _ in range(3)]
            WPG = [C.sb(st, "WPG", [128, KC, 256], BF16, dma=True) for _ in range(2)]
            WPP = C.sb(st, "WPP", [128, 2, D], BF16, dma=True)
            sqb = [C.sb(st, "sq", [128, 512], F32) for _ in range(4)]
            rstd = C.sb(st, "rstd", [128, 512], F32)
            g0b = [C.sb(st, "g0", [128, 512], F32) for _ in range(2)]
            g1b = [C.sb(st, "g1", [128, 512], F32) for _ in range(2)]
            sgb = [C.sb(st, "sg", [128, 512], F32) for _ in range(2)]
            ptok = [C.sb(st, "ptok", [128, PLE], F32, dma=True) for _ in range(2)]
            PT = C.sb(st, "PT", [128, 2, 512], BF16)
            C.dma("sp", out=WPP[:], in_=wpp[0, 0], sbt=WPP, writes=[WPP])
            iw = 0
            idn = 0
            ipg = 0
            for (t0, nbi) in blocks_of(T):
                nb = nbi + 2
                self.load_xblock(XB, xsrc, t0, nbi, 1)
                self.rms_to_h(XB, H, nb, "norm_ffn", l, sqb, rstd, PS[4])
                for hf in range(2):
                    for jp in range(11):
                        jpg = hf * 11 + jp
                        W = WGU[iw % 2]
                        iw += 1
                        C.dma("sp", out=W[:, 0], in_=wgu[jpg, 0], sbt=W, writes=[W])
                        C.dma("sp", out=W[:, 1], in_=wgu[22 + jpg, 0], sbt=W, writes=[W])
                        for jj in range(2):
                            j = jpg * 2 + jj
                            ja = jp * 2 + jj
                            pg, pu = PS[j % 2], PS[2 + j % 2]
                            for k in range(KC):
                                C.op("pe", "matmul", reads=[W, H], writes=[pg], out=pg[:, :nb], lhsT=W[:, 0, k, jj * 128:(jj + 1) * 128],
                                     rhs=H[:, k, :nb], start=(k == 0), stop=(k == KC - 1))
                            for k in range(KC):
                                C.op("pe", "matmul", reads=[W, H], writes=[pu], out=pu[:, :nb], lhsT=W[:, 1, k, jj * 128:(jj + 1) * 128],
                                     rhs=H[:, k, :nb], start=(k == 0), stop=(k == KC - 1))
                            g0, g1, sg = g0b[j % 2], g1b[j % 2], sgb[j % 2]
                            cw = lambda kk: self.sm("conv_w", (l * 3 + kk) * NFC + j)
                            C.op("act", "activation", reads=[pg, self.small], writes=[g0], out=g0[:, :nbi], in_=pg[:, 1:1 + nbi],
                                 func=AF.Identity, scale=cw(1))
                            C.op("dve", "scalar_tensor_tensor", reads=[pg, g0, self.small], writes=[g1], out=g1[:, :nbi],
                                 in0=pg[:, 0:nbi], scalar=cw(0), in1=g0[:, :nbi], op0=ALU.mult, op1=ALU.add)
                            C.op("dve", "scalar_tensor_tensor", reads=[pg, g1, self.small], writes=[g0], out=g0[:, :nbi],
                                 in0=pg[:, 2:2 + nbi], scalar=cw(2), in1=g1[:, :nbi], op0=ALU.mult, op1=ALU.add)
                            C.op("act", "activation", reads=[g0, self.small], writes=[sg], out=sg[:, :nbi], in_=g0[:, :nbi],
                                 func=AF.Silu, bias=self.sm("conv_b", l * NFC + j), scale=1.0)
                            C.op("dve", "tensor_tensor", reads=[sg, pu], writes=[A], out=A[:, ja, :nbi], in0=sg[:, :nbi],
                                 in1=pu[:, 1:1 + nbi], op=ALU.mult)
                    for q in range(4):
                        for g in range(2):
                            Wd = WD[idn % 3]
                            idn += 1
                            C.dma("sp", out=Wd[:], in_=wdn[q, hf * 2 + g], sbt=Wd, writes=[Wd])
                            for jj in range(11):
                                ja = g * 11 + jj
                                for i in range(4):
                                    C.op("pe", "matmul", reads=[Wd, A], writes=[PS[4 + i]], out=PS[4 + i][:, :nbi],
                                         lhsT=Wd[:, jj, i * 128:(i + 1) * 128], rhs=A[:, ja, :nbi],
                                         start=(ja == 0), stop=(ja == 21))
                        for i in range(4):
                            c = q * 4 + i
                            C.op("dve", "tensor_tensor", reads=[XB, PS[4 + i]], writes=[XB], out=XB[:, c, 1:1 + nbi],
                                 in0=XB[:, c, 1:1 + nbi], in1=PS[4 + i][:, :nbi], op=ALU.add)
                for tt in range(0, nbi, 128):
                    tn = min(128, nbi - tt)
                    pt = ptok[(tt // 128) % 2]
                    C.dma("sp", out=pt[:tn, :], in_=p_l[t0 + tt:t0 + tt + tn, :], sbt=pt, writes=[pt])
                    pst = PS[(tt // 128) % 2]
                    for e in range(2):
                        C.op("pe", "transpose", reads=[pt, self.ident], writes=[pst], out=pst[:, e * 128:e * 128 + tn],
                             in_=pt[:tn, e * 128:(e + 1) * 128], identity=self.ident[:tn, :tn])
                    C.op("act", "activation", reads=[pst], writes=[PT], out=PT[:, :, tt:tt + tn],
                         in_=pst[:, 0:256].rearrange("p (e t) -> p e t", e=2)[:, :, :tn], func=AF.Copy)
                self.rms_to_h(XB, H, nb, "norm_ple", l, sqb, rstd, PS[4])
                for og in range(8):
                    Wg = WPG[ipg % 2]
                    ipg += 1
                    C.dma("sp", out=Wg[:], in_=wpg[og, 0], sbt=Wg, writes=[Wg])
                    for oo in range(2):
                        oc = og * 2 + oo
                        pgate, pproj = PS[oc % 2], PS[2 + oc % 2]
                        for k in range(KC):
                            C.op("pe", "matmul", reads=[Wg, H], writes=[pgate], out=pgate[:, :nbi], lhsT=Wg[:, k, oo * 128:(oo + 1) * 128],
                                 rhs=H[:, k, 1:1 + nbi], start=(k == 0), stop=(k == KC - 1))
                        for e in range(2):
                            C.op("pe", "matmul", reads=[WPP, PT], writes=[pproj], out=pproj[:, :nbi], lhsT=WPP[:, e, oc * 128:(oc + 1) * 128],
                                 rhs=PT[:, e, :nbi], start=(e == 0), stop=(e == 1))
                        sg, g0 = sgb[oc % 2], g0b[oc % 2]
                        C.op("act", "activation", reads=[pgate], writes=[sg], out=sg[:, :nbi], in_=pgate[:, :nbi], func=AF.Sigmoid)
                        C.op("dve", "tensor_tensor", reads=[sg, pproj], writes=[g0], out=g0[:, :nbi], in0=sg[:, :nbi], in1=pproj[:, :nbi],
                             op=ALU.mult)
                        C.op("pool", "tensor_tensor", reads=[XB, g0], writes=[XB], out=XB[:, oc, 1:1 + nbi], in0=XB[:, oc, 1:1 + nbi],
                             in1=g0[:, :nbi], op=ALU.add)
                dst = xdst.rearrange("(c p) t -> p c t", p=128)[:, :, t0:t0 + nbi]
                C.dma("pool", out=dst, in_=XB[:, :, 1:1 + nbi], sbt=XB, reads=[XB])
            C.barrier()

    def epilogue(self, xT, y_out):
        C, T = self.C, self.T
        PS = self.PS
        with contextlib.ExitStack() as st:
            self.ensure_eps(st)
            XB = [C.sb(st, "XBe", [128, KC, 512], F32, dma=True) for _ in range(2)]
            Y = C.sb(st, "Ye", [128, KC, 512], F32)
            sqb = [C.sb(st, "sq", [128, 512], F32) for _ in range(4)]
            rstd = C.sb(st, "rstd", [128, 512], F32)
            otok = [C.sb(st, "otok", [128, D], F32, dma=True) for _ in range(2)]
            ib = 0
            io = 0
            for s in range(self.NS):
                for g0 in range(0, T, 512):
                    gn = min(512, T - g0)
                    X = XB[ib % 2]
                    ib += 1
                    self.load_xblock(X, xT[s], g0, gn, 0)
                    self.rms_to_h(X, None, gn, "final_norm", 0, sqb, rstd, PS[4], out_f32=Y)
                    for tt in range(0, gn, 128):
                        ot = otok[io % 2]
                        io += 1
                        for c4 in range(0, KC, 4):
                            pst = PS[(c4 // 4) % 4]
                            for c in range(c4, c4 + 4):
                                C.op("pe", "transpose", reads=[Y, self.ident], writes=[pst], out=pst[:, (c - c4) * 128:(c - c4 + 1) * 128],
                                     in_=Y[:, c, tt:tt + 128], identity=self.ident[:])
                            if (c4 // 4) % 2 == 0:
                                C.op("dve", "tensor_copy", reads=[pst], writes=[ot], out=ot[:, c4 * 128:(c4 + 4) * 128], in_=pst[:, 0:512])
                            else:
                                C.op("act", "activation", reads=[pst], writes=[ot], out=ot[:, c4 * 128:(c4 + 4) * 128], in_=pst[:, 0:512],
                                     func=AF.Copy)
                        C.dma("pool", out=y_out[s, g0 + tt:g0 + tt + 128, :], in_=ot[:], sbt=ot, reads=[ot])
            C.barrier()


def pack_small(P, inputs):
    sp = np.zeros((128, P.small_n), np.float32)

    def put(name, arr2d):
        o = P.small_off[name]
        sp[:, o:o + arr2d.shape[0]] = arr2d.T

    def chunked(a, nch):
        return np.ascontiguousarray(a).reshape(-1, 128)

    put("norm_mix", chunked(inputs["norm_mix"], KC))
    put("norm_ffn", chunked(inputs["norm_ffn"], KC))
    put("norm_ple", chunked(inputs["norm_ple"], KC))
    put("final_norm", chunked(inputs["final_norm"], KC))
    put("conv_w", chunked(inputs["ffn_conv_w"], NFC))
    put("conv_b", chunked(inputs["ffn_conv_b"], NFC))
    if "s5_d" in inputs:
        put("s5_d", chunked(inputs["s5_d"], KC))
    if "gdn_conv_w" in inputs:
        put("gdn_cw", chunked(inputs["gdn_conv_w"], 64))
    return sp


def na_tables(rpb):
    NEG = -1.0e4
    par = np.arange(2)[:, None, None, None]
    j = np.arange(64)[None, :, None, None]
    m = np.arange(16)[None, None, :, None]
    c = np.arange(64)[None, None, None, :]
    dr = 7 - m + par
    dc = j - c
    g = rpb[:, np.clip(dr + 7, 0, 14), np.clip(dc + 15, 0, 30)]
    g = np.ascontiguousarray(np.broadcast_to(g, (rpb.shape[0], 2, 64, 16, 64))).reshape(rpb.shape[0], 128, 16, 64)
    cs = np.clip(c - 8, 0, 48)
    colv = (j >= cs) & (j < cs + 16)
    v_full = colv & (dr >= -7) & (dr <= 7)
    v_int = colv & (dr >= -4) & (dr <= 3)
    mask = np.stack([np.where(v_int, 0.0, NEG), np.where(v_full, 0.0, NEG)], 0)
    mask = np.ascontiguousarray(np.broadcast_to(mask, (2, 2, 64, 16, 64))).reshape(2, 128, 16, 64).transpose(1, 0, 2, 3)
    return g.astype(np.float32), np.ascontiguousarray(mask).astype(np.float32)


def s5_tables(inputs):
    lam = np.zeros((128, 2, 3, 64), np.float32)
    Bp = np.zeros((2, 16, 128, 2, 4, 128), np.float32)
    Cp = np.zeros((2, 16, 128, 2, 4, 128), np.float32)
    for d in range(2):
        lam[:, d, 0, :] = inputs["s5_a_re"][0, d].reshape(64, 128).T
        lam[:, d, 1, :] = inputs["s5_a_im"][0, d].reshape(64, 128).T
        lam[:, d, 2, :] = np.repeat(inputs["s5_log_dt"][0, d], 64).reshape(64, 128).T
        for ri, (bn, cn) in enumerate((("s5_b_re", "s5_c_re"), ("s5_b_im", "s5_c_im"))):
            b = inputs[bn][0, d]
            c = inputs[cn][0, d]
            for fc in range(16):
                for gi in range(8):
                    g = 8 * fc + gi
                    q4, half = gi // 2, gi % 2
                    Bp[d, fc, gi * 16:(gi + 1) * 16, ri, q4, half * 64:(half + 1) * 64] = b[g].T
                    Cp[d, fc, half * 64:(half + 1) * 64, ri, q4, gi * 16:(gi + 1) * 16] = c[g].T
    return lam, Bp, Cp


def gdn_consts():
    NEG = -1.0e5
    s = np.arange(64)[:, None]
    i = np.arange(64)[None, :]
    cs = np.zeros((64, 8, 64), np.float32)
    cs[:, 0] = s <= i
    cs[:, 1] = s >= i
    cs[:, 2] = s > i
    cs[:, 3] = s < i
    cs[:, 4] = np.where(i >= s, 0.0, NEG)
    cs[:, 5] = np.where(i <= s, 0.0, NEG)
    cs[:, 6] = i > s
    cs[:, 7] = i < s
    return cs


def run_prog(P, nc, inputs, slot_x, slot_p, n_cores):
    sp = pack_small(P, inputs)
    ident = np.eye(128, dtype=np.float32)
    in_maps = []
    for c in range(n_cores):
        m = {"x_in": slot_x[c], "p_in": slot_p[c], "smallp": sp, "ident": ident}
        if "gdn_consts" in P.din:
            m["gdn_consts"] = gdn_consts()
        if "s5_lam" in P.din:
            m["s5_lam"], m["s5_B"], m["s5_C"] = s5_tables(inputs)
            m["iota128"] = np.ascontiguousarray(np.broadcast_to(np.arange(1, 129, dtype=np.float32), (128, 128)))
        if "na_g" in P.din:
            m["na_g"], m["na_mask"] = na_tables(np.asarray(inputs["na_rpb"][0]))
        for k in P.din:
            if k not in m:
                if "." in k:
                    nm, l = k.split(".")
                    m[k] = np.ascontiguousarray(inputs[nm][int(l)])
                else:
                    m[k] = np.ascontiguousarray(inputs[k])
        in_maps.append(m)
    res = run_bass_kernel_spmd(nc, in_maps, core_ids=list(range(n_cores)))
    return [r["y_out"] for r in res.results]


def kernel(**inputs):
    T, NS = 4096, 2
    P = Prog(T, NS, list(range(DEPTH)))
    nc = P.build()
    xp, xs = inputs["x_prompt"], inputs["x_sample"]
    pp, psm = inputs["p_prompt"], inputs["p_sample"]
    slot_x, slot_p = [], []
    for c in range(8):
        j = c if c < 2 else 0
        slot_x.append(np.stack([xs[c], xp[j]], 0))
        slot_p.append(np.stack([psm[:, c], pp[:, j]], 1))
    outs = run_prog(P, nc, inputs, slot_x, slot_p, 8)
    y_sample = np.stack([outs[c][0] for c in range(8)], 0)
    y_prompt = np.stack([outs[0][1], outs[1][1]], 0)
    return (y_prompt, y_sample)
```
